# Optimizing a Trainium2 kernel written in Bass

```python
import math
import jax
import jax.numpy as jnp
from jax import lax
import numpy as np

D_MODEL = 1024
BATCH = 8
SEQ = 8192
DEPTH = 2

CTX_LEN = 256
GRID_W = 64
EPS = 1e-6
N_MOD = 6
MLP_HIDDEN = 4 * D_MODEL

CONV_CH = D_MODEL // 2
CONV_W = 31
DIFF_HEADS = 4
DIFF_DQK = 64
DIFF_DV = 2 * DIFF_DQK
Q_BLOCK = 128
ROPE_BASE = 10000.0
EV_QK = DIFF_HEADS * 2 * DIFF_DQK
EV_VW = DIFF_HEADS * DIFF_DV
EV_Q0 = 2 * CONV_CH
EV_K0 = EV_Q0 + EV_QK
EV_V0 = EV_K0 + EV_QK
EVEN_IN = EV_V0 + EV_VW
EVEN_MIX = CONV_CH + EV_VW

HY_CH = D_MODEL // 2
HY_ORDER = 2
HY_SHORT_W = 3
HY_EMB = 33
HY_FFN = 64
HY_FAST = 0.3
HY_SLOW = 1.5
HY_TARGET = 1e-2
HY_MAX_DECAY = math.log(HY_TARGET) / HY_FAST
HY_MIN_DECAY = math.log(HY_TARGET) / HY_SLOW
HY_IN = (HY_ORDER + 1) * HY_CH
DN_HEADS = 4
DN_DK = 128
DN_DV = 128
DN_WIDTH = DN_HEADS * DN_DK
DN_CONV_W = 5
DN_CHUNK = 64
OD_G0 = HY_IN
OD_Q0 = OD_G0 + DN_HEADS * DN_DV
OD_K0 = OD_Q0 + DN_WIDTH
ODD_IN = OD_K0 + 2 * DN_WIDTH + 4 * DN_HEADS
ODD_MIX = HY_CH + DN_HEADS * DN_DV

N_EVEN = (DEPTH + 1) // 2
N_ODD = DEPTH // 2

kernel_name = 'hybrid_conformer_diffattn_hyena_gdn_dit'

F32 = jnp.float32


def rms_norm(x, g):
    xf = x.astype(F32)
    y = xf * lax.rsqrt(jnp.mean(xf * xf, axis=-1, keepdims=True) + EPS)
    return (y * g.astype(F32)).astype(x.dtype)


def layer_norm(x, g, b):
    xf = x.astype(F32)
    mu = jnp.mean(xf, axis=-1, keepdims=True)
    var = jnp.mean(jnp.square(xf - mu), axis=-1, keepdims=True)
    y = (xf - mu) * lax.rsqrt(var + EPS)
    return (y * g.astype(F32) + b.astype(F32)).astype(x.dtype)


def l2norm(t):
    return t * lax.rsqrt(jnp.sum(t * t, axis=-1, keepdims=True) + 1e-6)


def modulate(h, shift, scale):
    return h * (1.0 + scale) + shift


def squared_relu_mlp(h, w1, w2):
    return jnp.square(jax.nn.relu(h @ w1)) @ w2


def depthwise_conv(x, w, b=None):
    width = w.shape[0]
    left = (width - 1) // 2
    y = lax.conv_general_dilated(x, w[:, None, :].astype(x.dtype), (1,), [(left, width - 1 - left)],
                                 dimension_numbers=('NWC', 'WIO', 'NWC'), feature_group_count=x.shape[-1])
    return y if b is None else y + b


def axial_rope_tables(length):
    rows = length // GRID_W
    row = jnp.repeat(jnp.arange(rows, dtype=F32), GRID_W)
    col = jnp.tile(jnp.arange(GRID_W, dtype=F32), rows)
    n_freq = DIFF_DQK // 4
    inv = ROPE_BASE ** (-jnp.arange(n_freq, dtype=F32) / n_freq)
    ang_r = row[:, None] * inv
    ang_c = col[:, None] * inv
    return (jnp.cos(ang_r), jnp.sin(ang_r), jnp.cos(ang_c), jnp.sin(ang_c))


def rotate(x, cos, sin):
    x1, x2 = jnp.split(x, 2, axis=-1)
    return jnp.concatenate([x1 * cos - x2 * sin, x1 * sin + x2 * cos], axis=-1)


def apply_axial_rope(x, rope):
    cr, sr, cc, sc = rope
    xf = x.astype(F32)
    half = x.shape[-1] // 2
    return jnp.concatenate([rotate(xf[..., :half], cr, sr), rotate(xf[..., half:], cc, sc)], axis=-1).astype(x.dtype)


def conformer_conv(p_glu, conv_w, conv_b, ln_g, ln_b):
    a, gate = jnp.split(p_glu, 2, axis=-1)
    u = depthwise_conv(a * jax.nn.sigmoid(gate), conv_w, conv_b)
    return jax.nn.silu(layer_norm(u, ln_g, ln_b))


def qk_heads(t):
    b, l, _ = t.shape
    return t.reshape(b, l, DIFF_HEADS, 2, DIFF_DQK).transpose(0, 2, 3, 1, 4)


def v_heads(t):
    b, l, _ = t.shape
    return t.reshape(b, l, DIFF_HEADS, DIFF_DV).transpose(0, 2, 1, 3)


def diff_attend(q, k, v, lam):
    s = jnp.einsum('bhmqd,bhmkd->bhmqk', q, k).astype(F32) * (DIFF_DQK ** -0.5)
    p = jax.nn.softmax(s, axis=-1)
    p = p[:, :, 0] - lam * p[:, :, 1]
    return jnp.einsum('bhqk,bhkv->bhqv', p.astype(v.dtype), v)


def diff_merge(o, subln_g, lambda_init):
    b, h, l, dv = o.shape
    o = rms_norm(o, subln_g) * (1.0 - lambda_init)
    return o.transpose(0, 2, 1, 3).reshape(b, l, h * dv)


def even_mixer(h_lat, h_ctx, rope, w_in, conv_w, conv_b, ln_g, ln_b, lq1, lk1, lq2, lk2, subln_g, w_out,
               lambda_init, ctx_out):
    lam = (jnp.exp(jnp.sum(lq1.astype(F32) * lk1.astype(F32)))
           - jnp.exp(jnp.sum(lq2.astype(F32) * lk2.astype(F32))) + lambda_init)
    b, length = h_lat.shape[:2]
    p_lat = h_lat @ w_in
    conv_lat = conformer_conv(p_lat[..., :EV_Q0], conv_w, conv_b, ln_g, ln_b)
    q_lat = apply_axial_rope(qk_heads(p_lat[..., EV_Q0:EV_K0]), rope)
    k_lat = apply_axial_rope(qk_heads(p_lat[..., EV_K0:EV_V0]), rope)
    v_lat = v_heads(p_lat[..., EV_V0:])
    if ctx_out:
        p_ctx = h_ctx @ w_in
        kv_ctx = p_ctx[..., EV_K0:]
    else:
        kv_ctx = h_ctx @ w_in[:, EV_K0:]
    k_ctx = qk_heads(kv_ctx[..., :EV_QK])
    v_ctx = v_heads(kv_ctx[..., EV_QK:])
    k_all = jnp.concatenate([k_lat, k_ctx], axis=3)
    v_all = jnp.concatenate([v_lat, v_ctx], axis=2)
    nb = length // Q_BLOCK
    q_blocks = jnp.moveaxis(q_lat.reshape(b, DIFF_HEADS, 2, nb, Q_BLOCK, DIFF_DQK), 3, 0)
    o = lax.map(lambda qb: diff_attend(qb, k_all, v_all, lam), q_blocks)
    o = jnp.moveaxis(o, 0, 2).reshape(b, DIFF_HEADS, length, DIFF_DV)
    y_lat = jnp.concatenate([conv_lat, diff_merge(o, subln_g, lambda_init)], axis=-1) @ w_out
    if not ctx_out:
        return y_lat, None
    conv_ctx = conformer_conv(p_ctx[..., :EV_Q0], conv_w, conv_b, ln_g, ln_b)
    o_ctx = diff_attend(qk_heads(p_ctx[..., EV_Q0:EV_K0]), k_ctx, v_ctx, lam)
    y_ctx = jnp.concatenate([conv_ctx, diff_merge(o_ctx, subln_g, lambda_init)], axis=-1) @ w_out
    return y_lat, y_ctx


def hyena_filters(length, w1, b1, freq1, w2, b2, freq2, w3):
    t = jnp.linspace(0.0, 1.0, length, dtype=F32)[:, None]
    bands = (HY_EMB - 1) // 2
    omega = 2.0 * math.pi * jnp.arange(length, dtype=F32)[:, None] / length
    ang = omega * jnp.linspace(1e-4, bands - 1, bands, dtype=F32)
    z = jnp.concatenate([t, jnp.cos(ang), -jnp.sin(ang)], axis=-1)
    hid = jnp.sin(freq1 * (z @ w1 + b1))
    hid = jnp.sin(freq2 * (hid @ w2 + b2))
    filt = (hid @ w3).astype(F32).reshape(length, HY_ORDER, 2, HY_CH)
    deltas = jnp.abs(jnp.linspace(HY_MIN_DECAY, HY_MAX_DECAY, HY_CH, dtype=F32))
    filt = filt * jnp.exp(-t * deltas)[:, None, None, :]
    return filt / jnp.sum(jnp.abs(filt), axis=(0, 2), keepdims=True)


def two_sided(h_fwd, h_bwd):
    return jnp.concatenate([h_fwd, jnp.zeros((1, h_fwd.shape[1]), h_fwd.dtype), h_bwd[:0:-1]], axis=0)


def fft_long_conv(u, filt2, skip):
    length = u.shape[1]
    n = 2 * length
    y = jnp.fft.irfft(jnp.fft.rfft(u, n=n, axis=1) * jnp.fft.rfft(filt2, n=n, axis=0), n=n, axis=1)[:, :length]
    return y + u * skip.astype(F32)


def hyena(p_hy, short_w, short_b, w1, b1, freq1, w2, b2, freq2, w3, skip):
    length = p_hy.shape[1]
    u = depthwise_conv(p_hy, short_w, short_b).astype(F32)
    v, x1, x2 = jnp.split(u, 3, axis=-1)
    filt = hyena_filters(length, w1, b1, freq1, w2, b2, freq2, w3)
    z = v
    for order, gate in enumerate((x1, x2)):
        z = gate * fft_long_conv(z, two_sided(filt[:, order, 0], filt[:, order, 1]), skip[order])
    return z.astype(p_hy.dtype)


def delta_features(p, conv_w, a_log_f, a_log_b, dtb_f, dtb_b, with_q):
    n_qkv = (3 if with_q else 2) * DN_WIDTH
    qkv = jax.nn.silu(depthwise_conv(p[..., :n_qkv], conv_w[:, -n_qkv:])).astype(F32)
    b, l = qkv.shape[:2]
    parts = [t.reshape(b, l, DN_HEADS, -1).transpose(0, 2, 1, 3) for t in jnp.split(qkv, n_qkv // DN_WIDTH, axis=-1)]
    if with_q:
        q, k, v = parts
        q = l2norm(q)
    else:
        k, v = parts
        q = None
    k = l2norm(k)
    rest = p[..., n_qkv:].astype(F32).transpose(0, 2, 1)
    h = DN_HEADS
    beta_f = jax.nn.sigmoid(rest[:, 0:h])
    beta_b = jax.nn.sigmoid(rest[:, h:2 * h])
    g_f = -jnp.exp(a_log_f.astype(F32))[None, :, None] * jax.nn.softplus(rest[:, 2 * h:3 * h] + dtb_f.astype(F32)[None, :, None])
    g_b = -jnp.exp(a_log_b.astype(F32))[None, :, None] * jax.nn.softplus(rest[:, 3 * h:4 * h] + dtb_b.astype(F32)[None, :, None])
    return q, k, v, beta_f, beta_b, g_f, g_b


def gated_delta_chunked(q, k, v, beta, g, state):
    b, h, length, dk = k.shape
    n = length // DN_CHUNK
    chunk = lambda t: t.reshape(b, h, n, DN_CHUNK, *t.shape[3:])
    k, v, beta, g = chunk(k), chunk(v), chunk(beta), chunk(g)
    gcum = jnp.cumsum(g, axis=-1)
    pos = jnp.arange(DN_CHUNK)
    incl = pos[:, None] >= pos[None, :]
    strict = pos[:, None] > pos[None, :]
    decay = jnp.exp(jnp.where(incl, gcum[..., :, None] - gcum[..., None, :], -jnp.inf))
    kb = k * beta[..., None]
    lower = jnp.where(strict, jnp.einsum('bhncd,bhnsd->bhncs', kb, k) * decay, 0.0)
    eye = jnp.eye(DN_CHUNK, dtype=F32)
    tmat = lax.linalg.triangular_solve(eye + lower, jnp.broadcast_to(eye, lower.shape),
                                       left_side=True, lower=True, unit_diagonal=True)
    u = tmat @ (v * beta[..., None])
    w = tmat @ (kb * jnp.exp(gcum)[..., None])
    g_last = gcum[..., -1]
    k_dec = k * jnp.exp(g_last[..., None] - gcum)[..., None]
    front = lambda t: jnp.moveaxis(t, 2, 0)
    if q is None:
        def step_state(s, xs):
            u_c, w_c, kd_c, gl_c = xs
            v_new = u_c - w_c @ s
            return s * jnp.exp(gl_c)[..., None, None] + jnp.swapaxes(kd_c, -1, -2) @ v_new, None
        state, _ = lax.scan(step_state, state, (front(u), front(w), front(k_dec), front(g_last)))
        return None, state
    q = chunk(q) * (dk ** -0.5)
    attn = jnp.einsum('bhncd,bhnsd->bhncs', q, k) * decay
    q_dec = q * jnp.exp(gcum)[..., None]

    def step(s, xs):
        u_c, w_c, kd_c, gl_c, a_c, qd_c = xs
        v_new = u_c - w_c @ s
        o = qd_c @ s + a_c @ v_new
        return s * jnp.exp(gl_c)[..., None, None] + jnp.swapaxes(kd_c, -1, -2) @ v_new, o
    state, o = lax.scan(step, state, (front(u), front(w), front(k_dec), front(g_last), front(attn), front(q_dec)))
    o = jnp.moveaxis(o, 0, 2).reshape(b, h, length, -1)
    return o, state


def flip_seq(t):
    return None if t is None else jnp.flip(t, axis=2)


def delta_output(o, gate, norm_g):
    b, h, l, dv = o.shape
    o = rms_norm(o.transpose(0, 2, 1, 3), norm_g)
    return (o * jax.nn.silu(gate.reshape(b, l, h, dv).astype(F32))).reshape(b, l, h * dv).astype(gate.dtype)


def odd_mixer(h_lat, h_ctx, w_in, hy_short_w, hy_short_b, hy_w1, hy_b1, hy_freq1, hy_w2, hy_b2, hy_freq2, hy_w3,
              hy_skip, dn_conv_w, a_log_f, a_log_b, dtb_f, dtb_b, dn_norm_g, w_out, ctx_out):
    hy_args = (hy_short_w, hy_short_b, hy_w1, hy_b1, hy_freq1, hy_w2, hy_b2, hy_freq2, hy_w3, hy_skip)
    dn_args = (a_log_f, a_log_b, dtb_f, dtb_b)
    p_lat = h_lat @ w_in
    hy_lat = hyena(p_lat[..., :HY_IN], *hy_args)
    q_l, k_l, v_l, bf_l, bb_l, gf_l, gb_l = delta_features(p_lat[..., OD_Q0:], dn_conv_w, *dn_args, True)
    if ctx_out:
        p_ctx = h_ctx @ w_in
        feats_ctx = delta_features(p_ctx[..., OD_Q0:], dn_conv_w, *dn_args, True)
    else:
        feats_ctx = delta_features(h_ctx @ w_in[:, OD_K0:], dn_conv_w, *dn_args, False)
    q_c, k_c, v_c, bf_c, bb_c, gf_c, gb_c = feats_ctx
    s0 = jnp.zeros((h_lat.shape[0], DN_HEADS, DN_DK, DN_DV), F32)
    o_cf, s_cf = gated_delta_chunked(q_c, k_c, v_c, bf_c, gf_c, s0)
    o_cb, s_cb = gated_delta_chunked(flip_seq(q_c), flip_seq(k_c), flip_seq(v_c), flip_seq(bb_c), flip_seq(gb_c), s0)
    o_lf, _ = gated_delta_chunked(q_l, k_l, v_l, bf_l, gf_l, s_cf)
    o_lb, _ = gated_delta_chunked(flip_seq(q_l), flip_seq(k_l), flip_seq(v_l), flip_seq(bb_l), flip_seq(gb_l), s_cb)
    dn_lat = delta_output(o_lf + flip_seq(o_lb), p_lat[..., OD_G0:OD_Q0], dn_norm_g)
    y_lat = jnp.concatenate([hy_lat, dn_lat], axis=-1) @ w_out
    if not ctx_out:
        return y_lat, None
    hy_ctx = hyena(p_ctx[..., :HY_IN], *hy_args)
    dn_ctx = delta_output(o_cf + flip_seq(o_cb), p_ctx[..., OD_G0:OD_Q0], dn_norm_g)
    y_ctx = jnp.concatenate([hy_ctx, dn_ctx], axis=-1) @ w_out
    return y_lat, y_ctx


def setup_inputs(seed: int = 0) -> dict:
    key = jax.random.key(seed)
    ks = list(jax.random.split(key, 48))
    nrm = lambda shape, scale: scale * jax.random.normal(ks.pop(), shape, F32)
    gain = lambda shape: 1.0 + 0.05 * jax.random.normal(ks.pop(), shape, F32)
    d = D_MODEL
    a_log = lambda: jnp.log(jax.random.uniform(ks.pop(), (N_ODD, DN_HEADS), F32, 1.0, 16.0))

    def dt_bias():
        dt = jnp.exp(jax.random.uniform(ks.pop(), (N_ODD, DN_HEADS), F32, math.log(1e-3), math.log(1e-1)))
        return dt + jnp.log(-jnp.expm1(-dt))
    return {
        'x': nrm((BATCH, SEQ, d), 1.0),
        'c': nrm((BATCH, d), 1.0),
        'ctx': nrm((BATCH, CTX_LEN, d), 1.0),
        'c_ctx': nrm((d,), 1.0),
        'ada_w': nrm((DEPTH, d, N_MOD * d), 0.5 * d ** -0.5),
        'ada_b': nrm((DEPTH, N_MOD * d), 0.02),
        'norm1_g': gain((DEPTH, d)),
        'norm2_g': gain((DEPTH, d)),
        'mlp_w1': nrm((DEPTH, d, MLP_HIDDEN), d ** -0.5),
        'mlp_w2': nrm((DEPTH, MLP_HIDDEN, d), MLP_HIDDEN ** -0.5),
        'ev_w_in': nrm((N_EVEN, d, EVEN_IN), d ** -0.5),
        'ev_conv_w': nrm((N_EVEN, CONV_W, CONV_CH), CONV_W ** -0.5),
        'ev_conv_b': nrm((N_EVEN, CONV_CH), 0.02),
        'ev_ln_g': gain((N_EVEN, CONV_CH)),
        'ev_ln_b': nrm((N_EVEN, CONV_CH), 0.02),
        'ev_lq1': nrm((N_EVEN, DIFF_DQK), 0.1),
        'ev_lk1': nrm((N_EVEN, DIFF_DQK), 0.1),
        'ev_lq2': nrm((N_EVEN, DIFF_DQK), 0.1),
        'ev_lk2': nrm((N_EVEN, DIFF_DQK), 0.1),
        'ev_subln_g': gain((N_EVEN, DIFF_DV)),
        'ev_w_out': nrm((N_EVEN, EVEN_MIX, d), EVEN_MIX ** -0.5),
        'od_w_in': nrm((N_ODD, d, ODD_IN), d ** -0.5),
        'od_hy_short_w': nrm((N_ODD, HY_SHORT_W, HY_IN), HY_SHORT_W ** -0.5),
        'od_hy_short_b': nrm((N_ODD, HY_IN), 0.02),
        'od_hy_w1': nrm((N_ODD, HY_EMB, HY_FFN), HY_EMB ** -0.5),
        'od_hy_b1': nrm((N_ODD, HY_FFN), 0.02),
        'od_hy_freq1': gain((N_ODD, HY_FFN)),
        'od_hy_w2': nrm((N_ODD, HY_FFN, HY_FFN), HY_FFN ** -0.5),
        'od_hy_b2': nrm((N_ODD, HY_FFN), 0.02),
        'od_hy_freq2': gain((N_ODD, HY_FFN)),
        'od_hy_w3': nrm((N_ODD, HY_FFN, HY_ORDER * 2 * HY_CH), HY_FFN ** -0.5),
        'od_hy_skip': nrm((N_ODD, HY_ORDER, HY_CH), 0.5),
        'od_dn_conv_w': nrm((N_ODD, DN_CONV_W, 3 * DN_WIDTH), DN_CONV_W ** -0.5),
        'od_dn_alog_f': a_log(),
        'od_dn_alog_b': a_log(),
        'od_dn_dtb_f': dt_bias(),
        'od_dn_dtb_b': dt_bias(),
        'od_dn_norm_g': gain((N_ODD, DN_DV)),
        'od_w_out': nrm((N_ODD, ODD_MIX, d), ODD_MIX ** -0.5),
        'final_g': gain((d,)),
    }


def reference(x, c, ctx, c_ctx, ada_w, ada_b, norm1_g, norm2_g, mlp_w1, mlp_w2,
              ev_w_in, ev_conv_w, ev_conv_b, ev_ln_g, ev_ln_b, ev_lq1, ev_lk1, ev_lq2, ev_lk2, ev_subln_g, ev_w_out,
              od_w_in, od_hy_short_w, od_hy_short_b, od_hy_w1, od_hy_b1, od_hy_freq1, od_hy_w2, od_hy_b2, od_hy_freq2,
              od_hy_w3, od_hy_skip, od_dn_conv_w, od_dn_alog_f, od_dn_alog_b, od_dn_dtb_f, od_dn_dtb_b, od_dn_norm_g,
              od_w_out, final_g):
    rope = axial_rope_tables(x.shape[1])
    cond_lat = jax.nn.silu(c)
    cond_ctx = jax.nn.silu(c_ctx)
    for i in range(DEPTH):
        last = i == DEPTH - 1
        mods = jnp.split(cond_lat @ ada_w[i] + ada_b[i], N_MOD, axis=-1)
        sh1, sc1, gt1, sh2, sc2, gt2 = [m[:, None, :] for m in mods]
        n_ctx_mod = 2 if last else N_MOD
        mods_c = jnp.split(cond_ctx @ ada_w[i][:, :n_ctx_mod * D_MODEL] + ada_b[i][:n_ctx_mod * D_MODEL], n_ctx_mod)
        h_lat = modulate(rms_norm(x, norm1_g[i]), sh1, sc1)
        h_ctx = modulate(rms_norm(ctx, norm1_g[i]), mods_c[0], mods_c[1])
        j = i // 2
        if i % 2 == 0:
            y_lat, y_ctx = even_mixer(h_lat, h_ctx, rope, ev_w_in[j], ev_conv_w[j], ev_conv_b[j], ev_ln_g[j], ev_ln_b[j],
                                      ev_lq1[j], ev_lk1[j], ev_lq2[j], ev_lk2[j], ev_subln_g[j], ev_w_out[j],
                                      0.8 - 0.6 * math.exp(-0.3 * i), not last)
        else:
            y_lat, y_ctx = odd_mixer(h_lat, h_ctx, od_w_in[j], od_hy_short_w[j], od_hy_short_b[j], od_hy_w1[j],
                                     od_hy_b1[j], od_hy_freq1[j], od_hy_w2[j], od_hy_b2[j], od_hy_freq2[j], od_hy_w3[j],
                                     od_hy_skip[j], od_dn_conv_w[j], od_dn_alog_f[j], od_dn_alog_b[j], od_dn_dtb_f[j],
                                     od_dn_dtb_b[j], od_dn_norm_g[j], od_w_out[j], not last)
        x = x + gt1 * y_lat
        x = x + gt2 * squared_relu_mlp(modulate(rms_norm(x, norm2_g[i]), sh2, sc2), mlp_w1[i], mlp_w2[i])
        if not last:
            ctx = ctx + mods_c[2] * y_ctx
            ctx = ctx + mods_c[5] * squared_relu_mlp(modulate(rms_norm(ctx, norm2_g[i]), mods_c[3], mods_c[4]),
                                                     mlp_w1[i], mlp_w2[i])
    return rms_norm(x, final_g)
```

```python
import math
import os
import numpy as np
import concourse.bass as bass
import concourse.mybir as mybir
from concourse.bass_utils import run_bass_kernel_spmd

F32 = mybir.dt.float32
BF16 = mybir.dt.bfloat16
AF = mybir.ActivationFunctionType
ALU = mybir.AluOpType
AX = mybir.AxisListType

ENGS = ("pe", "act", "dve", "pool", "sp")
N_DMA_SEMS = 48

D = 1024
L = 8192
LC = 256
T = L + LC
EPS = 1e-6


class Buf:
    def __init__(self, name, t):
        self.name = name
        self.t = t

    def __getitem__(self, idx):
        return self.t[idx]

    def rearrange(self, *a, **k):
        return self.t.rearrange(*a, **k)


class Prog:
    ENGOBJ = {"pe": "tensor", "act": "scalar", "dve": "vector", "pool": "gpsimd", "sp": "sync"}

    def __init__(self):
        self.nc = bass.Bass("TRN2", target_bir_lowering=False)
        self.cnt = {e: 0 for e in ENGS}
        self.seen = {e: {} for e in ENGS}
        self.last_w = {}
        self.readers = {}
        self.dma_rr = 0
        self.dma_val = [0] * N_DMA_SEMS
        self.scopes = [[]]
        self.n_ops = 0
        self.sems = {}
        self.uid = 0
        self.ext_in = []
        for n in list(ENGS) + ["dma%d" % i for i in range(N_DMA_SEMS)]:
            self.sems[n] = self._enter(self.nc.semaphore("s_" + n))

    def _enter(self, cm):
        v = cm.__enter__()
        self.scopes[-1].append(cm)
        return v

    def push(self):
        self.scopes.append([])

    def pop(self):
        for cm in reversed(self.scopes.pop()):
            cm.__exit__(None, None, None)

    def sbuf(self, name, shape, dt):
        self.uid += 1
        name = "%s_%d" % (name, self.uid)
        return Buf(name, self._enter(self.nc.sbuf_tensor(name, list(shape), dt)))

    def psum(self, name, shape, dt=F32):
        return Buf(name, self._enter(self.nc.psum_tensor(name, list(shape), dt)))

    def dram(self, name, shape, dt, kind="Internal"):
        if kind == "ExternalInput":
            self.ext_in.append(name)
        return Buf(name, self.nc.dram_tensor(name, list(shape), dt, kind=kind).ap())

    @staticmethod
    def _k(k):
        return k.name if isinstance(k, Buf) else k

    def _deps(self, reads, writes):
        deps = []
        for k in reads:
            k = self._k(k)
            if k in self.last_w:
                deps.append(self.last_w[k])
        for k in writes:
            k = self._k(k)
            if k in self.last_w:
                deps.append(self.last_w[k])
            deps.extend(self.readers.get(k, ()))
        return deps

    def _record(self, ev, reads, writes):
        for k in reads:
            k = self._k(k)
            self.readers.setdefault(k, []).append(ev)
        for k in writes:
            k = self._k(k)
            self.last_w[k] = ev
            self.readers[k] = []

    def _emit_waits(self, eng, deps):
        need = {}
        for (sname, val) in deps:
            if sname == "pe" and eng == "pe":
                continue
            if val > self.seen[eng].get(sname, 0) and val > need.get(sname, 0):
                need[sname] = val
        for sname, val in need.items():
            self.seen[eng][sname] = val
            getattr(self.nc, self.ENGOBJ[eng]).wait_ge(self.sems[sname], val)

    def op(self, eng, fn, reads=(), writes=()):
        deps = self._deps(reads, writes)
        self._emit_waits(eng, deps)
        self.cnt[eng] += 1
        ev = (eng, self.cnt[eng])
        fn(getattr(self.nc, self.ENGOBJ[eng])).then_inc(self.sems[eng], 1)
        self._record(ev, reads, writes)
        self.n_ops += 1
        return ev

    def dma(self, q, out, in_, reads=(), writes=(), **kw):
        deps = self._deps(reads, writes)
        k = self.dma_rr
        self.dma_rr = (self.dma_rr + 1) % N_DMA_SEMS
        sname = "dma%d" % k
        if self.dma_val[k] > 0:
            deps = list(deps) + [(sname, self.dma_val[k])]
        self._emit_waits(q, deps)
        self.dma_val[k] += 16
        ev = (sname, self.dma_val[k])
        getattr(self.nc, self.ENGOBJ[q]).dma_start(out=out, in_=in_, **kw).then_inc(self.sems[sname], 16)
        self._record(ev, reads, writes)
        self.n_ops += 1
        return ev

    def barrier(self):
        evs = [(e, self.cnt[e]) for e in ENGS if self.cnt[e] > 0]
        evs += [("dma%d" % i, v) for i, v in enumerate(self.dma_val) if v > 0]
        for e in ENGS:
            self._emit_waits(e, evs)

    def finish(self, final_events):
        self._emit_waits("sp", final_events)
        while self.scopes:
            self.pop()
        return self.nc


def fm(v):
    v = np.asarray(v, np.float32).reshape(-1, 128)
    return np.ascontiguousarray(v.T)


PV_SPEC = [
    ("c", 8), ("cctx", 8), ("ada_b0", 48), ("ada_b1", 48),
    ("n1g0", 8), ("n1g1", 8), ("n2g0", 8), ("n2g1", 8), ("fing", 8),
    ("ev_conv_w", 124), ("ev_conv_b", 4), ("ev_ln_g", 4), ("ev_ln_b", 4), ("subln_g", 1),
    ("lqk", 256),
    ("hy_sw", 36), ("hy_sb", 12), ("dn_cw", 60),
]
PV_OFF = {}
_o = 0
for _n, _c in PV_SPEC:
    PV_OFF[_n] = (_o, _c)
    _o += _c
NV = _o
PB_SPEC = [("dtb", 8), ("alog", 8), ("dn_ng", 128)]
PB_OFF = {}
_o = 0
for _n, _c in PB_SPEC:
    PB_OFF[_n] = (_o, _c)
    _o += _c
NPB = _o


def q_perm():
    A = [[], []]
    Bt = [[], []]
    for g in range(8):
        j = g // 4
        base = g * 64
        A[j] += list(range(base, base + 16)) + list(range(base + 32, base + 48))
        Bt[j] += list(range(base + 16, base + 32)) + list(range(base + 48, base + 64))
    return np.array(A[0] + A[1] + Bt[0] + Bt[1])


def rope_tables():
    n_freq = 16
    inv = (10000.0 ** (-np.arange(n_freq, dtype=np.float32) / np.float32(n_freq))).astype(np.float32)
    t = np.arange(L)
    row = (t // 64).astype(np.float32)
    col = (t % 64).astype(np.float32)
    ang = np.zeros((32, L), np.float32)
    ang[:16] = (row[None, :] * inv[:, None]).astype(np.float32)
    ang[16:] = (col[None, :] * inv[:, None]).astype(np.float32)
    C = np.ones((128, T), np.float32)
    S = np.zeros((128, T), np.float32)
    C[:, :L] = np.tile(np.cos(ang), (4, 1))
    S[:, :L] = np.tile(np.sin(ang), (4, 1))
    return C, S


def pack_core(inp, b, consts):
    m = {}
    m["xT"] = np.ascontiguousarray(np.concatenate([inp["x"][b].T, inp["ctx"][b].T], axis=1), np.float32)
    pv = np.zeros((128, NV), np.float32)

    def put(name, arr):
        o, c = PV_OFF[name]
        assert arr.shape == (128, c), (name, arr.shape, c)
        pv[:, o:o + c] = arr
    put("c", fm(inp["c"][b]))
    put("cctx", fm(inp["c_ctx"]))
    put("ada_b0", fm(inp["ada_b"][0]))
    put("ada_b1", fm(inp["ada_b"][1]))
    put("n1g0", fm(inp["norm1_g"][0])); put("n1g1", fm(inp["norm1_g"][1]))
    put("n2g0", fm(inp["norm2_g"][0])); put("n2g1", fm(inp["norm2_g"][1]))
    put("fing", fm(inp["final_g"]))
    put("ev_conv_w", np.ascontiguousarray(inp["ev_conv_w"][0].reshape(31, 4, 128).transpose(2, 0, 1).reshape(128, 124)))
    put("ev_conv_b", fm(inp["ev_conv_b"][0])); put("ev_ln_g", fm(inp["ev_ln_g"][0])); put("ev_ln_b", fm(inp["ev_ln_b"][0]))
    put("subln_g", fm(inp["ev_subln_g"][0]))
    lqk = np.concatenate([inp["ev_lq1"][0], inp["ev_lk1"][0], inp["ev_lq2"][0], inp["ev_lk2"][0]])
    put("lqk", np.ascontiguousarray(np.broadcast_to(lqk[None, :], (128, 256))))
    put("hy_sw", np.ascontiguousarray(inp["od_hy_short_w"][0].reshape(3, 12, 128).transpose(2, 0, 1).reshape(128, 36)))
    put("hy_sb", fm(inp["od_hy_short_b"][0]))
    put("dn_cw", np.ascontiguousarray(inp["od_dn_conv_w"][0].reshape(5, 12, 128).transpose(2, 0, 1).reshape(128, 60)))
    m["pv"] = pv
    m.update(consts)
    return m


def hyena_consts(inp):
    c = {}
    NF = 16384
    i128 = np.arange(128, dtype=np.float64)
    th = 2.0 * np.pi * np.outer(i128, i128) / 128.0
    ph = 2.0 * np.pi * np.outer(i128, i128) / NF
    hyc = np.zeros((128, 1280), np.float64)
    hyc[:, 0:128] = np.cos(th); hyc[:, 128:256] = -np.sin(th)
    hyc[:, 256:384] = np.cos(th); hyc[:, 384:512] = -np.sin(th); hyc[:, 512:640] = np.sin(th)
    hyc[:, 640:768] = np.cos(th); hyc[:, 768:896] = np.sin(th)
    hyc[:, 896:1024] = -np.sin(th); hyc[:, 1024:1152] = np.cos(th)
    hyc[:, 1152:1216] = np.cos(th)[:, 0:64] / NF; hyc[:, 1216:1280] = -np.sin(th)[:, 0:64] / NF
    c["hyc"] = hyc.astype(np.float32)
    hyt = np.zeros((128, 4, 2, 128), np.float64)
    hyt[:, 0] = np.cos(ph)[:, None, :]; hyt[:, 1] = (-np.sin(ph))[:, None, :]
    hyt[:, 2] = np.cos(ph)[:, None, :]; hyt[:, 3] = np.sin(ph)[:, None, :]
    c["hyt"] = hyt.astype(np.float32)
    t = np.linspace(0.0, 1.0, L, dtype=np.float32)
    bands = 16
    omega = (np.float32(2.0 * math.pi) * np.arange(L, dtype=np.float32) / np.float32(L)).astype(np.float32)
    ang = (omega[:, None] * np.linspace(1e-4, bands - 1, bands, dtype=np.float32)[None, :]).astype(np.float32)
    z = np.concatenate([t[:, None], np.cos(ang), -np.sin(ang)], axis=-1).astype(np.float32)
    n = np.arange(NF)
    pos = np.where(n < L, n, np.where(n == L, 0, NF - n))
    c["Z2"] = np.ascontiguousarray(z[pos].T)
    max_decay = math.log(1e-2) / 0.3
    min_decay = math.log(1e-2) / 1.5
    deltas = np.abs(np.linspace(min_decay, max_decay, 512, dtype=np.float32)).astype(np.float32)
    c["dlb"] = np.ascontiguousarray(np.broadcast_to(deltas[None, :], (128, 512))).astype(np.float32)
    c["tcoln"] = np.ascontiguousarray((-t[pos]).reshape(128, 128).T).astype(np.float32)
    c["hy_w1"] = np.ascontiguousarray(inp["od_hy_w1"][0]); c["hy_w2"] = np.ascontiguousarray(inp["od_hy_w2"][0])
    c["hy_w3"] = np.ascontiguousarray(inp["od_hy_w3"][0])
    c["hyv"] = np.ascontiguousarray(np.stack([inp["od_hy_b1"][0], inp["od_hy_freq1"][0], inp["od_hy_b2"][0], inp["od_hy_freq2"][0]], axis=1), np.float32)
    c["skc"] = np.ascontiguousarray(np.broadcast_to(inp["od_hy_skip"][0].reshape(1, 1024), (128, 1024)), np.float32)
    return c


def make_consts(inp):
    c = {}
    w = inp["ev_w_in"][0]
    qp = q_perm()
    cols = np.concatenate([np.arange(1024), 1024 + qp, 1536 + qp, np.arange(2048, 2560)])
    c["evw"] = np.ascontiguousarray(w[:, cols])
    c["evwo"] = np.ascontiguousarray(inp["ev_w_out"][0])
    c["ada_w"] = np.ascontiguousarray(inp["ada_w"])
    c["w1"] = np.ascontiguousarray(inp["mlp_w1"])
    c["w2"] = np.ascontiguousarray(inp["mlp_w2"])
    c["odw"] = np.ascontiguousarray(inp["od_w_in"][0])
    pb = np.zeros((128, NPB), np.float32)
    rows = {"dtb": np.concatenate([inp["od_dn_dtb_f"][0], inp["od_dn_dtb_b"][0]]),
            "alog": np.concatenate([inp["od_dn_alog_f"][0], inp["od_dn_alog_b"][0]]),
            "dn_ng": inp["od_dn_norm_g"][0]}
    for k_, v_ in rows.items():
        o_, c_ = PB_OFF[k_]
        pb[:, o_:o_ + c_] = np.broadcast_to(np.asarray(v_, np.float32)[None, :], (128, c_))
    c["pb"] = pb
    dnm = np.zeros((128, 6, 4, 128), np.float32)
    ii = np.arange(128)
    Ls = (ii[:, None] > ii[None, :]).astype(np.float32)
    Us_ = (ii[:, None] < ii[None, :]).astype(np.float32)
    Ui = (ii[:, None] <= ii[None, :]).astype(np.float32)
    Li = (ii[:, None] >= ii[None, :]).astype(np.float32)
    isq = np.float32(1.0 / math.sqrt(128.0))
    dnm[:, 0] = Ls[:, None, :]; dnm[:, 1] = Us_[:, None, :]
    dnm[:, 2] = (Ui * isq)[:, None, :]; dnm[:, 3] = (Li * isq)[:, None, :]
    dnm[:, 4] = np.eye(128, dtype=np.float32)[:, None, :]
    dnm[:, 5, 0] = Ui; dnm[:, 5, 1] = Li; dnm[:, 5, 2] = 1.0
    c["dnm"] = dnm
    dnk = np.zeros((128, 7, 4, 128), np.float32)
    for k_ in range(7):
        b_ = 2 ** k_
        mk = ((ii[:, None] // (2 * b_)) == (ii[None, :] // (2 * b_))) & ((ii[:, None] // b_) != (ii[None, :] // b_))
        dnk[:, k_] = mk.astype(np.float32)[:, None, :]
    c["dnk"] = dnk
    c.update(hyena_consts(inp))
    c["odwo"] = np.ascontiguousarray(inp["od_w_out"][0])
    c["identf"] = np.eye(128, dtype=np.float32)
    C, S = rope_tables()
    c["ropeC"] = C
    c["ropeS"] = S
    return c


NBLK = 17


def blk_range(blk):
    t0 = blk * 512
    nt = 512 if blk < 16 else 256
    s = 0 if blk < 16 else 1
    return t0, nt, s


UW = 8508


def ucol(t0):
    return 15 + t0 if t0 < L else 8237 + (t0 - L)


def build(debug=(), layers=(0, 1), l1parts=("A", "B", "DN", "CMB", "HYF", "HY", "OUT"), ext_in=()):
    P = Prog()
    nc = P.nc
    op = P.op

    def mm(out, lhsT, rhs, start, stop, reads, writes):
        op("pe", lambda e: e.matmul(out, lhsT, rhs, start=start, stop=stop), reads, writes)

    xT = P.dram("xT", [D, T], F32, "ExternalInput")
    pvd = P.dram("pv", [128, NV], F32, "ExternalInput")
    evw = P.dram("evw", [D, 2560], F32, "ExternalInput")
    evwo = P.dram("evwo", [D, D], F32, "ExternalInput")
    adaw = P.dram("ada_w", [2, D, 6 * D], F32, "ExternalInput")
    w1d = P.dram("w1", [2, D, 4 * D], F32, "ExternalInput")
    w2d = P.dram("w2", [2, 4 * D, D], F32, "ExternalInput")
    ropeC = P.dram("ropeC", [128, T], F32, "ExternalInput")
    ropeS = P.dram("ropeS", [128, T], F32, "ExternalInput")

    def scratch(name, shape, dt):
        kind = "ExternalInput" if name in ext_in else ("ExternalOutput" if name in debug else "Internal")
        return P.dram(name, shape, dt, kind)

    Us = scratch("Us", [512, UW], BF16)
    QTs = scratch("QTs", [512, T], BF16)
    KTs = scratch("KTs", [512, T], BF16)
    Vs = scratch("Vs", [T, 512], BF16)
    AOs = scratch("AOs", [512, T], BF16)
    XAs = scratch("XAs", [D, T], F32)
    X1s = P.dram("X1s", [D, T], F32, "ExternalInput" if 0 not in layers else ("ExternalOutput" if ("X1s" in debug or 1 not in layers) else "Internal"))

    Ys = P.dram("yT", [D, L], F32, "ExternalOutput" if 1 in layers else "Internal")
    Yv = Ys.rearrange("(k p) t -> p k t", p=128)
    ps_all = P.psum("ps_all", [128, 4096])
    ps = [Buf("ps%d" % i, ps_all[:, i * 512:(i + 1) * 512]) for i in range(8)]
    sc3 = [Buf("sc3_%d" % i, ps_all[:, i * 1536:(i + 1) * 1536]) for i in range(2)]
    pvt = P.sbuf("pvt", [128, NV], F32)
    ones_bf = P.sbuf("ones_bf", [128, 128], BF16)
    onesm_bf = P.sbuf("onesm_bf", [128, 128], BF16)
    epst = P.sbuf("epst", [128, 1], F32)
    zt = P.sbuf("zt", [128, 4, 30], BF16)
    identd = P.dram("identf", [128, 128], F32, "ExternalInput")
    identf = P.sbuf("identf", [128, 128], F32)
    identb = P.sbuf("identb", [128, 128], BF16)
    ones_f32 = P.sbuf("ones_f32", [128, 128], F32)
    op("dve", lambda e: e.memset(ones_f32[:], 1.0), writes=[ones_f32])
    P.dma("sp", identf[:], identd[:], writes=[identf])
    op("act", lambda e: e.copy(identb[:], identf[:]), [identf], [identb])
    P.dma("sp", pvt[:], pvd[:], writes=[pvt])
    op("dve", lambda e: e.memset(ones_bf[:], 1.0), writes=[ones_bf])
    op("dve", lambda e: e.memset(onesm_bf[:], 1.0 / 512.0), writes=[onesm_bf])
    op("dve", lambda e: e.memset(epst[:], EPS), writes=[epst])
    op("dve", lambda e: e.memset(zt[:], 0.0), writes=[zt])

    stg = [P.sbuf("stg", [128, 1024], F32) for _ in range(2)]
    stg_i = [0]

    def load_cast(dst_buf, dst_fn, src_ap, ncols, rows=128):
        c0 = 0
        while c0 < ncols:
            n = min(1024, ncols - c0)
            st = stg[stg_i[0] % 2]
            stg_i[0] += 1
            P.dma("sp", st[0:rows, 0:n], src_ap[:, c0:c0 + n], writes=[st])
            d_ = dst_fn(c0, n)
            if stg_i[0] % 2 == 0:
                op("dve", lambda e: e.tensor_copy(d_, st[0:rows, 0:n]), [st], [dst_buf])
            else:
                op("act", lambda e: e.copy(d_, st[0:rows, 0:n]), [st], [dst_buf])
            c0 += n

    def pvs(name, j=0, n=None):
        o, c = PV_OFF[name]
        n = c - j if n is None else n
        return pvt[:, o + j:o + j + n]

    mod = [P.sbuf("modL", [128, 48], F32), P.sbuf("modC", [128, 48], F32)]
    A1 = [P.sbuf("A1L", [128, 8], F32), P.sbuf("A1C", [128, 8], F32)]
    A2 = [P.sbuf("A2L", [128, 8], F32), P.sbuf("A2C", [128, 8], F32)]
    scs = P.sbuf("scs", [128, 8, 2], F32)
    op("act", lambda e: e.activation(scs[:, :, 0], pvs("c"), AF.Silu), reads=[pvt], writes=[scs])
    op("act", lambda e: e.activation(scs[:, :, 1], pvs("cctx"), AF.Silu), reads=[pvt], writes=[scs])

    def prep_mods(li):
        P.push()
        aw = [P.sbuf("aw", [128, 8, 1024], F32) for _ in range(2)]
        pm = ps[0]
        for m6 in range(6):
            a = aw[m6 % 2]
            P.dma("sp", a[:], adaw[li].rearrange("(k p) n -> p k n", p=128)[:, :, m6 * 1024:(m6 + 1) * 1024], writes=[a])
            for fc in range(8):
                j = m6 * 8 + fc
                for k in range(8):
                    mm(pm[:, 2 * j:2 * j + 2], a[:, k, fc * 128:(fc + 1) * 128], scs[:, k, :], k == 0, k == 7, [a, scs], [pm])
        pm3 = pm[:, 0:96].rearrange("p (j s) -> p j s", s=2)
        for s in range(2):
            op("dve", lambda e: e.tensor_tensor(mod[s][:], pm3[:, :, s], pvs("ada_b%d" % li), ALU.add), [pm, pvt], [mod[s]])
            op("dve", lambda e: e.scalar_tensor_tensor(A1[s][:], mod[s][:, 8:16], 1.0, pvs("n1g%d" % li), ALU.add, ALU.mult),
               [mod[s], pvt], [A1[s]])
            op("dve", lambda e: e.scalar_tensor_tensor(A2[s][:], mod[s][:, 32:40], 1.0, pvs("n2g%d" % li), ALU.add, ALU.mult),
               [mod[s], pvt], [A2[s]])
        P.barrier()
        P.pop()

    def norm_mod(xs, nt, Avec, shbuf, sho, sq, rt, rstd, tmp, h, pss):
        op("act", lambda e: e.activation(sq[:, :, :nt], xs[:, :, :nt], AF.Square), [xs], [sq])
        for k in range(8):
            mm(pss[:, :nt], ones_bf[:], sq[:, k, :nt], k == 0, k == 7, [ones_bf, sq], [pss])
        op("act", lambda e: e.activation(rt[:, :nt], pss[:, :nt], AF.Sqrt, bias=epst[:], scale=1.0 / D), [pss, epst], [rt])
        op("dve", lambda e: e.reciprocal(rstd[:, :nt], rt[:, :nt]), [rt], [rstd])
        for k in range(8):
            op("dve", lambda e: e.scalar_tensor_tensor(tmp[:, k, :nt], xs[:, k, :nt], Avec[:, k:k + 1], rstd[:, :nt], ALU.mult, ALU.mult),
               [xs, rstd, Avec], [tmp])
        for k in range(8):
            op("act", lambda e: e.activation(h[:, k, :nt], tmp[:, k, :nt], AF.Identity, bias=shbuf[:, sho + k:sho + k + 1]), [tmp, shbuf], [h])

    xTv = xT.rearrange("(k p) t -> p k t", p=128)
    XAv = XAs.rearrange("(k p) t -> p k t", p=128)
    X1v = X1s.rearrange("(k p) t -> p k t", p=128)

    def layer0():
        prep_mods(0)

        lamt = P.sbuf("lamt", [128, 4], F32)
        neglam = P.sbuf("neglam", [128, 1], F32)
        gsub = P.sbuf("gsub", [128, 1], F32)
        lam_init0 = 0.8 - 0.6 * math.exp(-0.3 * 0)
        lq = pvs("lqk")
        ltmp = P.sbuf("ltmp", [128, 2, 64], F32)
        op("dve", lambda e: e.tensor_tensor(ltmp[:, 0, :], lq[:, 0:64], lq[:, 64:128], ALU.mult), [pvt], [ltmp])
        op("dve", lambda e: e.tensor_tensor(ltmp[:, 1, :], lq[:, 128:192], lq[:, 192:256], ALU.mult), [pvt], [ltmp])
        op("dve", lambda e: e.tensor_reduce(lamt[:, 0:2], ltmp[:], AX.X, ALU.add), [ltmp], [lamt])
        op("act", lambda e: e.activation(lamt[:, 2:4], lamt[:, 0:2], AF.Exp), [lamt], [lamt])
        op("dve", lambda e: e.scalar_tensor_tensor(neglam[:], lamt[:, 3:4], -lam_init0, lamt[:, 2:3], ALU.add, ALU.subtract), [lamt], [neglam])
        op("dve", lambda e: e.tensor_scalar(gsub[:], pvs("subln_g"), 1.0 - lam_init0, None, ALU.mult), [pvt], [gsub])

        Uv = Us.rearrange("(c p) t -> p c t", p=128)
        P.dma("sp", Uv[:, :, 0:15], zt[:, :, 0:15], reads=[zt], writes=[Us])
        P.dma("sp", Uv[:, :, 8207:8237], zt[:, :, 0:30], reads=[zt], writes=[Us])
        P.dma("sp", Uv[:, :, 8493:8508], zt[:, :, 0:15], reads=[zt], writes=[Us])

        P.push()
        win = P.sbuf("win", [128, 8, 2560], BF16)
        for k in range(8):
            load_cast(win, lambda c0, n, k=k: win[:, k, c0:c0 + n], evw[k * 128:(k + 1) * 128, :], 2560)
        xsb = [P.sbuf("xs", [128, 8, 512], F32) for _ in range(2)]
        csb = [P.sbuf("cs", [128, 2, 512], F32) for _ in range(2)]
        sq = P.sbuf("sq", [128, 8, 512], BF16)
        rt = P.sbuf("rt", [128, 512], F32)
        rstd = P.sbuf("rstd", [128, 512], F32)
        tmp = P.sbuf("tmp", [128, 8, 512], F32)
        h = P.sbuf("h", [128, 8, 512], BF16)
        sig = [P.sbuf("sig", [128, 512], F32) for _ in range(2)]
        ub = [P.sbuf("ub", [128, 4, 512], BF16) for _ in range(2)]
        rtm = [P.sbuf("rtm", [128, 4, 512], F32) for _ in range(2)]
        qkb = [P.sbuf("qkb", [128, 4, 512], BF16) for _ in range(4)]
        vtb = [P.sbuf("vtb", [128, 4, 512], BF16) for _ in range(2)]
        QTv = QTs.rearrange("(c p) t -> p c t", p=128)
        KTv = KTs.rearrange("(c p) t -> p c t", p=128)
        Vv = Vs.rearrange("(n p) f -> p n f", p=128)
        pbank = [1]

        def nb():
            pbank[0] = 1 + (pbank[0] % 7)
            return ps[pbank[0]]

        def loadA(blk):
            t0, nt, s = blk_range(blk)
            xs = xsb[blk % 2]
            cs = csb[blk % 2]
            P.dma("sp", xs[:, :, :nt], xTv[:, :, t0:t0 + nt], writes=[xs])
            P.dma("sp", cs[:, 0, :nt], ropeC[:, t0:t0 + nt], writes=[cs])
            P.dma("sp", cs[:, 1, :nt], ropeS[:, t0:t0 + nt], writes=[cs])

        loadA(0)
        for blk in range(NBLK):
            t0, nt, s = blk_range(blk)
            if blk + 1 < NBLK:
                loadA(blk + 1)
            xs = xsb[blk % 2]
            cs = csb[blk % 2]
            norm_mod(xs, nt, A1[s], mod[s], 0, sq, rt, rstd, tmp, h, ps[0])
            u = ub[blk % 2]
            for i in range(4):
                pa = nb()
                pg = nb()
                for k in range(8):
                    mm(pa[:, :nt], win[:, k, i * 128:(i + 1) * 128], h[:, k, :nt], k == 0, k == 7, [win, h], [pa])
                for k in range(8):
                    mm(pg[:, :nt], win[:, k, (4 + i) * 128:(5 + i) * 128], h[:, k, :nt], k == 0, k == 7, [win, h], [pg])
                sg = sig[i % 2]
                op("act", lambda e: e.activation(sg[:, :nt], pg[:, :nt], AF.Sigmoid), [pg], [sg])
                op("dve", lambda e: e.tensor_tensor(u[:, i, :nt], pa[:, :nt], sg[:, :nt], ALU.mult), [pa, sg], [u])
            c0 = ucol(t0)
            P.dma("sp", Uv[:, :, c0:c0 + nt], u[:, :, :nt], reads=[u], writes=[Us])
            for wi, (base, dst) in enumerate(((1024, QTv), (1536, KTv))):
                qk = qkb[(blk % 2) * 2 + wi]
                r = rtm[wi]
                for j in range(2):
                    pA = nb()
                    pB = nb()
                    for k in range(8):
                        mm(pA[:, :nt], win[:, k, base + j * 128:base + (j + 1) * 128], h[:, k, :nt], k == 0, k == 7, [win, h], [pA])
                    for k in range(8):
                        mm(pB[:, :nt], win[:, k, base + 256 + j * 128:base + 256 + (j + 1) * 128], h[:, k, :nt], k == 0, k == 7, [win, h], [pB])
                    Cc = cs[:, 0, :nt]
                    Ss = cs[:, 1, :nt]
                    op("dve", lambda e: e.tensor_tensor(r[:, 0, :nt], pA[:, :nt], Cc, ALU.mult), [pA, cs], [r])
                    op("dve", lambda e: e.tensor_tensor(r[:, 1, :nt], pB[:, :nt], Ss, ALU.mult), [pB, cs], [r])
                    op("dve", lambda e: e.tensor_tensor(qk[:, j, :nt], r[:, 0, :nt], r[:, 1, :nt], ALU.subtract), [r], [qk])
                    op("dve", lambda e: e.tensor_tensor(r[:, 2, :nt], pA[:, :nt], Ss, ALU.mult), [pA, cs], [r])
                    op("dve", lambda e: e.tensor_tensor(r[:, 3, :nt], pB[:, :nt], Cc, ALU.mult), [pB, cs], [r])
                    op("dve", lambda e: e.tensor_tensor(qk[:, 2 + j, :nt], r[:, 2, :nt], r[:, 3, :nt], ALU.add), [r], [qk])
                P.dma("sp", dst[:, :, t0:t0 + nt], qk[:, :, :nt], reads=[qk], writes=[QTs if wi == 0 else KTs])
            vt = vtb[blk % 2]
            nsub = nt // 128
            for sub in range(nsub):
                pv_ = nb()
                for k in range(8):
                    mm(pv_[:, :], h[:, k, sub * 128:(sub + 1) * 128], win[:, k, 2048:2560], k == 0, k == 7, [win, h], [pv_])
                op("act", lambda e: e.copy(vt[:, sub, :], pv_[:, :]), [pv_], [vt])
            P.dma("sp", Vv[:, blk * 4:blk * 4 + nsub, :], vt[:, 0:nsub, :], reads=[vt], writes=[Vs])
        P.barrier()
        P.pop()

        P.push()
        Vall = P.sbuf("Vall", [128, 66, 512], BF16)
        for i in range(6):
            P.dma("sp", Vall[:, i * 11:(i + 1) * 11, :], Vv[:, i * 11:(i + 1) * 11, :], reads=[Vs], writes=[Vall])
        KTh = [P.sbuf("KTh", [128, T], BF16) for _ in range(2)]
        QTb = [P.sbuf("QTb", [128, 512], BF16) for _ in range(2)]
        PT = [P.sbuf("PT", [128, 3, 512], BF16) for _ in range(2)]
        pvs0 = P.sbuf("pvs0", [128, 512], F32)
        zacc = P.sbuf("zacc", [128, 512], F32)
        ztmp = P.sbuf("ztmp", [128, 512], F32)
        rz = [P.sbuf("rz", [128, 512], F32) for _ in range(2)]
        oo = [P.sbuf("oo", [128, 512], F32) for _ in range(2)]
        ofin = P.sbuf("ofin", [128, 512], F32)
        osq = P.sbuf("osq", [128, 512], BF16)
        ort = P.sbuf("ort", [128, 512], F32)
        orstd = P.sbuf("orstd", [128, 512], F32)
        aob = [P.sbuf("aob", [128, 512], BF16) for _ in range(2)]

        def rows_for(h_, m):
            g = 2 * h_ + m
            j = g // 4
            gg = g % 4
            return j * 128 + 32 * gg, 256 + j * 128 + 32 * gg

        def loadK(h_):
            kt_ = KTh[h_ % 2]
            for m in range(2):
                ra, rb = rows_for(h_, m)
                P.dma("sp", kt_[64 * m:64 * m + 32, :], KTs[ra:ra + 32, :], reads=[KTs], writes=[kt_])
                P.dma("sp", kt_[64 * m + 32:64 * m + 64, :], KTs[rb:rb + 32, :], reads=[KTs], writes=[kt_])

        def loadQ(idx):
            h_, blk = divmod(idx, NBLK)
            t0, nt, s = blk_range(blk)
            q_ = QTb[idx % 2]
            for m in range(2):
                ra, rb = rows_for(h_, m)
                P.dma("sp", q_[64 * m:64 * m + 32, :nt], QTs[ra:ra + 32, t0:t0 + nt], reads=[QTs], writes=[q_])
                P.dma("sp", q_[64 * m + 32:64 * m + 64, :nt], QTs[rb:rb + 32, t0:t0 + nt], reads=[QTs], writes=[q_])

        groups = []
        for idx in range(4 * NBLK):
            h_, blk = divmod(idx, NBLK)
            t0, nt, s = blk_range(blk)
            ktiles = list(range(66)) if s == 0 else [64, 65]
            for m in range(2):
                gl = [ktiles[i:i + 3] for i in range(0, len(ktiles), 3)]
                for gi_, kts in enumerate(gl):
                    groups.append((idx, h_, blk, nt, m, kts, gi_ == 0, gi_ == len(gl) - 1))
        loadK(0)
        loadQ(0)
        ppv = ps[6]
        pz = ps[7]

        def emit_S(g):
            idx, h_, blk, nt, m, kts, first, last = groups[g]
            if m == 0 and first:
                if blk == 0 and h_ + 1 < 4:
                    loadK(h_ + 1)
                if idx + 1 < 4 * NBLK:
                    loadQ(idx + 1)
            kt_ = KTh[h_ % 2]
            q_ = QTb[idx % 2]
            sc = sc3[g % 2]
            for j, kt in enumerate(kts):
                mm(sc[:, j * 512:j * 512 + nt], kt_[64 * m:64 * m + 64, kt * 128:(kt + 1) * 128], q_[64 * m:64 * m + 64, :nt], True, True, [kt_, q_], [sc])
            pt = PT[g % 2]
            n = len(kts)
            op("act", lambda e: e.activation(pt[:, 0:n, :nt], sc[:, :].rearrange("p (j c) -> p j c", c=512)[:, 0:n, :nt], AF.Exp, scale=0.125), [sc], [pt])

        def emit_PV(g):
            idx, h_, blk, nt, m, kts, first, last = groups[g]
            pt = PT[g % 2]
            for j, kt in enumerate(kts):
                f_ = first and j == 0
                l_ = last and j == len(kts) - 1
                mm(ppv[:, :nt], Vall[:, kt, h_ * 128:(h_ + 1) * 128], pt[:, j, :nt], f_, l_, [Vall, pt], [ppv])
                mm(pz[:, :nt], ones_bf[:], pt[:, j, :nt], f_, l_, [ones_bf, pt], [pz])
            if not last:
                return
            t0 = blk_range(blk)[0]
            op("dve", lambda e: e.reciprocal(rz[m][:, :nt], pz[:, :nt]), [pz], [rz[m]])
            if m == 0:
                op("act", lambda e: e.copy(pvs0[:, :nt], ppv[:, :nt]), [ppv], [pvs0])
                return
            op("dve", lambda e: e.tensor_tensor(oo[1][:, :nt], ppv[:, :nt], rz[1][:, :nt], ALU.mult), [ppv, rz[1]], [oo[1]])
            op("dve", lambda e: e.tensor_tensor(oo[0][:, :nt], pvs0[:, :nt], rz[0][:, :nt], ALU.mult), [pvs0, rz[0]], [oo[0]])
            op("dve", lambda e: e.scalar_tensor_tensor(ofin[:, :nt], oo[1][:, :nt], neglam[:, 0:1], oo[0][:, :nt], ALU.mult, ALU.add),
               [oo[0], oo[1], neglam], [ofin])
            op("dve", lambda e: e.tensor_tensor(osq[:, :nt], ofin[:, :nt], ofin[:, :nt], ALU.mult), [ofin], [osq])
            pq_ = sc3[g % 2]
            mm(pq_[:, :nt], ones_bf[:], osq[:, :nt], True, True, [ones_bf, osq], [pq_])
            op("act", lambda e: e.activation(ort[:, :nt], pq_[:, :nt], AF.Sqrt, bias=epst[:], scale=1.0 / 128.0), [pq_, epst], [ort])
            op("dve", lambda e: e.reciprocal(orstd[:, :nt], ort[:, :nt]), [ort], [orstd])
            ao = aob[idx % 2]
            op("dve", lambda e: e.scalar_tensor_tensor(ao[:, :nt], ofin[:, :nt], gsub[:, 0:1], orstd[:, :nt], ALU.mult, ALU.mult),
               [ofin, gsub, orstd], [ao])
            P.dma("sp", AOs[h_ * 128:(h_ + 1) * 128, t0:t0 + nt], ao[:, :nt], reads=[ao], writes=[AOs])

        for g in range(len(groups) + 1):
            if g < len(groups):
                emit_S(g)
            if g >= 1:
                emit_PV(g - 1)
        P.barrier()
        P.pop()

        P.push()
        wout = P.sbuf("wout", [128, 8, 1024], BF16)
        for k in range(8):
            load_cast(wout, lambda c0, n, k=k: wout[:, k, c0:c0 + n], evwo[k * 128:(k + 1) * 128, :], 1024)
        Ub = [P.sbuf("Ub", [128, 4, 542], BF16) for _ in range(2)]
        dg = P.sbuf("dg", [128, 124, 128], BF16)
        AOb = [P.sbuf("AOb", [128, 4, 512], BF16) for _ in range(2)]
        xsb = [P.sbuf("xs", [128, 8, 512], F32) for _ in range(2)]
        acc = P.sbuf("acc", [128, 4, 512], F32)
        vb = P.sbuf("vb", [128, 4, 512], BF16)
        v2 = P.sbuf("v2", [128, 4, 512], BF16)
        msb = P.sbuf("msb", [128, 512], F32)
        nm2 = P.sbuf("nm2", [128, 512], F32)
        var = P.sbuf("var", [128, 512], F32)
        crt = P.sbuf("crt", [128, 512], F32)
        crs = P.sbuf("crs", [128, 512], F32)
        ctmp = P.sbuf("ctmp", [128, 4, 512], F32)
        cout = P.sbuf("cout", [128, 4, 512], BF16)
        x1b = [P.sbuf("x1b", [128, 8, 512], F32) for _ in range(2)]
        AOv = AOs.rearrange("(c p) t -> p c t", p=128)
        cw = pvs("ev_conv_w")
        for idx_ in range(124):
            op("dve", lambda e: e.tensor_scalar(dg[:, idx_, :], identb[:], cw[:, idx_:idx_ + 1], None, ALU.mult), [identb, pvt], [dg])

        def loadD1(blk):
            t0, nt, s = blk_range(blk)
            c0 = ucol(t0)
            P.dma("sp", Ub[blk % 2][:, :, :nt + 30], Uv[:, :, c0 - 15:c0 + nt + 15], reads=[Us], writes=[Ub[blk % 2]])
            P.dma("sp", AOb[blk % 2][:, :, :nt], AOv[:, :, t0:t0 + nt], reads=[AOs], writes=[AOb[blk % 2]])
            P.dma("sp", xsb[blk % 2][:, :, :nt], xTv[:, :, t0:t0 + nt], writes=[xsb[blk % 2]])

        loadD1(0)
        for blk in range(NBLK):
            t0, nt, s = blk_range(blk)
            if blk + 1 < NBLK:
                loadD1(blk + 1)
            U_ = Ub[blk % 2]
            ao_ = AOb[blk % 2]
            xs = xsb[blk % 2]
            x1 = x1b[blk % 2]
            for c in range(4):
                pacc = ps[2 + c]
                for j in range(31):
                    mm(pacc[:, :nt], dg[:, j * 4 + c, :], U_[:, c, j:j + nt], j == 0, j == 30, [dg, U_], [pacc])
                op("act", lambda e: e.activation(acc[:, c, :nt], pacc[:, :nt], AF.Identity, bias=pvs("ev_conv_b", c, 1)), [pacc, pvt], [acc])
            op("act", lambda e: e.copy(vb[:, :, :nt], acc[:, :, :nt]), [acc], [vb])
            op("act", lambda e: e.activation(v2[:, :, :nt], acc[:, :, :nt], AF.Square), [acc], [v2])
            for c in range(4):
                mm(ps[0][:, :nt], onesm_bf[:], vb[:, c, :nt], c == 0, c == 3, [onesm_bf, vb], [ps[0]])
            for c in range(4):
                mm(ps[1][:, :nt], onesm_bf[:], v2[:, c, :nt], c == 0, c == 3, [onesm_bf, v2], [ps[1]])
            op("act", lambda e: e.copy(msb[:, :nt], ps[0][:, :nt]), [ps[0]], [msb])
            op("dve", lambda e: e.scalar_tensor_tensor(nm2[:, :nt], msb[:, :nt], -1.0, msb[:, :nt], ALU.mult, ALU.mult), [msb], [nm2])
            op("dve", lambda e: e.tensor_tensor(var[:, :nt], ps[1][:, :nt], nm2[:, :nt], ALU.add), [ps[1], nm2], [var])
            op("act", lambda e: e.activation(crt[:, :nt], var[:, :nt], AF.Sqrt, bias=epst[:]), [var, epst], [crt])
            op("dve", lambda e: e.reciprocal(crs[:, :nt], crt[:, :nt]), [crt], [crs])
            for c in range(4):
                op("dve", lambda e: e.tensor_tensor(ctmp[:, c, :nt], acc[:, c, :nt], msb[:, :nt], ALU.subtract), [acc, msb], [ctmp])
                op("dve", lambda e: e.tensor_tensor(ctmp[:, c, :nt], ctmp[:, c, :nt], crs[:, :nt], ALU.mult), [ctmp, crs], [ctmp])
                op("dve", lambda e: e.tensor_scalar(ctmp[:, c, :nt], ctmp[:, c, :nt], pvs("ev_ln_g", c, 1), pvs("ev_ln_b", c, 1), ALU.mult, ALU.add),
                   [ctmp, pvt], [ctmp])
                op("act", lambda e: e.activation(cout[:, c, :nt], ctmp[:, c, :nt], AF.Silu), [ctmp], [cout])
            for o in range(8):
                po = ps[2 + (o % 6)]
                for k in range(8):
                    rhs = cout[:, k, :nt] if k < 4 else ao_[:, k - 4, :nt]
                    mm(po[:, :nt], wout[:, k, o * 128:(o + 1) * 128], rhs, k == 0, k == 7, [wout, cout, ao_], [po])
                op("dve", lambda e: e.scalar_tensor_tensor(x1[:, o, :nt], po[:, :nt], mod[s][:, 16 + o:17 + o], xs[:, o, :nt], ALU.mult, ALU.add),
                   [po, mod[s], xs], [x1])
            P.dma("sp", XAv[:, :, t0:t0 + nt], x1[:, :, :nt], reads=[x1], writes=[XAs])
        P.barrier()
        P.pop()

    def mlp_phase(li, src_v, src_buf, dst_v, dst_buf, nblk, final=False):
        P.push()
        w1 = P.sbuf("w1", [128, 8, 4096], BF16)
        w2 = P.sbuf("w2", [128, 32, 1024], BF16)
        for k in range(8):
            load_cast(w1, lambda c0, n, k=k: w1[:, k, c0:c0 + n], w1d[li, k * 128:(k + 1) * 128, :], 4096)
        for k in range(32):
            load_cast(w2, lambda c0, n, k=k: w2[:, k, c0:c0 + n], w2d[li, k * 128:(k + 1) * 128, :], 1024)
        xsb = [P.sbuf("xs", [128, 8, 512], F32) for _ in range(1)]
        rt = P.sbuf("rt", [128, 512], F32)
        rstd = P.sbuf("rstd", [128, 512], F32)
        h2 = P.sbuf("h2", [128, 8, 512], BF16)
        hd = P.sbuf("hid", [128, 32, 512], BF16)
        sq = hd
        rl = [P.sbuf("rl", [128, 512], BF16) for _ in range(2)]
        for blk in range(nblk):
            t0, nt, s = blk_range(blk)
            xs = xsb[0]
            P.dma("sp", xs[:, :, :nt], src_v[:, :, t0:t0 + nt], reads=[src_buf], writes=[xs])
            op("act", lambda e: e.activation(sq[:, 0:8, :nt], xs[:, :, :nt], AF.Square), [xs], [sq])
            for k in range(8):
                mm(ps[0][:, :nt], ones_bf[:], sq[:, k, :nt], k == 0, k == 7, [ones_bf, sq], [ps[0]])
            op("act", lambda e: e.activation(rt[:, :nt], ps[0][:, :nt], AF.Sqrt, bias=epst[:], scale=1.0 / D), [ps[0], epst], [rt])
            op("dve", lambda e: e.reciprocal(rstd[:, :nt], rt[:, :nt]), [rt], [rstd])
            for k in range(8):
                tk = ps[1 + (k % 2)]
                op("dve", lambda e: e.scalar_tensor_tensor(tk[:, :nt], xs[:, k, :nt], A2[s][:, k:k + 1], rstd[:, :nt], ALU.mult, ALU.mult),
                   [xs, rstd, A2[s]], [tk])
                op("act", lambda e: e.activation(h2[:, k, :nt], tk[:, :nt], AF.Identity, bias=mod[s][:, 24 + k:25 + k]), [tk, mod[s]], [h2])
            for j in range(32):
                ph = ps[3 + (j % 3)]
                for k in range(8):
                    mm(ph[:, :nt], w1[:, k, j * 128:(j + 1) * 128], h2[:, k, :nt], k == 0, k == 7, [w1, h2], [ph])
                r_ = rl[j % 2]
                op("act", lambda e: e.activation(r_[:, :nt], ph[:, :nt], AF.Relu), [ph], [r_])
                if j % 4 == 3:
                    op("dve", lambda e: e.tensor_tensor(hd[:, j, :nt], r_[:, :nt], r_[:, :nt], ALU.mult), [r_], [hd])
                else:
                    op("dve", lambda e: e.tensor_tensor(hd[:, j, :nt], r_[:, :nt], r_[:, :nt], ALU.mult), [r_], [hd])
            for o in range(8):
                po = ps[6 + (o % 2)]
                for j in range(32):
                    mm(po[:, :nt], w2[:, j, o * 128:(o + 1) * 128], hd[:, j, :nt], j == 0, j == 31, [w2, hd], [po])
                op("dve", lambda e: e.scalar_tensor_tensor(xs[:, o, :nt], po[:, :nt], mod[s][:, 40 + o:41 + o], xs[:, o, :nt],
                                                           ALU.mult, ALU.add), [po, mod[s], xs], [xs])
            if final:
                op("act", lambda e: e.activation(sq[:, 0:8, :nt], xs[:, :, :nt], AF.Square), [xs], [sq])
                for k in range(8):
                    mm(ps[0][:, :nt], ones_bf[:], sq[:, k, :nt], k == 0, k == 7, [ones_bf, sq], [ps[0]])
                op("act", lambda e: e.activation(rt[:, :nt], ps[0][:, :nt], AF.Sqrt, bias=epst[:], scale=1.0 / D), [ps[0], epst], [rt])
                op("dve", lambda e: e.reciprocal(rstd[:, :nt], rt[:, :nt]), [rt], [rstd])
                fg = pvs("fing")
                for k in range(8):
                    op("dve", lambda e: e.scalar_tensor_tensor(xs[:, k, :nt], xs[:, k, :nt], fg[:, k:k + 1], rstd[:, :nt], ALU.mult, ALU.mult),
                       [xs, rstd, pvt], [xs])
            P.dma("sp", dst_v[:, :, t0:t0 + nt], xs[:, :, :nt], reads=[xs], writes=[dst_buf])
        P.barrier()
        P.pop()

    PQW = 8456

    def pcol(t0):
        return 2 + t0 if t0 < L else 8198 + (t0 - L)

    def layer1(parts=l1parts):
        prep_mods(1)
        odw = P.dram("odw", [D, 3600], F32, "ExternalInput")
        pbd = P.dram("pb", [128, NPB], F32, "ExternalInput")
        dnmd = P.dram("dnm", [128, 6, 4, 128], F32, "ExternalInput")
        PFs = scratch("PFs", [3584, PQW], BF16)
        BGTs = scratch("BGTs", [T, 16], F32)
        UHs = scratch("UHs", [L, 1536], BF16)
        GSs = scratch("GSs", [L, 512], BF16)
        QNTs = scratch("QNTs", [512, T], F32)
        KNTs = scratch("KNTs", [512, T], F32)
        KTMs = scratch("KTMs", [T, 512], F32)
        VTMs = scratch("VTMs", [T, 512], F32)
        ODs = [scratch("ODf", [L, 512], F32), scratch("ODb", [L, 512], F32)]
        DNs = scratch("DNs", [L, 512], F32)
        HYs = scratch("HYs", [L, 512], BF16)
        PFv = PFs.rearrange("(c p) t -> p c t", p=128)
        pbt = P.sbuf("pbt", [128, NPB], F32)
        P.dma("sp", pbt[:], pbd[:], writes=[pbt])
        dnkd = P.dram("dnk", [128, 7, 4, 128], F32, "ExternalInput")
        ident = identf
        onec = P.sbuf("onec", [128, 1], F32)
        op("dve", lambda e: e.memset(onec[:], 1.0), writes=[onec])

        def pbs(name, j=0, n=None):
            o, c = PB_OFF[name]
            n = c - j if n is None else n
            return pbt[:, o + j:o + j + n]

        nega = P.sbuf("nega", [128, 8], F32)
        op("act", lambda e: e.activation(nega[:], pbs("alog"), AF.Exp), [pbt], [nega])
        op("dve", lambda e: e.tensor_scalar(nega[:], nega[:], -1.0, None, ALU.mult), [nega], [nega])
        pbank = [0]

        def nb():
            pbank[0] = (pbank[0] + 1) % 8
            return ps[pbank[0]]

        if "A" in parts:
            P.push()
            z4 = P.sbuf("z4", [128, 28, 4], BF16)
            op("dve", lambda e: e.memset(z4[:], 0.0), writes=[z4])
            P.dma("sp", PFv[:, :, 0:2], z4[:, :, 0:2], reads=[z4], writes=[PFs])
            P.dma("sp", PFv[:, :, 8194:8198], z4[:, :, 0:4], reads=[z4], writes=[PFs])
            P.dma("sp", PFv[:, :, 8454:8456], z4[:, :, 0:2], reads=[z4], writes=[PFs])
            win1 = P.sbuf("win1", [128, 8, 3600], BF16)
            for k in range(8):
                load_cast(win1, lambda c0, n, k=k: win1[:, k, c0:c0 + n], odw[k * 128:(k + 1) * 128, :], 3600)
            xsb = [P.sbuf("xs", [128, 8, 512], F32) for _ in range(2)]
            sq = P.sbuf("sq", [128, 8, 512], BF16)
            rt = P.sbuf("rt", [128, 512], F32)
            rstd = P.sbuf("rstd", [128, 512], F32)
            tmp = P.sbuf("tmp", [128, 8, 512], F32)
            h = P.sbuf("h", [128, 8, 512], BF16)
            pfb = [P.sbuf("pfb", [128, 4, 512], BF16) for _ in range(3)]
            bgt = [P.sbuf("bgt", [128, 4, 16], F32) for _ in range(2)]
            t8 = P.sbuf("t8", [128, 8], F32)
            BGTv = BGTs.rearrange("(n p) f -> p n f", p=128)
            P.dma("sp", xsb[0][:, :, :512], X1v[:, :, 0:512], reads=[X1s], writes=[xsb[0]])
            for blk in range(NBLK):
                t0, nt, s = blk_range(blk)
                if blk + 1 < NBLK:
                    t0n, ntn, _ = blk_range(blk + 1)
                    P.dma("sp", xsb[(blk + 1) % 2][:, :, :ntn], X1v[:, :, t0n:t0n + ntn], reads=[X1s], writes=[xsb[(blk + 1) % 2]])
                xs = xsb[blk % 2]
                norm_mod(xs, nt, A1[s], mod[s], 0, sq, rt, rstd, tmp, h, ps[0])
                pc0 = pcol(t0)
                for c4 in range(7):
                    pf = pfb[c4 % 3]
                    for cc in range(4):
                        c = c4 * 4 + cc
                        pc = nb()
                        for k in range(8):
                            mm(pc[:, :nt], win1[:, k, c * 128:(c + 1) * 128], h[:, k, :nt], k == 0, k == 7, [win1, h], [pc])
                        if cc % 2 == 0:
                            op("act", lambda e: e.copy(pf[:, cc, :nt], pc[:, :nt]), [pc], [pf])
                        else:
                            op("dve", lambda e: e.tensor_copy(pf[:, cc, :nt], pc[:, :nt]), [pc], [pf])
                    P.dma("sp", PFv[:, c4 * 4:(c4 + 1) * 4, pc0:pc0 + nt], pf[:, :, :nt], reads=[pf], writes=[PFs])
                bg = bgt[blk % 2]
                nsub = nt // 128
                for sub in range(nsub):
                    pb_ = nb()
                    for k in range(8):
                        mm(pb_[:, 0:16], h[:, k, sub * 128:(sub + 1) * 128], win1[:, k, 3584:3600], k == 0, k == 7, [win1, h], [pb_])
                    op("act", lambda e: e.activation(bg[:, sub, 0:8], pb_[:, 0:8], AF.Sigmoid), [pb_], [bg])
                    op("dve", lambda e: e.tensor_tensor(t8[:], pb_[:, 8:16], pbs("dtb"), ALU.add), [pb_, pbt], [t8])
                    op("act", lambda e: e.activation(t8[:], t8[:], AF.Exp), [t8], [t8])
                    op("act", lambda e: e.activation(t8[:], t8[:], AF.Ln, bias=onec[:]), [t8, onec], [t8])
                    op("dve", lambda e: e.tensor_tensor(bg[:, sub, 8:16], t8[:], nega[:], ALU.mult), [t8, nega], [bg])
                P.dma("sp", BGTv[:, blk * 4:blk * 4 + nsub, :], bg[:, 0:nsub, :], reads=[bg], writes=[BGTs])
            P.barrier()
            P.pop()

        if "B" in parts:
            P.push()
            pq = [P.sbuf("pq", [128, 12, 516], BF16) for _ in range(2)]
            dgh = P.sbuf("dgh", [128, 36, 128], BF16)
            dgd = P.sbuf("dgd", [128, 60, 128], BF16)
            accH = P.sbuf("accH", [128, 12, 512], F32)
            accG = P.sbuf("accG", [128, 4, 512], F32)
            acc = P.sbuf("acc", [128, 12, 512], F32)
            sqb = P.sbuf("sqb", [128, 512], BF16)
            rtb = P.sbuf("rtb", [128, 512], F32)
            rnb = P.sbuf("rnb", [128, 512], F32)
            ut = [P.sbuf("ut", [128, 4, 1536], BF16) for _ in range(2)]
            gtb = [P.sbuf("gtb", [128, 4, 512], BF16) for _ in range(2)]
            ktm = [P.sbuf("ktm", [128, 4, 512], F32) for _ in range(2)]
            vtm = [P.sbuf("vtm", [128, 4, 512], F32) for _ in range(2)]
            UHv = UHs.rearrange("(n p) f -> p n f", p=128)
            GSv = GSs.rearrange("(n p) f -> p n f", p=128)
            KTMv = KTMs.rearrange("(n p) f -> p n f", p=128)
            VTMv = VTMs.rearrange("(n p) f -> p n f", p=128)
            QNTv = QNTs.rearrange("(c p) t -> p c t", p=128)
            KNTv = KNTs.rearrange("(c p) t -> p c t", p=128)
            hw = pvs("hy_sw")
            dw = pvs("dn_cw")
            for idx_ in range(36):
                op("dve", lambda e: e.tensor_scalar(dgh[:, idx_, :], identb[:], hw[:, idx_:idx_ + 1], None, ALU.mult), [identb, pvt], [dgh])
            for idx_ in range(60):
                op("dve", lambda e: e.tensor_scalar(dgd[:, idx_, :], identb[:], dw[:, idx_:idx_ + 1], None, ALU.mult), [identb, pvt], [dgd])
            li = 0
            for blk in range(NBLK):
                t0, nt, s = blk_range(blk)
                pc0 = pcol(t0)
                nsub = nt // 128
                if s == 0:
                    pq_ = pq[li % 2]; li += 1
                    P.dma("sp", pq_[:, :, :nt + 4], PFv[:, 0:12, pc0 - 2:pc0 + nt + 2], reads=[PFs], writes=[pq_])
                    for c in range(12):
                        pc_ = nb()
                        for j in range(3):
                            mm(pc_[:, :nt], dgh[:, j * 12 + c, :], pq_[:, c, 1 + j:1 + j + nt], j == 0, j == 2, [dgh, pq_], [pc_])
                        op("act", lambda e: e.activation(accH[:, c, :nt], pc_[:, :nt], AF.Identity, bias=pvs("hy_sb", c, 1)), [pc_, pvt], [accH])
                    u_ = ut[blk % 2]
                    for sub in range(nsub):
                        for g3 in range(3):
                            pt = nb()
                            for j in range(4):
                                op("pe", lambda e: e.transpose(pt[:, j * 128:(j + 1) * 128], accH[:, g3 * 4 + j, sub * 128:(sub + 1) * 128], ident[:]),
                                   [accH, ident], [pt])
                            op("act", lambda e: e.copy(u_[:, sub, g3 * 512:(g3 + 1) * 512], pt[:, :]), [pt], [u_])
                    P.dma("sp", UHv[:, blk * 4:blk * 4 + nsub, :], u_[:, 0:nsub, :], reads=[u_], writes=[UHs])
                    pq_ = pq[li % 2]; li += 1
                    P.dma("sp", pq_[:, 0:4, :nt], PFv[:, 12:16, pc0:pc0 + nt], reads=[PFs], writes=[pq_])
                    op("act", lambda e: e.activation(accG[:, 0:4, :nt], pq_[:, 0:4, :nt], AF.Silu), [pq_], [accG])
                    g_ = gtb[blk % 2]
                    for sub in range(nsub):
                        pt = nb()
                        for j in range(4):
                            op("pe", lambda e: e.transpose(pt[:, j * 128:(j + 1) * 128], accG[:, j, sub * 128:(sub + 1) * 128], ident[:]), [accG, ident], [pt])
                        op("act", lambda e: e.copy(g_[:, sub, :], pt[:, :]), [pt], [g_])
                    P.dma("sp", GSv[:, blk * 4:blk * 4 + nsub, :], g_[:, 0:nsub, :], reads=[g_], writes=[GSs])
                pq_ = pq[li % 2]; li += 1
                P.dma("sp", pq_[:, :, :nt + 4], PFv[:, 16:28, pc0 - 2:pc0 + nt + 2], reads=[PFs], writes=[pq_])
                for c in range(12):
                    pc_ = nb()
                    for j in range(5):
                        mm(pc_[:, :nt], dgd[:, j * 12 + c, :], pq_[:, c, j:j + nt], j == 0, j == 4, [dgd, pq_], [pc_])
                    op("act", lambda e: e.activation(acc[:, c, :nt], pc_[:, :nt], AF.Silu), [pc_], [acc])
                for c in range(8):
                    op("act", lambda e: e.activation(sqb[:, :nt], acc[:, c, :nt], AF.Square), [acc], [sqb])
                    pss = nb()
                    mm(pss[:, :nt], ones_bf[:], sqb[:, :nt], True, True, [ones_bf, sqb], [pss])
                    op("act", lambda e: e.activation(rtb[:, :nt], pss[:, :nt], AF.Sqrt, bias=epst[:]), [pss, epst], [rtb])
                    op("dve", lambda e: e.reciprocal(rnb[:, :nt], rtb[:, :nt]), [rtb], [rnb])
                    op("dve", lambda e: e.tensor_tensor(acc[:, c, :nt], acc[:, c, :nt], rnb[:, :nt], ALU.mult), [acc, rnb], [acc])
                P.dma("sp", QNTv[:, :, t0:t0 + nt], acc[:, 0:4, :nt], reads=[acc], writes=[QNTs])
                P.dma("sp", KNTv[:, :, t0:t0 + nt], acc[:, 4:8, :nt], reads=[acc], writes=[KNTs])
                k_ = ktm[blk % 2]
                v_ = vtm[blk % 2]
                for sub in range(nsub):
                    for (dst, c0) in ((k_, 4), (v_, 8)):
                        pt = nb()
                        for j in range(4):
                            op("pe", lambda e: e.transpose(pt[:, j * 128:(j + 1) * 128], acc[:, c0 + j, sub * 128:(sub + 1) * 128], ident[:]), [acc, ident], [pt])
                        op("act", lambda e: e.copy(dst[:, sub, :], pt[:, :]), [pt], [dst])
                P.dma("sp", KTMv[:, blk * 4:blk * 4 + nsub, :], k_[:, 0:nsub, :], reads=[k_], writes=[KTMs])
                P.dma("sp", VTMv[:, blk * 4:blk * 4 + nsub, :], v_[:, 0:nsub, :], reads=[v_], writes=[VTMs])
            P.barrier()
            P.pop()

        if "DN" in parts:
            P.push()
            dnm = P.sbuf("dnm", [128, 6, 4, 128], F32)
            P.dma("sp", dnm[:], dnmd[:], writes=[dnm])
            dnk = P.sbuf("dnk", [128, 7, 4, 128], F32)
            P.dma("sp", dnk[:], dnkd[:], writes=[dnk])
            ones_f = dnm[:, 5, 2, :]
            KNTh = KNTs.rearrange("(h p) t -> p h t", p=128)
            QNTh = QNTs.rearrange("(h p) t -> p h t", p=128)
            Sst = [P.sbuf("Sst", [128, 4, 128], F32) for _ in range(2)]
            for d in range(2):
                op("dve", lambda e: e.memset(Sst[d][:], 0.0), writes=[Sst[d]])
            bnames = ["kT", "ktm", "vtm", "rhsg", "E1", "E2", "DmM", "DTm", "Pa", "Qa", "Dm", "Ck", "CkT", "Ee", "Ee2", "X", "vb", "kbg"]
            rnames = ["qT", "qdec", "AT", "u", "wT", "kdec", "vnew", "o"]
            bufs = [{n: P.sbuf("dn_" + n, [128, 4, 128], F32) for n in bnames} for _ in range(2)]
            rbufs = [[{n: P.sbuf("dr_" + n, [128, 4, 128], F32) for n in rnames} for _ in range(2)] for _ in range(2)]
            bgs = [[P.sbuf("dn_bg", [128, 16], F32) for _ in range(2)] for _ in range(2)]
            sms = [[P.sbuf("dn_sm", [128, 32], F32) for _ in range(2)] for _ in range(2)]
            isq = 1.0 / math.sqrt(128.0)

            def f512(b):
                return b[:].rearrange("p h j -> p (h j)")

            def dn_prep(tile, d, par, banks):
                free = list(banks)
                B_ = bufs[d]
                R_ = rbufs[d][par]
                bg = bgs[d][par]
                sm = sms[d][par]
                is_ctx = tile >= 64
                t0 = tile * 128
                kT, ktm_, vtm_ = B_["kT"], B_["ktm"], B_["vtm"]
                qT = R_["qT"]
                P.dma("sp", kT[:], KNTh[:, :, t0:t0 + 128], reads=[KNTs], writes=[kT])
                if not is_ctx:
                    P.dma("sp", qT[:], QNTh[:, :, t0:t0 + 128], reads=[QNTs], writes=[qT])
                P.dma("sp", f512(ktm_), KTMs[t0:t0 + 128, :], reads=[KTMs], writes=[ktm_])
                P.dma("sp", f512(vtm_), VTMs[t0:t0 + 128, :], reads=[VTMs], writes=[vtm_])
                P.dma("sp", bg[:], BGTs[t0:t0 + 128, :], reads=[BGTs], writes=[bg])
                yield
                bcol = lambda h_: bg[:, 4 * d + h_:4 * d + h_ + 1]
                gcols = bg[:, 8 + 4 * d:12 + 4 * d]
                TRI = dnm[:, 5, d, :]
                MS4 = dnm[:, 0 + d, :, :]
                MTI4 = dnm[:, 2 + d, :, :]
                I4 = dnm[:, 4, :, :]
                pg = free.pop()
                mm(pg[:, 0:4], TRI, gcols, True, True, [dnm, bg], [pg])
                mm(pg[:, 4:8], ones_f, gcols, True, True, [dnm, bg], [pg])
                rhsg = B_["rhsg"]
                for h_ in range(4):
                    op("dve", lambda e: e.tensor_scalar(rhsg[:, h_, :], TRI, bg[:, 8 + 4 * d + h_:9 + 4 * d + h_], None, ALU.mult), [dnm, bg], [rhsg])
                pr = free.pop()
                mm(pr[:, :], ones_f, f512(rhsg), True, True, [dnm, rhsg], [pr])
                yield
                op("act", lambda e: e.copy(sm[:, 0:4], pg[:, 0:4]), [pg], [sm])
                op("act", lambda e: e.copy(sm[:, 28:32], pg[:, 4:8]), [pg], [sm])
                free.append(pg)
                op("dve", lambda e: e.tensor_scalar(sm[:, 4:8], sm[:, 0:4], -1.0, None, ALU.mult), [sm], [sm])
                op("act", lambda e: e.activation(sm[:, 8:12], sm[:, 0:4], AF.Exp), [sm], [sm])
                yield
                op("dve", lambda e: e.tensor_tensor(sm[:, 16:20], sm[:, 28:32], sm[:, 0:4], ALU.subtract), [sm], [sm])
                op("act", lambda e: e.activation(sm[:, 16:20], sm[:, 16:20], AF.Exp), [sm], [sm])
                op("act", lambda e: e.activation(sm[:, 20:24], sm[:, 28:32], AF.Exp), [sm], [sm])
                op("dve", lambda e: e.tensor_tensor(sm[:, 24:28], bg[:, 4 * d:4 * d + 4], sm[:, 8:12], ALU.mult), [sm, bg], [sm])
                yield
                E1, E2, DmM, DTm = B_["E1"], B_["E2"], B_["DmM"], B_["DTm"]
                qdec = R_["qdec"]
                for h_ in range(4):
                    op("act", lambda e: e.activation(E1[:, h_, :], pr[:, h_ * 128:(h_ + 1) * 128], AF.Exp, bias=sm[:, h_:h_ + 1], scale=-1.0), [pr, sm], [E1])
                yield
                if not is_ctx:
                    for h_ in range(4):
                        op("act", lambda e: e.activation(E2[:, h_, :], pr[:, h_ * 128:(h_ + 1) * 128], AF.Exp, bias=sm[:, 4 + h_:5 + h_], scale=1.0), [pr, sm], [E2])
                    op("act", lambda e: e.activation(f512(qdec), pr[:, :], AF.Exp), [pr], [qdec])
                free.append(pr)
                yield
                op("dve", lambda e: e.scalar_tensor_tensor(f512(DmM), f512(E1), 1.0, MS4.rearrange("p h j -> p (h j)"), ALU.min, ALU.mult), [E1, dnm], [DmM])
                if not is_ctx:
                    op("dve", lambda e: e.scalar_tensor_tensor(f512(DTm), f512(E2), 1.0, MTI4.rearrange("p h j -> p (h j)"), ALU.min, ALU.mult), [E2, dnm], [DTm])
                    op("dve", lambda e: e.scalar_tensor_tensor(f512(qdec), f512(qT), isq, f512(qdec), ALU.mult, ALU.mult), [qT, qdec], [qdec])
                pk = free.pop()
                for h_ in range(4):
                    mm(pk[:, h_ * 128:(h_ + 1) * 128], kT[:, h_, :], kT[:, h_, :], True, True, [kT], [pk])
                AT = R_["AT"]
                if not is_ctx:
                    pa = free.pop()
                    for h_ in range(4):
                        mm(pa[:, h_ * 128:(h_ + 1) * 128], kT[:, h_, :], qT[:, h_, :], True, True, [kT, qT], [pa])
                yield
                Pa, Qa, X = B_["Pa"], B_["Qa"], B_["X"]
                for h_ in range(4):
                    op("dve", lambda e: e.scalar_tensor_tensor(Pa[:, h_, :], pk[:, h_ * 128:(h_ + 1) * 128], bcol(h_), DmM[:, h_, :], ALU.mult, ALU.mult),
                       [pk, bg, DmM], [Pa])
                free.append(pk)
                if not is_ctx:
                    op("dve", lambda e: e.tensor_tensor(f512(AT), pa[:, :], f512(DTm), ALU.mult), [pa, DTm], [AT])
                    free.append(pa)
                yield
                pq0 = free.pop()
                for h_ in range(4):
                    op("pe", lambda e: e.transpose(pq0[:, h_ * 128:(h_ + 1) * 128], Pa[:, h_, :], ident[:]), [Pa, ident], [pq0])
                I4f = I4.rearrange("p h j -> p (h j)")
                Mk = lambda k_: dnk[:, k_, :, :].rearrange("p h j -> p (h j)")
                Dm, Ck, CkT, Ee, Ee2 = B_["Dm"], B_["Ck"], B_["CkT"], B_["Ee"], B_["Ee2"]
                op("dve", lambda e: e.tensor_tensor(f512(Ck), f512(Pa), Mk(0), ALU.mult), [Pa, dnk], [Ck])
                yield
                op("act", lambda e: e.copy(f512(Qa), pq0[:, :]), [pq0], [Qa])
                free.append(pq0)
                op("dve", lambda e: e.tensor_tensor(f512(Dm), I4f, f512(Ck), ALU.subtract), [dnm, Ck], [Dm])
                yield
                op("dve", lambda e: e.tensor_tensor(f512(CkT), f512(Qa), Mk(0), ALU.mult), [Qa, dnk], [CkT])
                yield
                op("dve", lambda e: e.tensor_tensor(f512(X), I4f, f512(CkT), ALU.subtract), [dnm, CkT], [X])
                for k_ in range(1, 7):
                    op("dve", lambda e: e.tensor_tensor(f512(Ck), f512(Pa), Mk(k_), ALU.mult), [Pa, dnk], [Ck])
                    yield
                    pE2 = free.pop()
                    for h_ in range(4):
                        mm(pE2[:, h_ * 128:(h_ + 1) * 128], Ck[:, h_, :], X[:, h_, :], True, True, [Ck, X], [pE2])
                    yield
                    op("act", lambda e: e.copy(f512(Ee2), pE2[:, :]), [pE2], [Ee2])
                    free.append(pE2)
                    yield
                    pF2 = free.pop()
                    for h_ in range(4):
                        mm(pF2[:, h_ * 128:(h_ + 1) * 128], Dm[:, h_, :], Ee2[:, h_, :], True, True, [Dm, Ee2], [pF2])
                    yield
                    op("dve", lambda e: e.tensor_tensor(f512(X), f512(X), pF2[:, :], ALU.subtract), [X, pF2], [X])
                    free.append(pF2)
                    yield
                    if k_ < 6:
                        pT = free.pop()
                        for h_ in range(4):
                            op("pe", lambda e: e.transpose(pT[:, h_ * 128:(h_ + 1) * 128], X[:, h_, :], ident[:]), [X, ident], [pT])
                        yield
                        op("act", lambda e: e.copy(f512(Dm), pT[:, :]), [pT], [Dm])
                        free.append(pT)
                        yield
                vb, kbg = B_["vb"], B_["kbg"]
                kdec = R_["kdec"]
                for h_ in range(4):
                    op("dve", lambda e: e.tensor_scalar(vb[:, h_, :], vtm_[:, h_, :], bcol(h_), None, ALU.mult), [vtm_, bg], [vb])
                    op("dve", lambda e: e.tensor_scalar(kbg[:, h_, :], ktm_[:, h_, :], sm[:, 24 + h_:25 + h_], None, ALU.mult), [ktm_, sm], [kbg])
                    op("dve", lambda e: e.tensor_scalar(kdec[:, h_, :], ktm_[:, h_, :], sm[:, 16 + h_:17 + h_], None, ALU.mult), [ktm_, sm], [kdec])
                yield
                u, wT = R_["u"], R_["wT"]
                pu = free.pop()
                for h_ in range(4):
                    mm(pu[:, h_ * 128:(h_ + 1) * 128], X[:, h_, :], vb[:, h_, :], True, True, [X, vb], [pu])
                pw = free.pop()
                for h_ in range(4):
                    mm(pw[:, h_ * 128:(h_ + 1) * 128], kbg[:, h_, :], X[:, h_, :], True, True, [X, kbg], [pw])
                yield
                op("act", lambda e: e.copy(f512(u), pu[:, :]), [pu], [u])
                op("act", lambda e: e.copy(f512(wT), pw[:, :]), [pw], [wT])
                free.append(pu)
                free.append(pw)
                yield

            def dn_recur(tile, d, par, banks):
                free = list(banks)
                R_ = rbufs[d][par]
                sm = sms[d][par]
                S = Sst[d]
                is_ctx = tile >= 64
                t0 = tile * 128
                qdec, AT, u, wT, kdec, vnew, o = (R_[n] for n in ("qdec", "AT", "u", "wT", "kdec", "vnew", "o"))
                p1 = free.pop()
                for h_ in range(4):
                    mm(p1[:, h_ * 128:(h_ + 1) * 128], wT[:, h_, :], S[:, h_, :], True, True, [wT, S], [p1])
                yield
                op("dve", lambda e: e.tensor_tensor(f512(vnew), f512(u), p1[:, :], ALU.subtract), [u, p1], [vnew])
                free.append(p1)
                yield
                if not is_ctx:
                    po = free.pop()
                    for h_ in range(4):
                        mm(po[:, h_ * 128:(h_ + 1) * 128], qdec[:, h_, :], S[:, h_, :], True, False, [qdec, S], [po])
                        mm(po[:, h_ * 128:(h_ + 1) * 128], AT[:, h_, :], vnew[:, h_, :], False, True, [AT, vnew], [po])
                p4 = free.pop()
                for h_ in range(4):
                    mm(p4[:, h_ * 128:(h_ + 1) * 128], kdec[:, h_, :], vnew[:, h_, :], True, True, [kdec, vnew], [p4])
                yield
                for h_ in range(4):
                    op("dve", lambda e: e.scalar_tensor_tensor(S[:, h_, :], S[:, h_, :], sm[:, 20 + h_:21 + h_], p4[:, h_ * 128:(h_ + 1) * 128], ALU.mult, ALU.add),
                       [S, sm, p4], [S])
                free.append(p4)
                if not is_ctx:
                    op("act", lambda e: e.copy(f512(o), po[:, :]), [po], [o])
                    free.append(po)
                    P.dma("sp", ODs[d][t0:t0 + 128, :], f512(o), reads=[o], writes=[ODs[d]])
                yield

            order_f = [64, 65] + list(range(64))
            order_b = [65, 64] + list(range(63, -1, -1))
            for st in range(67):
                gens = []
                if st < 66:
                    gens += [dn_prep(order_f[st], 0, st % 2, (ps[0], ps[1])), dn_prep(order_b[st], 1, st % 2, (ps[2], ps[3]))]
                if st >= 1:
                    gens += [dn_recur(order_f[st - 1], 0, (st - 1) % 2, (ps[4], ps[5])), dn_recur(order_b[st - 1], 1, (st - 1) % 2, (ps[6], ps[7]))]
                while gens:
                    nxt = []
                    for g_ in gens:
                        try:
                            next(g_)
                            nxt.append(g_)
                        except StopIteration:
                            pass
                    gens = nxt
            P.barrier()
            P.pop()

        if "CMB" in parts:
            P.push()
            of = [P.sbuf("of", [128, 4, 128], F32) for _ in range(2)]
            ob = [P.sbuf("ob", [128, 4, 128], F32) for _ in range(2)]
            gs = [P.sbuf("gs", [128, 4, 128], BF16) for _ in range(2)]
            osq = P.sbuf("osq", [128, 4, 128], F32)
            ss4 = P.sbuf("ss4", [128, 4], F32)
            rt4 = P.sbuf("rt4", [128, 4], F32)
            rs4 = P.sbuf("rs4", [128, 4], F32)
            dnb = [P.sbuf("dnb", [128, 4, 128], F32) for _ in range(2)]

            def f512(b):
                return b[:].rearrange("p h j -> p (h j)")
            for tile in range(64):
                t0 = tile * 128
                i = tile % 2
                P.dma("sp", f512(of[i]), ODs[0][t0:t0 + 128, :], reads=[ODs[0]], writes=[of[i]])
                P.dma("sp", f512(ob[i]), ODs[1][t0:t0 + 128, :], reads=[ODs[1]], writes=[ob[i]])
                P.dma("sp", f512(gs[i]), GSs[t0:t0 + 128, :], reads=[GSs], writes=[gs[i]])
                op("dve", lambda e: e.tensor_tensor(f512(of[i]), f512(of[i]), f512(ob[i]), ALU.add), [of[i], ob[i]], [of[i]])
                op("act", lambda e: e.activation(f512(osq), f512(of[i]), AF.Square), [of[i]], [osq])
                op("dve", lambda e: e.tensor_reduce(ss4[:], osq[:], AX.X, ALU.add), [osq], [ss4])
                op("act", lambda e: e.activation(rt4[:], ss4[:], AF.Sqrt, bias=epst[:], scale=1.0 / 128.0), [ss4, epst], [rt4])
                op("dve", lambda e: e.reciprocal(rs4[:], rt4[:]), [rt4], [rs4])
                for h_ in range(4):
                    op("dve", lambda e: e.scalar_tensor_tensor(dnb[i][:, h_, :], of[i][:, h_, :], rs4[:, h_:h_ + 1], pbs("dn_ng"), ALU.mult, ALU.mult),
                       [of[i], rs4, pbt], [dnb[i]])
                op("dve", lambda e: e.tensor_tensor(f512(dnb[i]), f512(dnb[i]), f512(gs[i]), ALU.mult), [dnb[i], gs[i]], [dnb[i]])
                P.dma("sp", DNs[t0:t0 + 128, :], f512(dnb[i]), reads=[dnb[i]], writes=[DNs])
            P.barrier()
            P.pop()
        if "HYF" in parts or "HY" in parts:
            P.push()
            hycd = P.dram("hyc", [128, 1280], F32, "ExternalInput")
            hytd = P.dram("hyt", [128, 4, 2, 128], F32, "ExternalInput")
            GFs = scratch("GFs", [2, 16384, 512], BF16)
            HSs = scratch("HSs", [2, 8, 128, 16384], BF16)
            hyc = P.sbuf("hyc", [128, 1280], BF16)
            load_cast(hyc, lambda c0, n: hyc[:, c0:c0 + n], hycd, 1280)
            hyt = P.sbuf("hyt", [128, 4, 2, 128], F32)
            P.dma("sp", hyt[:], hytd[:], writes=[hyt])
            hytb = P.sbuf("hytb", [128, 4, 2, 128], BF16)
            op("act", lambda e: e.copy(hytb[:], hyt[:]), [hyt], [hytb])
            F1b = hyc[:, 0:256]
            F2re, F2im, F2imn = hyc[:, 256:384], hyc[:, 384:512], hyc[:, 512:640]
            G1a, G1b = hyc[:, 640:896], hyc[:, 896:1152]
            Ere, Eimn = hyc[:, 1152:1216], hyc[:, 1216:1280]
            TT = [P.sbuf("TT", [128, 512], BF16) for _ in range(8)]
            tti = [0]

            def nT():
                tti[0] = (tti[0] + 1) % 8
                return TT[tti[0]]

            PB = [P.sbuf("PB", [128, 2, 512], BF16) for _ in range(3)]
            pbi = [0]

            def cmul_evac(pv, tre, tim, out_re, out_im, n):
                pre, pim, pbuf = pv
                pbuf = list(pbuf) if isinstance(pbuf, (list, tuple)) else [pbuf]
                pbi[0] = (pbi[0] + 1) % 3
                pb = PB[pbi[0]]
                t1, t2, t3, t4 = nT(), nT(), nT(), nT()
                three = len(pre.shape) == 3
                v = lambda t: t[:, :n] if not three else t[:, :n].rearrange("p (a b) -> p a b", a=pre.shape[1])
                vb_ = lambda r: pb[:, r, :n] if not three else pb[:, r, :n].rearrange("p (a b) -> p a b", a=pre.shape[1])
                op("act", lambda e: e.copy(vb_(0), pre), pbuf, [pb])
                op("act", lambda e: e.copy(vb_(1), pim), pbuf, [pb])
                op("dve", lambda e: e.tensor_tensor(v(t1), vb_(0), tre, ALU.mult), [pb] + pbuf[1:], [t1])
                op("dve", lambda e: e.tensor_tensor(v(t2), vb_(1), tim, ALU.mult), [pb] + pbuf[1:], [t2])
                op("dve", lambda e: e.tensor_tensor(out_re[0], v(t1), v(t2), ALU.subtract), [t1, t2], [out_re[1]])
                op("dve", lambda e: e.tensor_tensor(v(t3), vb_(0), tim, ALU.mult), [pb] + pbuf[1:], [t3])
                op("dve", lambda e: e.tensor_tensor(v(t4), vb_(1), tre, ALU.mult), [pb] + pbuf[1:], [t4])
                op("dve", lambda e: e.tensor_tensor(out_im[0], v(t3), v(t4), ALU.add), [t3, t4], [out_im[1]])

            def fft_fwd(X, Kp, Bb, consume):
                for c2 in range(32):
                    pS = nb()
                    for cc in range(2):
                        c = c2 * 2 + cc
                        mm(pS[:, cc * 256:(cc + 1) * 256], X[0:Kp, c, :], F1b[0:Kp, :], True, True, [X, hyc], [pS])
                    pv4 = pS[:, :].rearrange("p (c r k) -> p c r k", c=2, r=2)
                    cmul_evac((pv4[:, :, 0, :], pv4[:, :, 1, :], [pS, hytb]), hytb[:, 0, :, :], hytb[:, 1, :, :],
                              (Bb[:, 0, c2 * 2:c2 * 2 + 2, :], Bb), (Bb[:, 1, c2 * 2:c2 * 2 + 2, :], Bb), 256)
                for j in range(16):
                    bre = Bb[:, 0, 4 * j:4 * j + 4, :].rearrange("p c k -> p (c k)")
                    bim = Bb[:, 1, 4 * j:4 * j + 4, :].rearrange("p c k -> p (c k)")
                    pXre = nb()
                    mm(pXre[:, :], F2re, bre, True, False, [hyc, Bb], [pXre])
                    mm(pXre[:, :], F2imn, bim, False, True, [hyc, Bb], [pXre])
                    pXim = nb()
                    mm(pXim[:, :], F2re, bim, True, False, [hyc, Bb], [pXim])
                    mm(pXim[:, :], F2im, bre, False, True, [hyc, Bb], [pXim])
                    consume(j, pXre, pXim)

            def fft_inv(Yh, Cb, gate, Xout):
                for c2 in range(32):
                    pC = nb()
                    for cc in range(2):
                        c = c2 * 2 + cc
                        mm(pC[:, cc * 256:(cc + 1) * 256], Yh[:, 0, c, :], G1a, True, False, [Yh, hyc], [pC])
                        mm(pC[:, cc * 256:(cc + 1) * 256], Yh[:, 1, c, :], G1b, False, True, [Yh, hyc], [pC])
                    pv4 = pC[:, :].rearrange("p (c r k) -> p c r k", c=2, r=2)
                    cmul_evac((pv4[:, :, 0, :], pv4[:, :, 1, :], [pC, hytb]), hytb[:, 2, :, :], hytb[:, 3, :, :],
                              (Cb[:, 0, c2 * 2:c2 * 2 + 2, :], Cb), (Cb[:, 1, c2 * 2:c2 * 2 + 2, :], Cb), 256)
                for j in range(16):
                    cre = Cb[:, 0, 4 * j:4 * j + 4, :].rearrange("p c k -> p (c k)")
                    cim = Cb[:, 1, 4 * j:4 * j + 4, :].rearrange("p c k -> p (c k)")
                    py = nb()
                    mm(py[0:64, :], Ere, cre, True, False, [hyc, Cb], [py])
                    mm(py[0:64, :], Eimn, cim, False, True, [hyc, Cb], [py])
                    op("dve", lambda e: e.tensor_tensor(Xout[:, 4 * j:4 * j + 4, :].rearrange("p c k -> p (c k)"), py[0:64, :],
                                                        gate[:, 4 * j:4 * j + 4, :].rearrange("p c k -> p (c k)"), ALU.mult), [py, gate], [Xout])

        if "HYF" in parts:
            P.push()
            hw1d = P.dram("hy_w1", [33, 64], F32, "ExternalInput")
            hw2d = P.dram("hy_w2", [64, 64], F32, "ExternalInput")
            hw3d = P.dram("hy_w3", [64, 2048], F32, "ExternalInput")
            hyvd = P.dram("hyv", [64, 4], F32, "ExternalInput")
            z2d = P.dram("Z2", [33, 16384], F32, "ExternalInput")
            dlbd = P.dram("dlb", [128, 512], F32, "ExternalInput")
            tcold = P.dram("tcoln", [128, 128], F32, "ExternalInput")
            skcd = P.dram("skc", [128, 1024], F32, "ExternalInput")
            w1s = P.sbuf("w1s", [33, 64], F32)
            w2s = P.sbuf("w2s", [64, 64], F32)
            w3s = P.sbuf("w3s", [64, 2048], F32)
            w3n = P.sbuf("w3n", [64, 2048], F32)
            hyv = P.sbuf("hyv", [64, 4], F32)
            dlb = P.sbuf("dlb", [128, 512], F32)
            tcoln = P.sbuf("tcoln", [128, 128], F32)
            skc = P.sbuf("skc", [128, 1024], F32)
            for dst, src in ((w1s, hw1d), (w2s, hw2d), (w3s, hw3d), (hyv, hyvd), (dlb, dlbd), (tcoln, tcold), (skc, skcd)):
                P.dma("sp", dst[:], src[:], writes=[dst])
            negpi = P.sbuf("negpi", [128, 1], F32)
            op("dve", lambda e: e.memset(negpi[:], -math.pi), writes=[negpi])
            hid2 = P.sbuf("hid2", [64, 16384], F32)
            zb = [P.sbuf("zb", [33, 512], F32) for _ in range(2)]
            a1 = P.sbuf("a1", [64, 512], F32)
            h1 = P.sbuf("h1", [64, 512], F32)
            TWO_PI = 2.0 * math.pi
            qi = P.sbuf("qi", [64, 512], mybir.dt.int32)
            qf = P.sbuf("qf", [64, 512], F32)
            mw = P.sbuf("mw", [64, 512], F32)

            def range_reduce():
                op("dve", lambda e: e.tensor_scalar(qf[:], a1[:], 1.0 / TWO_PI, None, ALU.mult), [a1], [qf])
                op("dve", lambda e: e.tensor_copy(qi[:], qf[:]), [qf], [qi])
                op("dve", lambda e: e.tensor_copy(qf[:], qi[:]), [qi], [qf])
                op("dve", lambda e: e.scalar_tensor_tensor(a1[:], qf[:], -TWO_PI, a1[:], ALU.mult, ALU.add), [qf, a1], [a1])
                op("dve", lambda e: e.tensor_scalar(mw[:], a1[:], -math.pi, TWO_PI, ALU.is_lt, ALU.mult), [a1], [mw])
                op("dve", lambda e: e.tensor_tensor(a1[:], a1[:], mw[:], ALU.add), [a1, mw], [a1])
                op("dve", lambda e: e.tensor_scalar(mw[:], a1[:], math.pi, -TWO_PI, ALU.is_gt, ALU.mult), [a1], [mw])
                op("dve", lambda e: e.tensor_tensor(a1[:], a1[:], mw[:], ALU.add), [a1, mw], [a1])
            for blk in range(32):
                z_ = zb[blk % 2]
                P.dma("sp", z_[:], z2d[:, blk * 512:(blk + 1) * 512], writes=[z_])
                p1 = nb()
                mm(p1[0:64, :], w1s[:, :], z_[:, :], True, True, [w1s, z_], [p1])
                op("dve", lambda e: e.tensor_scalar(a1[:], p1[0:64, :], hyv[:, 0:1], hyv[:, 1:2], ALU.add, ALU.mult), [p1, hyv], [a1])
                range_reduce()
                op("act", lambda e: e.activation(h1[:], a1[:], AF.Sin), [a1], [h1])
                p2 = nb()
                mm(p2[0:64, :], w2s[:, :], h1[:, :], True, True, [w2s, h1], [p2])
                op("dve", lambda e: e.tensor_scalar(a1[:], p2[0:64, :], hyv[:, 2:3], hyv[:, 3:4], ALU.add, ALU.mult), [p2, hyv], [a1])
                range_reduce()
                op("act", lambda e: e.activation(hid2[:, blk * 512:(blk + 1) * 512], a1[:], AF.Sin), [a1], [hid2])
            wnd = [P.sbuf("wnd", [128, 512], F32) for _ in range(2)]
            fa = [P.sbuf("fa", [128, 512], F32) for _ in range(4)]
            fab = [P.sbuf("fab", [128, 512], BF16) for _ in range(4)]
            rn = [P.sbuf("rn", [128, 512], F32) for _ in range(2)]
            b6 = [0]

            def nb6():
                b6[0] = (b6[0] + 1) % 6
                return ps[b6[0]]

            def window(tile):
                w_ = wnd[tile % 2]
                op("dve", lambda e: e.tensor_scalar(w_[:], dlb[:], tcoln[:, tile:tile + 1], None, ALU.mult), [dlb, tcoln], [w_])
                op("act", lambda e: e.activation(w_[:], w_[:], AF.Exp), [w_], [w_])
                return w_
            for tile in range(128):
                dr = 0 if tile < 64 else 1
                w_ = window(tile)
                for o in range(2):
                    pf = nb6()
                    cb = (o * 2 + dr) * 512
                    mm(pf[:, :], hid2[:, tile * 128:(tile + 1) * 128], w3s[:, cb:cb + 512], True, True, [hid2, w3s], [pf])
                    f_ = fa[o + 2 * (tile % 2)]
                    fb_ = fab[o + 2 * (tile % 2)]
                    op("dve", lambda e: e.tensor_tensor(f_[:], pf[:, :], w_[:], ALU.mult), [pf, w_], [f_])
                    op("act", lambda e: e.activation(fb_[:], f_[:], AF.Abs), [f_], [fb_])
                    mm(ps[6 + o][:, :], ones_bf[:], fb_[:], tile == 0, tile == 127, [ones_bf, fb_], [ps[6 + o]])
            for o in range(2):
                op("dve", lambda e: e.reciprocal(rn[o][:], ps[6 + o][:, :]), [ps[6 + o]], [rn[o]])
                for dr in range(2):
                    cb = (o * 2 + dr) * 512
                    op("dve", lambda e: e.tensor_tensor(w3n[:, cb:cb + 512], w3s[:, cb:cb + 512], rn[o][0:64, :], ALU.mult), [w3s, rn[o]], [w3n])
            for tile in range(128):
                dr = 0 if tile < 64 else 1
                w_ = window(tile)
                for o in range(2):
                    pf = nb6()
                    cb = (o * 2 + dr) * 512
                    mm(pf[:, :], hid2[:, tile * 128:(tile + 1) * 128], w3n[:, cb:cb + 512], True, True, [hid2, w3n], [pf])
                    fb_ = fab[o + 2 * (tile % 2)]
                    op("dve", lambda e: e.tensor_tensor(fb_[:], pf[:, :], w_[:], ALU.mult), [pf, w_], [fb_])
                    P.dma("sp", GFs[o, tile * 128:(tile + 1) * 128, :], fb_[:], reads=[fb_], writes=[GFs])
            P.barrier()
            P.pop()
            P.push()
            skc = P.sbuf("skc", [128, 1024], F32)
            P.dma("sp", skc[:], skcd[:], writes=[skc])
            ldf = [P.sbuf("ldf", [128, 128, 64], BF16) for _ in range(2)]
            Xf = P.sbuf("Xf", [128, 64, 128], BF16)
            Bb = P.sbuf("Bb", [128, 2, 64, 128], BF16)
            Hh = [P.sbuf("Hh", [128, 2, 64, 128], BF16) for _ in range(2)]
            for og in range(16):
                o, g = divmod(og, 8)
                l_ = ldf[og % 2]
                P.dma("sp", l_[:], GFs[o].rearrange("(a b) c -> a b c", b=128)[:, :, g * 64:(g + 1) * 64], reads=[GFs], writes=[l_])
                op("dve", lambda e: e.memset(l_[64:65, 0, :], 0.0), writes=[l_])
                op("act", lambda e: e.copy(Xf[:], l_[:].rearrange("p n c -> p c n")), [l_], [Xf])
                H_ = Hh[og % 2]

                def consume(j, pXre, pXim):
                    for cc in range(4):
                        ch = o * 512 + g * 64 + 4 * j + cc
                        op("act", lambda e: e.activation(H_[:, 0, 4 * j + cc, :], pXre[:, cc * 128:(cc + 1) * 128], AF.Identity, bias=skc[:, ch:ch + 1]),
                           [pXre, skc], [H_])
                    op("act", lambda e: e.copy(H_[:, 1, 4 * j:4 * j + 4, :].rearrange("p c k -> p (c k)"), pXim[:, :]), [pXim], [H_])
                fft_fwd(Xf, 128, Bb, consume)
                P.dma("sp", HSs[o, g], H_[:].rearrange("p r c k -> p (r c k)"), reads=[H_], writes=[HSs])
            P.barrier()
            P.pop()

        if "HY" in parts:
            P.push()
            ld = [P.sbuf("ld", [64, 128, 64], BF16)] * 2
            Xz = P.sbuf("Xz", [64, 64, 128], BF16)
            g1 = P.sbuf("g1", [64, 64, 128], BF16)
            g2 = P.sbuf("g2", [64, 64, 128], BF16)
            Bb = P.sbuf("Bb", [128, 2, 64, 128], BF16)
            Yh = P.sbuf("Yh", [128, 2, 64, 128], BF16)
            Hh = [P.sbuf("Hh", [128, 2, 64, 128], BF16)] * 2
            UH3 = UHs.rearrange("(a b) f -> a b f", b=128)
            HY3 = HYs.rearrange("(a b) f -> a b f", b=128)
            lc = 0
            for g in range(8):
                for part, dst in ((0, Xz), (1, g1), (2, g2)):
                    l_ = ld[lc % 2]; lc += 1
                    P.dma("sp", l_[:], UH3[:, :, part * 512 + g * 64:part * 512 + (g + 1) * 64], reads=[UHs], writes=[l_])
                    op("act", lambda e: e.copy(dst[:], l_[:].rearrange("p n c -> p c n")), [l_], [dst])
                for o in range(2):
                    H_ = Hh[o]
                    P.dma("sp", H_[:].rearrange("p r c k -> p (r c k)"), HSs[o, g], reads=[HSs], writes=[H_])

                    def consume(j, pXre, pXim):
                        sl = lambda t, r: t[:, r, 4 * j:4 * j + 4, :].rearrange("p c k -> p (c k)")
                        cmul_evac((pXre[:, :], pXim[:, :], [pXre, pXim, H_]), sl(H_, 0), sl(H_, 1), (sl(Yh, 0), Yh), (sl(Yh, 1), Yh), 512)
                    fft_fwd(Xz, 64, Bb, consume)
                    fft_inv(Yh, Bb, g1 if o == 0 else g2, Xz)
                l_ = ld[lc % 2]; lc += 1
                op("act", lambda e: e.copy(l_[:].rearrange("p n c -> p c n"), Xz[:]), [Xz], [l_])
                P.dma("sp", HY3[:, :, g * 64:(g + 1) * 64], l_[:], reads=[l_], writes=[HYs])
            P.barrier()
            P.pop()
        if "HYF" in parts or "HY" in parts:
            P.barrier()
            P.pop()
        if "OUT" in parts:
            P.push()
            odwod = P.dram("odwo", [D, D], F32, "ExternalInput")
            wout = P.sbuf("wout", [128, 8, 1024], BF16)
            for k in range(8):
                load_cast(wout, lambda c0, n, k=k: wout[:, k, c0:c0 + n], odwod[k * 128:(k + 1) * 128, :], 1024)
            hyt_ = [P.sbuf("hyt_", [128, 4, 512], F32) for _ in range(2)]
            hyb_ = [P.sbuf("hyb_", [128, 4, 512], BF16) for _ in range(2)]
            dnt_ = [P.sbuf("dnt_", [128, 4, 512], F32) for _ in range(2)]
            xsb = [P.sbuf("xs", [128, 8, 512], F32) for _ in range(2)]
            mixT = P.sbuf("mixT", [128, 8, 512], BF16)
            x1b = [P.sbuf("x1b", [128, 8, 512], F32) for _ in range(2)]
            HYv = HYs.rearrange("(n p) f -> p n f", p=128)
            DNv = DNs.rearrange("(n p) f -> p n f", p=128)

            def loadO(blk):
                t0, nt, s = blk_range(blk)
                P.dma("sp", hyb_[blk % 2][:], HYv[:, blk * 4:blk * 4 + 4, :], reads=[HYs], writes=[hyb_[blk % 2]])
                op("dve", lambda e: e.tensor_copy(hyt_[blk % 2][:], hyb_[blk % 2][:]), [hyb_[blk % 2]], [hyt_[blk % 2]])
                P.dma("sp", dnt_[blk % 2][:], DNv[:, blk * 4:blk * 4 + 4, :], reads=[DNs], writes=[dnt_[blk % 2]])
                P.dma("sp", xsb[blk % 2][:], X1v[:, :, t0:t0 + nt], reads=[X1s], writes=[xsb[blk % 2]])
            loadO(0)
            for blk in range(16):
                t0, nt, s = blk_range(blk)
                if blk + 1 < 16:
                    loadO(blk + 1)
                xs = xsb[blk % 2]
                x1 = x1b[blk % 2]
                for sub in range(4):
                    for src, base in ((hyt_[blk % 2], 0), (dnt_[blk % 2], 4)):
                        pt = nb()
                        for j in range(4):
                            op("pe", lambda e: e.transpose(pt[:, j * 128:(j + 1) * 128], src[:, sub, j * 128:(j + 1) * 128], ident[:]), [src, ident], [pt])
                        op("act", lambda e: e.copy(mixT[:, base:base + 4, sub * 128:(sub + 1) * 128], pt[:, :].rearrange("p (c t) -> p c t", c=4)), [pt], [mixT])
                for o in range(8):
                    po = nb()
                    for k in range(8):
                        mm(po[:, :nt], wout[:, k, o * 128:(o + 1) * 128], mixT[:, k, :nt], k == 0, k == 7, [wout, mixT], [po])
                    op("dve", lambda e: e.scalar_tensor_tensor(x1[:, o, :nt], po[:, :nt], mod[0][:, 16 + o:17 + o], xs[:, o, :nt], ALU.mult, ALU.add),
                       [po, mod[0], xs], [x1])
                P.dma("sp", XAv[:, :, t0:t0 + nt], x1[:, :, :nt], reads=[x1], writes=[XAs])
            P.barrier()
            P.pop()
            mlp_phase(1, XAv, XAs, Yv, Ys, 16, final=True)

    if 0 in layers:
        layer0()
        mlp_phase(0, XAv, XAs, X1v, X1s, NBLK)
    if 1 in layers:
        layer1()

    fin = [(k, v) for k, v in P.last_w.items() if k in ("X1s",) or k in debug]
    return P.finish([v for _, v in fin]), P


_CACHE = {}


def _get(key, **kw):
    if key not in _CACHE:
        _CACHE[key] = build(**kw)
    return _CACHE[key]


def _run(ncP, maps):
    nc, P = ncP
    in_maps = [{k: m[k] for k in P.ext_in} for m in maps]
    res = run_bass_kernel_spmd(nc, in_maps, core_ids=list(range(len(maps))))
    return res.results


def kernel(**inputs):
    inp = {k: np.asarray(v) for k, v in inputs.items()}
    nb_ = inp["x"].shape[0]
    consts = make_consts(inp)
    maps = [pack_core(inp, b, consts) for b in range(nb_)]
    r2 = _run(_get("fused", layers=(0, 1)), maps)
    out = np.stack([np.asarray(r["yT"], np.float32).T for r in r2], axis=0)
    return np.ascontiguousarray(out, np.float32)
```

```python
import math
import os
import numpy as np
import concourse.bass as bass
import concourse.mybir as mybir
from concourse.bass_utils import run_bass_kernel_spmd

F32 = mybir.dt.float32
BF16 = mybir.dt.bfloat16
AF = mybir.ActivationFunctionType
ALU = mybir.AluOpType
AX = mybir.AxisListType

ENGS = ("pe", "act", "dve", "pool", "sp")
N_DMA_SEMS = 48

D = 1024
L = 8192
LC = 256
T = L + LC
EPS = 1e-6


class Buf:
    def __init__(self, name, t):
        self.name = name
        self.t = t

    def __getitem__(self, idx):
        return self.t[idx]

    def rearrange(self, *a, **k):
        return self.t.rearrange(*a, **k)


class Prog:
    ENGOBJ = {"pe": "tensor", "act": "scalar", "dve": "vector", "pool": "gpsimd", "sp": "sync"}

    def __init__(self):
        self.nc = bass.Bass("TRN2", target_bir_lowering=False)
        self.cnt = {e: 0 for e in ENGS}
        self.seen = {e: {} for e in ENGS}
        self.last_w = {}
        self.readers = {}
        self.dma_rr = 0
        self.dma_val = [0] * N_DMA_SEMS
        self.scopes = [[]]
        self.n_ops = 0
        self.sems = {}
        self.uid = 0
        self.ext_in = []
        for n in list(ENGS) + ["dma%d" % i for i in range(N_DMA_SEMS)]:
            self.sems[n] = self._enter(self.nc.semaphore("s_" + n))

    def _enter(self, cm):
        v = cm.__enter__()
        self.scopes[-1].append(cm)
        return v

    def push(self):
        self.scopes.append([])

    def pop(self):
        for cm in reversed(self.scopes.pop()):
            cm.__exit__(None, None, None)

    def sbuf(self, name, shape, dt):
        self.uid += 1
        name = "%s_%d" % (name, self.uid)
        return Buf(name, self._enter(self.nc.sbuf_tensor(name, list(shape), dt)))

    def psum(self, name, shape, dt=F32):
        return Buf(name, self._enter(self.nc.psum_tensor(name, list(shape), dt)))

    def dram(self, name, shape, dt, kind="Internal"):
        if kind == "ExternalInput":
            self.ext_in.append(name)
        return Buf(name, self.nc.dram_tensor(name, list(shape), dt, kind=kind).ap())

    @staticmethod
    def _k(k):
        return k.name if isinstance(k, Buf) else k

    def _deps(self, reads, writes):
        deps = []
        for k in reads:
            k = self._k(k)
            if k in self.last_w:
                deps.append(self.last_w[k])
        for k in writes:
            k = self._k(k)
            if k in self.last_w:
                deps.append(self.last_w[k])
            deps.extend(self.readers.get(k, ()))
        return deps

    def _record(self, ev, reads, writes):
        for k in reads:
            k = self._k(k)
            self.readers.setdefault(k, []).append(ev)
        for k in writes:
            k = self._k(k)
            self.last_w[k] = ev
            self.readers[k] = []

    def _emit_waits(self, eng, deps):
        need = {}
        for (sname, val) in deps:
            if sname == "pe" and eng == "pe":
                continue
            if val > self.seen[eng].get(sname, 0) and val > need.get(sname, 0):
                need[sname] = val
        for sname, val in need.items():
            self.seen[eng][sname] = val
            getattr(self.nc, self.ENGOBJ[eng]).wait_ge(self.sems[sname], val)

    def op(self, eng, fn, reads=(), writes=()):
        deps = self._deps(reads, writes)
        self._emit_waits(eng, deps)
        self.cnt[eng] += 1
        ev = (eng, self.cnt[eng])
        fn(getattr(self.nc, self.ENGOBJ[eng])).then_inc(self.sems[eng], 1)
        self._record(ev, reads, writes)
        self.n_ops += 1
        return ev

    def dma(self, q, out, in_, reads=(), writes=(), **kw):
        deps = self._deps(reads, writes)
        k = self.dma_rr
        self.dma_rr = (self.dma_rr + 1) % N_DMA_SEMS
        sname = "dma%d" % k
        if self.dma_val[k] > 0:
            deps = list(deps) + [(sname, self.dma_val[k])]
        self._emit_waits(q, deps)
        self.dma_val[k] += 16
        ev = (sname, self.dma_val[k])
        getattr(self.nc, self.ENGOBJ[q]).dma_start(out=out, in_=in_, **kw).then_inc(self.sems[sname], 16)
        self._record(ev, reads, writes)
        self.n_ops += 1
        return ev

    def barrier(self):
        evs = [(e, self.cnt[e]) for e in ENGS if self.cnt[e] > 0]
        evs += [("dma%d" % i, v) for i, v in enumerate(self.dma_val) if v > 0]
        for e in ENGS:
            self._emit_waits(e, evs)

    def finish(self, final_events):
        self._emit_waits("sp", final_events)
        while self.scopes:
            self.pop()
        return self.nc


def fm(v):
    v = np.asarray(v, np.float32).reshape(-1, 128)
    return np.ascontiguousarray(v.T)


PV_SPEC = [
    ("c", 8), ("cctx", 8), ("ada_b0", 48), ("ada_b1", 48),
    ("n1g0", 8), ("n1g1", 8), ("n2g0", 8), ("n2g1", 8), ("fing", 8),
    ("ev_conv_w", 124), ("ev_conv_b", 4), ("ev_ln_g", 4), ("ev_ln_b", 4), ("subln_g", 1),
    ("lqk", 256),
    ("hy_sw", 36), ("hy_sb", 12), ("dn_cw", 60),
]
PV_OFF = {}
_o = 0
for _n, _c in PV_SPEC:
    PV_OFF[_n] = (_o, _c)
    _o += _c
NV = _o
PB_SPEC = [("dtb", 8), ("alog", 8), ("dn_ng", 128)]
PB_OFF = {}
_o = 0
for _n, _c in PB_SPEC:
    PB_OFF[_n] = (_o, _c)
    _o += _c
NPB = _o


def q_perm():
    A = [[], []]
    Bt = [[], []]
    for g in range(8):
        j = g // 4
        base = g * 64
        A[j] += list(range(base, base + 16)) + list(range(base + 32, base + 48))
        Bt[j] += list(range(base + 16, base + 32)) + list(range(base + 48, base + 64))
    return np.array(A[0] + A[1] + Bt[0] + Bt[1])


def rope_tables():
    n_freq = 16
    inv = (10000.0 ** (-np.arange(n_freq, dtype=np.float32) / np.float32(n_freq))).astype(np.float32)
    t = np.arange(L)
    row = (t // 64).astype(np.float32)
    col = (t % 64).astype(np.float32)
    ang = np.zeros((32, L), np.float32)
    ang[:16] = (row[None, :] * inv[:, None]).astype(np.float32)
    ang[16:] = (col[None, :] * inv[:, None]).astype(np.float32)
    C = np.ones((128, T), np.float32)
    S = np.zeros((128, T), np.float32)
    C[:, :L] = np.tile(np.cos(ang), (4, 1))
    S[:, :L] = np.tile(np.sin(ang), (4, 1))
    return C, S


def pack_core(inp, b, consts):
    m = {}
    m["xT"] = np.ascontiguousarray(np.concatenate([inp["x"][b].T, inp["ctx"][b].T], axis=1), np.float32)
    pv = np.zeros((128, NV), np.float32)

    def put(name, arr):
        o, c = PV_OFF[name]
        assert arr.shape == (128, c), (name, arr.shape, c)
        pv[:, o:o + c] = arr
    put("c", fm(inp["c"][b]))
    put("cctx", fm(inp["c_ctx"]))
    put("ada_b0", fm(inp["ada_b"][0]))
    put("ada_b1", fm(inp["ada_b"][1]))
    put("n1g0", fm(inp["norm1_g"][0])); put("n1g1", fm(inp["norm1_g"][1]))
    put("n2g0", fm(inp["norm2_g"][0])); put("n2g1", fm(inp["norm2_g"][1]))
    put("fing", fm(inp["final_g"]))
    put("ev_conv_w", np.ascontiguousarray(inp["ev_conv_w"][0].reshape(31, 4, 128).transpose(2, 0, 1).reshape(128, 124)))
    put("ev_conv_b", fm(inp["ev_conv_b"][0])); put("ev_ln_g", fm(inp["ev_ln_g"][0])); put("ev_ln_b", fm(inp["ev_ln_b"][0]))
    put("subln_g", fm(inp["ev_subln_g"][0]))
    lqk = np.concatenate([inp["ev_lq1"][0], inp["ev_lk1"][0], inp["ev_lq2"][0], inp["ev_lk2"][0]])
    put("lqk", np.ascontiguousarray(np.broadcast_to(lqk[None, :], (128, 256))))
    put("hy_sw", np.ascontiguousarray(inp["od_hy_short_w"][0].reshape(3, 12, 128).transpose(2, 0, 1).reshape(128, 36)))
    put("hy_sb", fm(inp["od_hy_short_b"][0]))
    put("dn_cw", np.ascontiguousarray(inp["od_dn_conv_w"][0].reshape(5, 12, 128).transpose(2, 0, 1).reshape(128, 60)))
    m["pv"] = pv
    m.update(consts)
    return m


def hyena_consts(inp):
    c = {}
    NF = 16384
    i128 = np.arange(128, dtype=np.float64)
    th = 2.0 * np.pi * np.outer(i128, i128) / 128.0
    ph = 2.0 * np.pi * np.outer(i128, i128) / NF
    hyc = np.zeros((128, 1280), np.float64)
    hyc[:, 0:128] = np.cos(th); hyc[:, 128:256] = -np.sin(th)
    hyc[:, 256:384] = np.cos(th); hyc[:, 384:512] = -np.sin(th); hyc[:, 512:640] = np.sin(th)
    hyc[:, 640:768] = np.cos(th); hyc[:, 768:896] = np.sin(th)
    hyc[:, 896:1024] = -np.sin(th); hyc[:, 1024:1152] = np.cos(th)
    hyc[:, 1152:1216] = np.cos(th)[:, 0:64] / NF; hyc[:, 1216:1280] = -np.sin(th)[:, 0:64] / NF
    c["hyc"] = hyc.astype(np.float32)
    hyt = np.zeros((128, 4, 2, 128), np.float64)
    hyt[:, 0] = np.cos(ph)[:, None, :]; hyt[:, 1] = (-np.sin(ph))[:, None, :]
    hyt[:, 2] = np.cos(ph)[:, None, :]; hyt[:, 3] = np.sin(ph)[:, None, :]
    c["hyt"] = hyt.astype(np.float32)
    t = np.linspace(0.0, 1.0, L, dtype=np.float32)
    bands = 16
    omega = (np.float32(2.0 * math.pi) * np.arange(L, dtype=np.float32) / np.float32(L)).astype(np.float32)
    ang = (omega[:, None] * np.linspace(1e-4, bands - 1, bands, dtype=np.float32)[None, :]).astype(np.float32)
    z = np.concatenate([t[:, None], np.cos(ang), -np.sin(ang)], axis=-1).astype(np.float32)
    n = np.arange(NF)
    pos = np.where(n < L, n, np.where(n == L, 0, NF - n))
    c["Z2"] = np.ascontiguousarray(z[pos].T)
    max_decay = math.log(1e-2) / 0.3
    min_decay = math.log(1e-2) / 1.5
    deltas = np.abs(np.linspace(min_decay, max_decay, 512, dtype=np.float32)).astype(np.float32)
    c["dlb"] = np.ascontiguousarray(np.broadcast_to(deltas[None, :], (128, 512))).astype(np.float32)
    c["tcoln"] = np.ascontiguousarray((-t[pos]).reshape(128, 128).T).astype(np.float32)
    c["hy_w1"] = np.ascontiguousarray(inp["od_hy_w1"][0]); c["hy_w2"] = np.ascontiguousarray(inp["od_hy_w2"][0])
    c["hy_w3"] = np.ascontiguousarray(inp["od_hy_w3"][0])
    c["hyv"] = np.ascontiguousarray(np.stack([inp["od_hy_b1"][0], inp["od_hy_freq1"][0], inp["od_hy_b2"][0], inp["od_hy_freq2"][0]], axis=1), np.float32)
    c["skc"] = np.ascontiguousarray(np.broadcast_to(inp["od_hy_skip"][0].reshape(1, 1024), (128, 1024)), np.float32)
    return c


def make_consts(inp):
    c = {}
    w = inp["ev_w_in"][0]
    qp = q_perm()
    cols = np.concatenate([np.arange(1024), 1024 + qp, 1536 + qp, np.arange(2048, 2560)])
    c["evw"] = np.ascontiguousarray(w[:, cols])
    c["evwo"] = np.ascontiguousarray(inp["ev_w_out"][0])
    c["ada_w"] = np.ascontiguousarray(inp["ada_w"])
    c["w1"] = np.ascontiguousarray(inp["mlp_w1"])
    c["w2"] = np.ascontiguousarray(inp["mlp_w2"])
    c["odw"] = np.ascontiguousarray(inp["od_w_in"][0])
    pb = np.zeros((128, NPB), np.float32)
    rows = {"dtb": np.concatenate([inp["od_dn_dtb_f"][0], inp["od_dn_dtb_b"][0]]),
            "alog": np.concatenate([inp["od_dn_alog_f"][0], inp["od_dn_alog_b"][0]]),
            "dn_ng": inp["od_dn_norm_g"][0]}
    for k_, v_ in rows.items():
        o_, c_ = PB_OFF[k_]
        pb[:, o_:o_ + c_] = np.broadcast_to(np.asarray(v_, np.float32)[None, :], (128, c_))
    c["pb"] = pb
    dnm = np.zeros((128, 6, 4, 128), np.float32)
    ii = np.arange(128)
    Ls = (ii[:, None] > ii[None, :]).astype(np.float32)
    Us_ = (ii[:, None] < ii[None, :]).astype(np.float32)
    Ui = (ii[:, None] <= ii[None, :]).astype(np.float32)
    Li = (ii[:, None] >= ii[None, :]).astype(np.float32)
    isq = np.float32(1.0 / math.sqrt(128.0))
    dnm[:, 0] = Ls[:, None, :]; dnm[:, 1] = Us_[:, None, :]
    dnm[:, 2] = (Ui * isq)[:, None, :]; dnm[:, 3] = (Li * isq)[:, None, :]
    dnm[:, 4] = np.eye(128, dtype=np.float32)[:, None, :]
    dnm[:, 5, 0] = Ui; dnm[:, 5, 1] = Li; dnm[:, 5, 2] = 1.0
    c["dnm"] = dnm
    dnk = np.zeros((128, 7, 4, 128), np.float32)
    for k_ in range(7):
        b_ = 2 ** k_
        mk = ((ii[:, None] // (2 * b_)) == (ii[None, :] // (2 * b_))) & ((ii[:, None] // b_) != (ii[None, :] // b_))
        dnk[:, k_] = mk.astype(np.float32)[:, None, :]
    c["dnk"] = dnk
    c.update(hyena_consts(inp))
    c["odwo"] = np.ascontiguousarray(inp["od_w_out"][0])
    c["identf"] = np.eye(128, dtype=np.float32)
    C, S = rope_tables()
    c["ropeC"] = C
    c["ropeS"] = S
    return c


NBLK = 17


def blk_range(blk):
    t0 = blk * 512
    nt = 512 if blk < 16 else 256
    s = 0 if blk < 16 else 1
    return t0, nt, s


UW = 8508


def ucol(t0):
    return 15 + t0 if t0 < L else 8237 + (t0 - L)


def build(debug=(), layers=(0, 1), l1parts=("A", "B", "DN", "CMB", "HYF", "HY", "OUT"), ext_in=()):
    P = Prog()
    nc = P.nc
    op = P.op

    def mm(out, lhsT, rhs, start, stop, reads, writes):
        op("pe", lambda e: e.matmul(out, lhsT, rhs, start=start, stop=stop), reads, writes)

    xT = P.dram("xT", [D, T], F32, "ExternalInput")
    pvd = P.dram("pv", [128, NV], F32, "ExternalInput")
    evw = P.dram("evw", [D, 2560], F32, "ExternalInput")
    evwo = P.dram("evwo", [D, D], F32, "ExternalInput")
    adaw = P.dram("ada_w", [2, D, 6 * D], F32, "ExternalInput")
    w1d = P.dram("w1", [2, D, 4 * D], F32, "ExternalInput")
    w2d = P.dram("w2", [2, 4 * D, D], F32, "ExternalInput")
    ropeC = P.dram("ropeC", [128, T], F32, "ExternalInput")
    ropeS = P.dram("ropeS", [128, T], F32, "ExternalInput")

    def scratch(name, shape, dt):
        kind = "ExternalInput" if name in ext_in else ("ExternalOutput" if name in debug else "Internal")
        return P.dram(name, shape, dt, kind)

    Us = scratch("Us", [512, UW], BF16)
    QTs = scratch("QTs", [512, T], BF16)
    KTs = scratch("KTs", [512, T], BF16)
    Vs = scratch("Vs", [T, 512], BF16)
    AOs = scratch("AOs", [512, T], BF16)
    XAs = scratch("XAs", [D, T], F32)
    X1s = P.dram("X1s", [D, T], F32, "ExternalInput" if 0 not in layers else ("ExternalOutput" if ("X1s" in debug or 1 not in layers) else "Internal"))

    Ys = P.dram("yT", [D, L], F32, "ExternalOutput" if 1 in layers else "Internal")
    Yv = Ys.rearrange("(k p) t -> p k t", p=128)
    ps_all = P.psum("ps_all", [128, 4096])
    ps = [Buf("ps%d" % i, ps_all[:, i * 512:(i + 1) * 512]) for i in range(8)]
    sc3 = [Buf("sc3_%d" % i, ps_all[:, i * 1536:(i + 1) * 1536]) for i in range(2)]
    pvt = P.sbuf("pvt", [128, NV], F32)
    ones_bf = P.sbuf("ones_bf", [128, 128], BF16)
    onesm_bf = P.sbuf("onesm_bf", [128, 128], BF16)
    epst = P.sbuf("epst", [128, 1], F32)
    zt = P.sbuf("zt", [128, 4, 30], BF16)
    identd = P.dram("identf", [128, 128], F32, "ExternalInput")
    identf = P.sbuf("identf", [128, 128], F32)
    identb = P.sbuf("identb", [128, 128], BF16)
    ones_f32 = P.sbuf("ones_f32", [128, 128], F32)
    op("dve", lambda e: e.memset(ones_f32[:], 1.0), writes=[ones_f32])
    P.dma("sp", identf[:], identd[:], writes=[identf])
    op("act", lambda e: e.copy(identb[:], identf[:]), [identf], [identb])
    P.dma("sp", pvt[:], pvd[:], writes=[pvt])
    op("dve", lambda e: e.memset(ones_bf[:], 1.0), writes=[ones_bf])
    op("dve", lambda e: e.memset(onesm_bf[:], 1.0 / 512.0), writes=[onesm_bf])
    op("dve", lambda e: e.memset(epst[:], EPS), writes=[epst])
    op("dve", lambda e: e.memset(zt[:], 0.0), writes=[zt])

    stg = [P.sbuf("stg", [128, 1024], F32) for _ in range(2)]
    stg_i = [0]

    def load_cast(dst_buf, dst_fn, src_ap, ncols, rows=128):
        c0 = 0
        while c0 < ncols:
            n = min(1024, ncols - c0)
            st = stg[stg_i[0] % 2]
            stg_i[0] += 1
            P.dma("sp", st[0:rows, 0:n], src_ap[:, c0:c0 + n], writes=[st])
            d_ = dst_fn(c0, n)
            if stg_i[0] % 2 == 0:
                op("dve", lambda e: e.tensor_copy(d_, st[0:rows, 0:n]), [st], [dst_buf])
            else:
                op("act", lambda e: e.copy(d_, st[0:rows, 0:n]), [st], [dst_buf])
            c0 += n

    def pvs(name, j=0, n=None):
        o, c = PV_OFF[name]
        n = c - j if n is None else n
        return pvt[:, o + j:o + j + n]

    mod = [P.sbuf("modL", [128, 48], F32), P.sbuf("modC", [128, 48], F32)]
    A1 = [P.sbuf("A1L", [128, 8], F32), P.sbuf("A1C", [128, 8], F32)]
    A2 = [P.sbuf("A2L", [128, 8], F32), P.sbuf("A2C", [128, 8], F32)]
    scs = P.sbuf("scs", [128, 8, 2], F32)
    op("act", lambda e: e.activation(scs[:, :, 0], pvs("c"), AF.Silu), reads=[pvt], writes=[scs])
    op("act", lambda e: e.activation(scs[:, :, 1], pvs("cctx"), AF.Silu), reads=[pvt], writes=[scs])

    def prep_mods(li):
        P.push()
        aw = [P.sbuf("aw", [128, 8, 1024], F32) for _ in range(2)]
        pm = ps[0]
        for m6 in range(6):
            a = aw[m6 % 2]
            P.dma("sp", a[:], adaw[li].rearrange("(k p) n -> p k n", p=128)[:, :, m6 * 1024:(m6 + 1) * 1024], writes=[a])
            for fc in range(8):
                j = m6 * 8 + fc
                for k in range(8):
                    mm(pm[:, 2 * j:2 * j + 2], a[:, k, fc * 128:(fc + 1) * 128], scs[:, k, :], k == 0, k == 7, [a, scs], [pm])
        pm3 = pm[:, 0:96].rearrange("p (j s) -> p j s", s=2)
        for s in range(2):
            op("dve", lambda e: e.tensor_tensor(mod[s][:], pm3[:, :, s], pvs("ada_b%d" % li), ALU.add), [pm, pvt], [mod[s]])
            op("dve", lambda e: e.scalar_tensor_tensor(A1[s][:], mod[s][:, 8:16], 1.0, pvs("n1g%d" % li), ALU.add, ALU.mult),
               [mod[s], pvt], [A1[s]])
            op("dve", lambda e: e.scalar_tensor_tensor(A2[s][:], mod[s][:, 32:40], 1.0, pvs("n2g%d" % li), ALU.add, ALU.mult),
               [mod[s], pvt], [A2[s]])
        P.barrier()
        P.pop()

    def norm_mod(xs, nt, Avec, shbuf, sho, sq, rt, rstd, tmp, h, pss):
        op("act", lambda e: e.activation(sq[:, :, :nt], xs[:, :, :nt], AF.Square), [xs], [sq])
        for k in range(8):
            mm(pss[:, :nt], ones_bf[:], sq[:, k, :nt], k == 0, k == 7, [ones_bf, sq], [pss])
        op("act", lambda e: e.activation(rt[:, :nt], pss[:, :nt], AF.Sqrt, bias=epst[:], scale=1.0 / D), [pss, epst], [rt])
        op("dve", lambda e: e.reciprocal(rstd[:, :nt], rt[:, :nt]), [rt], [rstd])
        for k in range(8):
            op("dve", lambda e: e.scalar_tensor_tensor(tmp[:, k, :nt], xs[:, k, :nt], Avec[:, k:k + 1], rstd[:, :nt], ALU.mult, ALU.mult),
               [xs, rstd, Avec], [tmp])
        for k in range(8):
            op("act", lambda e: e.activation(h[:, k, :nt], tmp[:, k, :nt], AF.Identity, bias=shbuf[:, sho + k:sho + k + 1]), [tmp, shbuf], [h])

    xTv = xT.rearrange("(k p) t -> p k t", p=128)
    XAv = XAs.rearrange("(k p) t -> p k t", p=128)
    X1v = X1s.rearrange("(k p) t -> p k t", p=128)

    def layer0():
        prep_mods(0)

        lamt = P.sbuf("lamt", [128, 4], F32)
        neglam = P.sbuf("neglam", [128, 1], F32)
        gsub = P.sbuf("gsub", [128, 1], F32)
        lam_init0 = 0.8 - 0.6 * math.exp(-0.3 * 0)
        lq = pvs("lqk")
        ltmp = P.sbuf("ltmp", [128, 2, 64], F32)
        op("dve", lambda e: e.tensor_tensor(ltmp[:, 0, :], lq[:, 0:64], lq[:, 64:128], ALU.mult), [pvt], [ltmp])
        op("dve", lambda e: e.tensor_tensor(ltmp[:, 1, :], lq[:, 128:192], lq[:, 192:256], ALU.mult), [pvt], [ltmp])
        op("dve", lambda e: e.tensor_reduce(lamt[:, 0:2], ltmp[:], AX.X, ALU.add), [ltmp], [lamt])
        op("act", lambda e: e.activation(lamt[:, 2:4], lamt[:, 0:2], AF.Exp), [lamt], [lamt])
        op("dve", lambda e: e.scalar_tensor_tensor(neglam[:], lamt[:, 3:4], -lam_init0, lamt[:, 2:3], ALU.add, ALU.subtract), [lamt], [neglam])
        op("dve", lambda e: e.tensor_scalar(gsub[:], pvs("subln_g"), 1.0 - lam_init0, None, ALU.mult), [pvt], [gsub])

        Uv = Us.rearrange("(c p) t -> p c t", p=128)
        P.dma("sp", Uv[:, :, 0:15], zt[:, :, 0:15], reads=[zt], writes=[Us])
        P.dma("sp", Uv[:, :, 8207:8237], zt[:, :, 0:30], reads=[zt], writes=[Us])
        P.dma("sp", Uv[:, :, 8493:8508], zt[:, :, 0:15], reads=[zt], writes=[Us])

        P.push()
        win = P.sbuf("win", [128, 8, 2560], BF16)
        for k in range(8):
            load_cast(win, lambda c0, n, k=k: win[:, k, c0:c0 + n], evw[k * 128:(k + 1) * 128, :], 2560)
        xsb = [P.sbuf("xs", [128, 8, 512], F32) for _ in range(2)]
        csb = [P.sbuf("cs", [128, 2, 512], F32) for _ in range(2)]
        sq = P.sbuf("sq", [128, 8, 512], BF16)
        rt = P.sbuf("rt", [128, 512], F32)
        rstd = P.sbuf("rstd", [128, 512], F32)
        tmp = P.sbuf("tmp", [128, 8, 512], F32)
        h = P.sbuf("h", [128, 8, 512], BF16)
        sig = [P.sbuf("sig", [128, 512], F32) for _ in range(2)]
        ub = [P.sbuf("ub", [128, 4, 512], BF16) for _ in range(2)]
        rtm = [P.sbuf("rtm", [128, 4, 512], F32) for _ in range(2)]
        qkb = [P.sbuf("qkb", [128, 4, 512], BF16) for _ in range(4)]
        vtb = [P.sbuf("vtb", [128, 4, 512], BF16) for _ in range(2)]
        QTv = QTs.rearrange("(c p) t -> p c t", p=128)
        KTv = KTs.rearrange("(c p) t -> p c t", p=128)
        Vv = Vs.rearrange("(n p) f -> p n f", p=128)
        pbank = [1]

        def nb():
            pbank[0] = 1 + (pbank[0] % 7)
            return ps[pbank[0]]

        def loadA(blk):
            t0, nt, s = blk_range(blk)
            xs = xsb[blk % 2]
            cs = csb[blk % 2]
            P.dma("sp", xs[:, :, :nt], xTv[:, :, t0:t0 + nt], writes=[xs])
            P.dma("sp", cs[:, 0, :nt], ropeC[:, t0:t0 + nt], writes=[cs])
            P.dma("sp", cs[:, 1, :nt], ropeS[:, t0:t0 + nt], writes=[cs])

        loadA(0)
        for blk in range(NBLK):
            t0, nt, s = blk_range(blk)
            if blk + 1 < NBLK:
                loadA(blk + 1)
            xs = xsb[blk % 2]
            cs = csb[blk % 2]
            norm_mod(xs, nt, A1[s], mod[s], 0, sq, rt, rstd, tmp, h, ps[0])
            u = ub[blk % 2]
            for i in range(4):
                pa = nb()
                pg = nb()
                for k in range(8):
                    mm(pa[:, :nt], win[:, k, i * 128:(i + 1) * 128], h[:, k, :nt], k == 0, k == 7, [win, h], [pa])
                for k in range(8):
                    mm(pg[:, :nt], win[:, k, (4 + i) * 128:(5 + i) * 128], h[:, k, :nt], k == 0, k == 7, [win, h], [pg])
                sg = sig[i % 2]
                op("act", lambda e: e.activation(sg[:, :nt], pg[:, :nt], AF.Sigmoid), [pg], [sg])
                op("dve", lambda e: e.tensor_tensor(u[:, i, :nt], pa[:, :nt], sg[:, :nt], ALU.mult), [pa, sg], [u])
            c0 = ucol(t0)
            P.dma("sp", Uv[:, :, c0:c0 + nt], u[:, :, :nt], reads=[u], writes=[Us])
            for wi, (base, dst) in enumerate(((1024, QTv), (1536, KTv))):
                qk = qkb[(blk % 2) * 2 + wi]
                r = rtm[wi]
                for j in range(2):
                    pA = nb()
                    pB = nb()
                    for k in range(8):
                        mm(pA[:, :nt], win[:, k, base + j * 128:base + (j + 1) * 128], h[:, k, :nt], k == 0, k == 7, [win, h], [pA])
                    for k in range(8):
                        mm(pB[:, :nt], win[:, k, base + 256 + j * 128:base + 256 + (j + 1) * 128], h[:, k, :nt], k == 0, k == 7, [win, h], [pB])
                    Cc = cs[:, 0, :nt]
                    Ss = cs[:, 1, :nt]
                    op("dve", lambda e: e.tensor_tensor(r[:, 0, :nt], pA[:, :nt], Cc, ALU.mult), [pA, cs], [r])
                    op("dve", lambda e: e.tensor_tensor(r[:, 1, :nt], pB[:, :nt], Ss, ALU.mult), [pB, cs], [r])
                    op("dve", lambda e: e.tensor_tensor(qk[:, j, :nt], r[:, 0, :nt], r[:, 1, :nt], ALU.subtract), [r], [qk])
                    op("dve", lambda e: e.tensor_tensor(r[:, 2, :nt], pA[:, :nt], Ss, ALU.mult), [pA, cs], [r])
                    op("dve", lambda e: e.tensor_tensor(r[:, 3, :nt], pB[:, :nt], Cc, ALU.mult), [pB, cs], [r])
                    op("dve", lambda e: e.tensor_tensor(qk[:, 2 + j, :nt], r[:, 2, :nt], r[:, 3, :nt], ALU.add), [r], [qk])
                P.dma("sp", dst[:, :, t0:t0 + nt], qk[:, :, :nt], reads=[qk], writes=[QTs if wi == 0 else KTs])
            vt = vtb[blk % 2]
            nsub = nt // 128
            for sub in range(nsub):
                pv_ = nb()
                for k in range(8):
                    mm(pv_[:, :], h[:, k, sub * 128:(sub + 1) * 128], win[:, k, 2048:2560], k == 0, k == 7, [win, h], [pv_])
                op("act", lambda e: e.copy(vt[:, sub, :], pv_[:, :]), [pv_], [vt])
            P.dma("sp", Vv[:, blk * 4:blk * 4 + nsub, :], vt[:, 0:nsub, :], reads=[vt], writes=[Vs])
        P.barrier()
        P.pop()

        P.push()
        Vall = P.sbuf("Vall", [128, 66, 512], BF16)
        for i in range(6):
            P.dma("sp", Vall[:, i * 11:(i + 1) * 11, :], Vv[:, i * 11:(i + 1) * 11, :], reads=[Vs], writes=[Vall])
        KTh = [P.sbuf("KTh", [128, T], BF16) for _ in range(2)]
        QTb = [P.sbuf("QTb", [128, 512], BF16) for _ in range(2)]
        PT = [P.sbuf("PT", [128, 3, 512], BF16) for _ in range(2)]
        pvsb = [P.sbuf("pvsb", [128, 512], F32) for _ in range(2)]
        zsb = [P.sbuf("zsb", [128, 512], F32) for _ in range(2)]
        zacc = P.sbuf("zacc", [128, 512], F32)
        ztmp = P.sbuf("ztmp", [128, 512], F32)
        rz = [P.sbuf("rz", [128, 512], F32) for _ in range(2)]
        oo = [P.sbuf("oo", [128, 512], F32) for _ in range(2)]
        ofin = P.sbuf("ofin", [128, 512], F32)
        osq = P.sbuf("osq", [128, 512], BF16)
        ort = P.sbuf("ort", [128, 512], F32)
        orstd = P.sbuf("orstd", [128, 512], F32)
        aob = [P.sbuf("aob", [128, 512], BF16) for _ in range(2)]

        def rows_for(h_, m):
            g = 2 * h_ + m
            j = g // 4
            gg = g % 4
            return j * 128 + 32 * gg, 256 + j * 128 + 32 * gg

        def loadK(h_):
            kt_ = KTh[h_ % 2]
            for m in range(2):
                ra, rb = rows_for(h_, m)
                P.dma("sp", kt_[64 * m:64 * m + 32, :], KTs[ra:ra + 32, :], reads=[KTs], writes=[kt_])
                P.dma("sp", kt_[64 * m + 32:64 * m + 64, :], KTs[rb:rb + 32, :], reads=[KTs], writes=[kt_])

        def loadQ(idx):
            h_, blk = divmod(idx, NBLK)
            t0, nt, s = blk_range(blk)
            q_ = QTb[idx % 2]
            for m in range(2):
                ra, rb = rows_for(h_, m)
                P.dma("sp", q_[64 * m:64 * m + 32, :nt], QTs[ra:ra + 32, t0:t0 + nt], reads=[QTs], writes=[q_])
                P.dma("sp", q_[64 * m + 32:64 * m + 64, :nt], QTs[rb:rb + 32, t0:t0 + nt], reads=[QTs], writes=[q_])

        groups = []
        for idx in range(4 * NBLK):
            h_, blk = divmod(idx, NBLK)
            t0, nt, s = blk_range(blk)
            ktiles = list(range(66)) if s == 0 else [64, 65]
            for m in range(2):
                gl = [ktiles[i:i + 3] for i in range(0, len(ktiles), 3)]
                for gi_, kts in enumerate(gl):
                    groups.append((idx, h_, blk, nt, m, kts, gi_ == 0, gi_ == len(gl) - 1))
        loadK(0)
        loadQ(0)
        ppv = ps[6]
        pz = ps[7]

        def emit_S(g):
            idx, h_, blk, nt, m, kts, first, last = groups[g]
            if m == 0 and first:
                if blk == 0 and h_ + 1 < 4:
                    loadK(h_ + 1)
                if idx + 1 < 4 * NBLK:
                    loadQ(idx + 1)
            kt_ = KTh[h_ % 2]
            q_ = QTb[idx % 2]
            sc = sc3[g % 2]
            for j, kt in enumerate(kts):
                mm(sc[:, j * 512:j * 512 + nt], kt_[64 * m:64 * m + 64, kt * 128:(kt + 1) * 128], q_[64 * m:64 * m + 64, :nt], True, True, [kt_, q_], [sc])
            pt = PT[g % 2]
            n = len(kts)
            op("act", lambda e: e.activation(pt[:, 0:n, :nt], sc[:, :].rearrange("p (j c) -> p j c", c=512)[:, 0:n, :nt], AF.Exp, scale=0.125), [sc], [pt])

        def emit_PV(g):
            idx, h_, blk, nt, m, kts, first, last = groups[g]
            pt = PT[g % 2]
            for j, kt in enumerate(kts):
                f_ = first and j == 0
                l_ = last and j == len(kts) - 1
                mm(ppv[:, :nt], Vall[:, kt, h_ * 128:(h_ + 1) * 128], pt[:, j, :nt], f_, l_, [Vall, pt], [ppv])
                mm(pz[:, :nt], ones_bf[:], pt[:, j, :nt], f_, l_, [ones_bf, pt], [pz])
            if not last:
                return
            t0 = blk_range(blk)[0]
            op("act", lambda e: e.copy(zsb[m][:, :nt], pz[:, :nt]), [pz], [zsb[m]])
            op("act", lambda e: e.copy(pvsb[m][:, :nt], ppv[:, :nt]), [ppv], [pvsb[m]])
            op("dve", lambda e: e.reciprocal(rz[m][:, :nt], zsb[m][:, :nt]), [zsb[m]], [rz[m]])
            if m == 0:
                return
            op("dve", lambda e: e.tensor_tensor(oo[1][:, :nt], pvsb[1][:, :nt], rz[1][:, :nt], ALU.mult), [pvsb[1], rz[1]], [oo[1]])
            op("dve", lambda e: e.tensor_tensor(oo[0][:, :nt], pvsb[0][:, :nt], rz[0][:, :nt], ALU.mult), [pvsb[0], rz[0]], [oo[0]])
            op("dve", lambda e: e.scalar_tensor_tensor(ofin[:, :nt], oo[1][:, :nt], neglam[:, 0:1], oo[0][:, :nt], ALU.mult, ALU.add),
               [oo[0], oo[1], neglam], [ofin])
            op("dve", lambda e: e.tensor_tensor(osq[:, :nt], ofin[:, :nt], ofin[:, :nt], ALU.mult), [ofin], [osq])
            pq_ = sc3[g % 2]
            mm(pq_[:, :nt], ones_bf[:], osq[:, :nt], True, True, [ones_bf, osq], [pq_])
            op("act", lambda e: e.activation(ort[:, :nt], pq_[:, :nt], AF.Sqrt, bias=epst[:], scale=1.0 / 128.0), [pq_, epst], [ort])
            op("dve", lambda e: e.reciprocal(orstd[:, :nt], ort[:, :nt]), [ort], [orstd])
            ao = aob[idx % 2]
            op("dve", lambda e: e.scalar_tensor_tensor(ao[:, :nt], ofin[:, :nt], gsub[:, 0:1], orstd[:, :nt], ALU.mult, ALU.mult),
               [ofin, gsub, orstd], [ao])
            P.dma("sp", AOs[h_ * 128:(h_ + 1) * 128, t0:t0 + nt], ao[:, :nt], reads=[ao], writes=[AOs])

        for g in range(len(groups) + 1):
            if g < len(groups):
                emit_S(g)
            if g >= 1:
                emit_PV(g - 1)
        P.barrier()
        P.pop()

        P.push()
        wout = P.sbuf("wout", [128, 8, 1024], BF16)
        for k in range(8):
            load_cast(wout, lambda c0, n, k=k: wout[:, k, c0:c0 + n], evwo[k * 128:(k + 1) * 128, :], 1024)
        Ub = [P.sbuf("Ub", [128, 4, 542], BF16) for _ in range(2)]
        dg = P.sbuf("dg", [128, 124, 128], BF16)
        AOb = [P.sbuf("AOb", [128, 4, 512], BF16) for _ in range(2)]
        xsb = [P.sbuf("xs", [128, 8, 512], F32) for _ in range(2)]
        acc = P.sbuf("acc", [128, 4, 512], F32)
        vb = P.sbuf("vb", [128, 4, 512], BF16)
        v2 = P.sbuf("v2", [128, 4, 512], BF16)
        msb = P.sbuf("msb", [128, 512], F32)
        nm2 = P.sbuf("nm2", [128, 512], F32)
        var = P.sbuf("var", [128, 512], F32)
        crt = P.sbuf("crt", [128, 512], F32)
        crs = P.sbuf("crs", [128, 512], F32)
        ctmp = P.sbuf("ctmp", [128, 4, 512], F32)
        cout = P.sbuf("cout", [128, 4, 512], BF16)
        x1b = [P.sbuf("x1b", [128, 8, 512], F32) for _ in range(2)]
        AOv = AOs.rearrange("(c p) t -> p c t", p=128)
        cw = pvs("ev_conv_w")
        for idx_ in range(124):
            op("dve", lambda e: e.tensor_scalar(dg[:, idx_, :], identb[:], cw[:, idx_:idx_ + 1], None, ALU.mult), [identb, pvt], [dg])

        def loadD1(blk):
            t0, nt, s = blk_range(blk)
            c0 = ucol(t0)
            P.dma("sp", Ub[blk % 2][:, :, :nt + 30], Uv[:, :, c0 - 15:c0 + nt + 15], reads=[Us], writes=[Ub[blk % 2]])
            P.dma("sp", AOb[blk % 2][:, :, :nt], AOv[:, :, t0:t0 + nt], reads=[AOs], writes=[AOb[blk % 2]])
            P.dma("sp", xsb[blk % 2][:, :, :nt], xTv[:, :, t0:t0 + nt], writes=[xsb[blk % 2]])

        loadD1(0)
        for blk in range(NBLK):
            t0, nt, s = blk_range(blk)
            if blk + 1 < NBLK:
                loadD1(blk + 1)
            U_ = Ub[blk % 2]
            ao_ = AOb[blk % 2]
            xs = xsb[blk % 2]
            x1 = x1b[blk % 2]
            for c in range(4):
                pacc = ps[2 + c]
                for j in range(31):
                    mm(pacc[:, :nt], dg[:, j * 4 + c, :], U_[:, c, j:j + nt], j == 0, j == 30, [dg, U_], [pacc])
                op("act", lambda e: e.activation(acc[:, c, :nt], pacc[:, :nt], AF.Identity, bias=pvs("ev_conv_b", c, 1)), [pacc, pvt], [acc])
            op("act", lambda e: e.copy(vb[:, :, :nt], acc[:, :, :nt]), [acc], [vb])
            op("act", lambda e: e.activation(v2[:, :, :nt], acc[:, :, :nt], AF.Square), [acc], [v2])
            for c in range(4):
                mm(ps[0][:, :nt], onesm_bf[:], vb[:, c, :nt], c == 0, c == 3, [onesm_bf, vb], [ps[0]])
            for c in range(4):
                mm(ps[1][:, :nt], onesm_bf[:], v2[:, c, :nt], c == 0, c == 3, [onesm_bf, v2], [ps[1]])
            op("act", lambda e: e.copy(msb[:, :nt], ps[0][:, :nt]), [ps[0]], [msb])
            op("dve", lambda e: e.scalar_tensor_tensor(nm2[:, :nt], msb[:, :nt], -1.0, msb[:, :nt], ALU.mult, ALU.mult), [msb], [nm2])
            op("dve", lambda e: e.tensor_tensor(var[:, :nt], ps[1][:, :nt], nm2[:, :nt], ALU.add), [ps[1], nm2], [var])
            op("act", lambda e: e.activation(crt[:, :nt], var[:, :nt], AF.Sqrt, bias=epst[:]), [var, epst], [crt])
            op("dve", lambda e: e.reciprocal(crs[:, :nt], crt[:, :nt]), [crt], [crs])
            for c in range(4):
                op("dve", lambda e: e.tensor_tensor(ctmp[:, c, :nt], acc[:, c, :nt], msb[:, :nt], ALU.subtract), [acc, msb], [ctmp])
                op("dve", lambda e: e.tensor_tensor(ctmp[:, c, :nt], ctmp[:, c, :nt], crs[:, :nt], ALU.mult), [ctmp, crs], [ctmp])
                op("dve", lambda e: e.tensor_scalar(ctmp[:, c, :nt], ctmp[:, c, :nt], pvs("ev_ln_g", c, 1), pvs("ev_ln_b", c, 1), ALU.mult, ALU.add),
                   [ctmp, pvt], [ctmp])
                op("act", lambda e: e.activation(cout[:, c, :nt], ctmp[:, c, :nt], AF.Silu), [ctmp], [cout])
            for o in range(8):
                po = ps[2 + (o % 6)]
                for k in range(8):
                    rhs = cout[:, k, :nt] if k < 4 else ao_[:, k - 4, :nt]
                    mm(po[:, :nt], wout[:, k, o * 128:(o + 1) * 128], rhs, k == 0, k == 7, [wout, cout, ao_], [po])
                op("dve", lambda e: e.scalar_tensor_tensor(x1[:, o, :nt], po[:, :nt], mod[s][:, 16 + o:17 + o], xs[:, o, :nt], ALU.mult, ALU.add),
                   [po, mod[s], xs], [x1])
            P.dma("sp", XAv[:, :, t0:t0 + nt], x1[:, :, :nt], reads=[x1], writes=[XAs])
        P.barrier()
        P.pop()

    def mlp_phase(li, src_v, src_buf, dst_v, dst_buf, nblk, final=False):
        P.push()
        w1 = P.sbuf("w1", [128, 8, 4096], BF16)
        w2 = P.sbuf("w2", [128, 32, 1024], BF16)
        for k in range(8):
            load_cast(w1, lambda c0, n, k=k: w1[:, k, c0:c0 + n], w1d[li, k * 128:(k + 1) * 128, :], 4096)
        for k in range(32):
            load_cast(w2, lambda c0, n, k=k: w2[:, k, c0:c0 + n], w2d[li, k * 128:(k + 1) * 128, :], 1024)
        xsb = [P.sbuf("xs", [128, 8, 512], F32) for _ in range(1)]
        rt = P.sbuf("rt", [128, 512], F32)
        rstd = P.sbuf("rstd", [128, 512], F32)
        h2 = P.sbuf("h2", [128, 8, 512], BF16)
        hd = P.sbuf("hid", [128, 32, 512], BF16)
        sq = hd
        rl = [P.sbuf("rl", [128, 512], BF16) for _ in range(2)]
        for blk in range(nblk):
            t0, nt, s = blk_range(blk)
            xs = xsb[0]
            P.dma("sp", xs[:, :, :nt], src_v[:, :, t0:t0 + nt], reads=[src_buf], writes=[xs])
            op("act", lambda e: e.activation(sq[:, 0:8, :nt], xs[:, :, :nt], AF.Square), [xs], [sq])
            for k in range(8):
                mm(ps[0][:, :nt], ones_bf[:], sq[:, k, :nt], k == 0, k == 7, [ones_bf, sq], [ps[0]])
            op("act", lambda e: e.activation(rt[:, :nt], ps[0][:, :nt], AF.Sqrt, bias=epst[:], scale=1.0 / D), [ps[0], epst], [rt])
            op("dve", lambda e: e.reciprocal(rstd[:, :nt], rt[:, :nt]), [rt], [rstd])
            for k in range(8):
                tk = ps[1 + (k % 2)]
                op("dve", lambda e: e.scalar_tensor_tensor(tk[:, :nt], xs[:, k, :nt], A2[s][:, k:k + 1], rstd[:, :nt], ALU.mult, ALU.mult),
                   [xs, rstd, A2[s]], [tk])
                op("act", lambda e: e.activation(h2[:, k, :nt], tk[:, :nt], AF.Identity, bias=mod[s][:, 24 + k:25 + k]), [tk, mod[s]], [h2])
            for j in range(32):
                ph = ps[3 + (j % 3)]
                for k in range(8):
                    mm(ph[:, :nt], w1[:, k, j * 128:(j + 1) * 128], h2[:, k, :nt], k == 0, k == 7, [w1, h2], [ph])
                r_ = rl[j % 2]
                op("act", lambda e: e.activation(r_[:, :nt], ph[:, :nt], AF.Relu), [ph], [r_])
                if j % 4 == 3:
                    op("dve", lambda e: e.tensor_tensor(hd[:, j, :nt], r_[:, :nt], r_[:, :nt], ALU.mult), [r_], [hd])
                else:
                    op("dve", lambda e: e.tensor_tensor(hd[:, j, :nt], r_[:, :nt], r_[:, :nt], ALU.mult), [r_], [hd])
            for o in range(8):
                po = ps[6 + (o % 2)]
                for j in range(32):
                    mm(po[:, :nt], w2[:, j, o * 128:(o + 1) * 128], hd[:, j, :nt], j == 0, j == 31, [w2, hd], [po])
                op("dve", lambda e: e.scalar_tensor_tensor(xs[:, o, :nt], po[:, :nt], mod[s][:, 40 + o:41 + o], xs[:, o, :nt],
                                                           ALU.mult, ALU.add), [po, mod[s], xs], [xs])
            if final:
                op("act", lambda e: e.activation(sq[:, 0:8, :nt], xs[:, :, :nt], AF.Square), [xs], [sq])
                for k in range(8):
                    mm(ps[0][:, :nt], ones_bf[:], sq[:, k, :nt], k == 0, k == 7, [ones_bf, sq], [ps[0]])
                op("act", lambda e: e.activation(rt[:, :nt], ps[0][:, :nt], AF.Sqrt, bias=epst[:], scale=1.0 / D), [ps[0], epst], [rt])
                op("dve", lambda e: e.reciprocal(rstd[:, :nt], rt[:, :nt]), [rt], [rstd])
                fg = pvs("fing")
                for k in range(8):
                    op("dve", lambda e: e.scalar_tensor_tensor(xs[:, k, :nt], xs[:, k, :nt], fg[:, k:k + 1], rstd[:, :nt], ALU.mult, ALU.mult),
                       [xs, rstd, pvt], [xs])
            P.dma("sp", dst_v[:, :, t0:t0 + nt], xs[:, :, :nt], reads=[xs], writes=[dst_buf])
        P.barrier()
        P.pop()

    PQW = 8456

    def pcol(t0):
        return 2 + t0 if t0 < L else 8198 + (t0 - L)

    def layer1(parts=l1parts):
        prep_mods(1)
        odw = P.dram("odw", [D, 3600], F32, "ExternalInput")
        pbd = P.dram("pb", [128, NPB], F32, "ExternalInput")
        dnmd = P.dram("dnm", [128, 6, 4, 128], F32, "ExternalInput")
        PFs = scratch("PFs", [3584, PQW], BF16)
        BGTs = scratch("BGTs", [T, 16], F32)
        UHs = scratch("UHs", [L, 1536], BF16)
        GSs = scratch("GSs", [L, 512], BF16)
        QNTs = scratch("QNTs", [512, T], F32)
        KNTs = scratch("KNTs", [512, T], F32)
        KTMs = scratch("KTMs", [T, 512], F32)
        VTMs = scratch("VTMs", [T, 512], F32)
        ODs = [scratch("ODf", [L, 512], F32), scratch("ODb", [L, 512], F32)]
        DNs = scratch("DNs", [L, 512], F32)
        HYs = scratch("HYs", [L, 512], BF16)
        PFv = PFs.rearrange("(c p) t -> p c t", p=128)
        pbt = P.sbuf("pbt", [128, NPB], F32)
        P.dma("sp", pbt[:], pbd[:], writes=[pbt])
        dnkd = P.dram("dnk", [128, 7, 4, 128], F32, "ExternalInput")
        ident = identf
        onec = P.sbuf("onec", [128, 1], F32)
        op("dve", lambda e: e.memset(onec[:], 1.0), writes=[onec])

        def pbs(name, j=0, n=None):
            o, c = PB_OFF[name]
            n = c - j if n is None else n
            return pbt[:, o + j:o + j + n]

        nega = P.sbuf("nega", [128, 8], F32)
        op("act", lambda e: e.activation(nega[:], pbs("alog"), AF.Exp), [pbt], [nega])
        op("dve", lambda e: e.tensor_scalar(nega[:], nega[:], -1.0, None, ALU.mult), [nega], [nega])
        pbank = [0]

        def nb():
            pbank[0] = (pbank[0] + 1) % 8
            return ps[pbank[0]]

        if "A" in parts:
            P.push()
            z4 = P.sbuf("z4", [128, 28, 4], BF16)
            op("dve", lambda e: e.memset(z4[:], 0.0), writes=[z4])
            P.dma("sp", PFv[:, :, 0:2], z4[:, :, 0:2], reads=[z4], writes=[PFs])
            P.dma("sp", PFv[:, :, 8194:8198], z4[:, :, 0:4], reads=[z4], writes=[PFs])
            P.dma("sp", PFv[:, :, 8454:8456], z4[:, :, 0:2], reads=[z4], writes=[PFs])
            win1 = P.sbuf("win1", [128, 8, 3600], BF16)
            for k in range(8):
                load_cast(win1, lambda c0, n, k=k: win1[:, k, c0:c0 + n], odw[k * 128:(k + 1) * 128, :], 3600)
            xsb = [P.sbuf("xs", [128, 8, 512], F32) for _ in range(2)]
            sq = P.sbuf("sq", [128, 8, 512], BF16)
            rt = P.sbuf("rt", [128, 512], F32)
            rstd = P.sbuf("rstd", [128, 512], F32)
            tmp = P.sbuf("tmp", [128, 8, 512], F32)
            h = P.sbuf("h", [128, 8, 512], BF16)
            pfb = [P.sbuf("pfb", [128, 4, 512], BF16) for _ in range(3)]
            bgt = [P.sbuf("bgt", [128, 4, 16], F32) for _ in range(2)]
            t8 = P.sbuf("t8", [128, 8], F32)
            BGTv = BGTs.rearrange("(n p) f -> p n f", p=128)
            P.dma("sp", xsb[0][:, :, :512], X1v[:, :, 0:512], reads=[X1s], writes=[xsb[0]])
            for blk in range(NBLK):
                t0, nt, s = blk_range(blk)
                if blk + 1 < NBLK:
                    t0n, ntn, _ = blk_range(blk + 1)
                    P.dma("sp", xsb[(blk + 1) % 2][:, :, :ntn], X1v[:, :, t0n:t0n + ntn], reads=[X1s], writes=[xsb[(blk + 1) % 2]])
                xs = xsb[blk % 2]
                norm_mod(xs, nt, A1[s], mod[s], 0, sq, rt, rstd, tmp, h, ps[0])
                pc0 = pcol(t0)
                for c4 in range(7):
                    pf = pfb[c4 % 3]
                    for cc in range(4):
                        c = c4 * 4 + cc
                        pc = nb()
                        for k in range(8):
                            mm(pc[:, :nt], win1[:, k, c * 128:(c + 1) * 128], h[:, k, :nt], k == 0, k == 7, [win1, h], [pc])
                        if cc % 2 == 0:
                            op("act", lambda e: e.copy(pf[:, cc, :nt], pc[:, :nt]), [pc], [pf])
                        else:
                            op("dve", lambda e: e.tensor_copy(pf[:, cc, :nt], pc[:, :nt]), [pc], [pf])
                    P.dma("sp", PFv[:, c4 * 4:(c4 + 1) * 4, pc0:pc0 + nt], pf[:, :, :nt], reads=[pf], writes=[PFs])
                bg = bgt[blk % 2]
                nsub = nt // 128
                for sub in range(nsub):
                    pb_ = nb()
                    for k in range(8):
                        mm(pb_[:, 0:16], h[:, k, sub * 128:(sub + 1) * 128], win1[:, k, 3584:3600], k == 0, k == 7, [win1, h], [pb_])
                    op("act", lambda e: e.activation(bg[:, sub, 0:8], pb_[:, 0:8], AF.Sigmoid), [pb_], [bg])
                    op("dve", lambda e: e.tensor_tensor(t8[:], pb_[:, 8:16], pbs("dtb"), ALU.add), [pb_, pbt], [t8])
                    op("act", lambda e: e.activation(t8[:], t8[:], AF.Exp), [t8], [t8])
                    op("act", lambda e: e.activation(t8[:], t8[:], AF.Ln, bias=onec[:]), [t8, onec], [t8])
                    op("dve", lambda e: e.tensor_tensor(bg[:, sub, 8:16], t8[:], nega[:], ALU.mult), [t8, nega], [bg])
                P.dma("sp", BGTv[:, blk * 4:blk * 4 + nsub, :], bg[:, 0:nsub, :], reads=[bg], writes=[BGTs])
            P.barrier()
            P.pop()

        if "B" in parts:
            P.push()
            pq = [P.sbuf("pq", [128, 12, 516], BF16) for _ in range(2)]
            dgh = P.sbuf("dgh", [128, 36, 128], BF16)
            dgd = P.sbuf("dgd", [128, 60, 128], BF16)
            accH = P.sbuf("accH", [128, 12, 512], F32)
            accG = P.sbuf("accG", [128, 4, 512], F32)
            acc = P.sbuf("acc", [128, 12, 512], F32)
            sqb = P.sbuf("sqb", [128, 512], BF16)
            rtb = P.sbuf("rtb", [128, 512], F32)
            rnb = P.sbuf("rnb", [128, 512], F32)
            ut = [P.sbuf("ut", [128, 4, 1536], BF16) for _ in range(2)]
            gtb = [P.sbuf("gtb", [128, 4, 512], BF16) for _ in range(2)]
            ktm = [P.sbuf("ktm", [128, 4, 512], F32) for _ in range(2)]
            vtm = [P.sbuf("vtm", [128, 4, 512], F32) for _ in range(2)]
            UHv = UHs.rearrange("(n p) f -> p n f", p=128)
            GSv = GSs.rearrange("(n p) f -> p n f", p=128)
            KTMv = KTMs.rearrange("(n p) f -> p n f", p=128)
            VTMv = VTMs.rearrange("(n p) f -> p n f", p=128)
            QNTv = QNTs.rearrange("(c p) t -> p c t", p=128)
            KNTv = KNTs.rearrange("(c p) t -> p c t", p=128)
            hw = pvs("hy_sw")
            dw = pvs("dn_cw")
            for idx_ in range(36):
                op("dve", lambda e: e.tensor_scalar(dgh[:, idx_, :], identb[:], hw[:, idx_:idx_ + 1], None, ALU.mult), [identb, pvt], [dgh])
            for idx_ in range(60):
                op("dve", lambda e: e.tensor_scalar(dgd[:, idx_, :], identb[:], dw[:, idx_:idx_ + 1], None, ALU.mult), [identb, pvt], [dgd])
            li = 0
            for blk in range(NBLK):
                t0, nt, s = blk_range(blk)
                pc0 = pcol(t0)
                nsub = nt // 128
                if s == 0:
                    pq_ = pq[li % 2]; li += 1
                    P.dma("sp", pq_[:, :, :nt + 4], PFv[:, 0:12, pc0 - 2:pc0 + nt + 2], reads=[PFs], writes=[pq_])
                    for c in range(12):
                        pc_ = nb()
                        for j in range(3):
                            mm(pc_[:, :nt], dgh[:, j * 12 + c, :], pq_[:, c, 1 + j:1 + j + nt], j == 0, j == 2, [dgh, pq_], [pc_])
                        op("act", lambda e: e.activation(accH[:, c, :nt], pc_[:, :nt], AF.Identity, bias=pvs("hy_sb", c, 1)), [pc_, pvt], [accH])
                    u_ = ut[blk % 2]
                    for sub in range(nsub):
                        for g3 in range(3):
                            pt = nb()
                            for j in range(4):
                                op("pe", lambda e: e.transpose(pt[:, j * 128:(j + 1) * 128], accH[:, g3 * 4 + j, sub * 128:(sub + 1) * 128], ident[:]),
                                   [accH, ident], [pt])
                            op("act", lambda e: e.copy(u_[:, sub, g3 * 512:(g3 + 1) * 512], pt[:, :]), [pt], [u_])
                    P.dma("sp", UHv[:, blk * 4:blk * 4 + nsub, :], u_[:, 0:nsub, :], reads=[u_], writes=[UHs])
                    pq_ = pq[li % 2]; li += 1
                    P.dma("sp", pq_[:, 0:4, :nt], PFv[:, 12:16, pc0:pc0 + nt], reads=[PFs], writes=[pq_])
                    op("act", lambda e: e.activation(accG[:, 0:4, :nt], pq_[:, 0:4, :nt], AF.Silu), [pq_], [accG])
                    g_ = gtb[blk % 2]
                    for sub in range(nsub):
                        pt = nb()
                        for j in range(4):
                            op("pe", lambda e: e.transpose(pt[:, j * 128:(j + 1) * 128], accG[:, j, sub * 128:(sub + 1) * 128], ident[:]), [accG, ident], [pt])
                        op("act", lambda e: e.copy(g_[:, sub, :], pt[:, :]), [pt], [g_])
                    P.dma("sp", GSv[:, blk * 4:blk * 4 + nsub, :], g_[:, 0:nsub, :], reads=[g_], writes=[GSs])
                pq_ = pq[li % 2]; li += 1
                P.dma("sp", pq_[:, :, :nt + 4], PFv[:, 16:28, pc0 - 2:pc0 + nt + 2], reads=[PFs], writes=[pq_])
                for c in range(12):
                    pc_ = nb()
                    for j in range(5):
                        mm(pc_[:, :nt], dgd[:, j * 12 + c, :], pq_[:, c, j:j + nt], j == 0, j == 4, [dgd, pq_], [pc_])
                    op("act", lambda e: e.activation(acc[:, c, :nt], pc_[:, :nt], AF.Silu), [pc_], [acc])
                for c in range(8):
                    op("act", lambda e: e.activation(sqb[:, :nt], acc[:, c, :nt], AF.Square), [acc], [sqb])
                    pss = nb()
                    mm(pss[:, :nt], ones_bf[:], sqb[:, :nt], True, True, [ones_bf, sqb], [pss])
                    op("act", lambda e: e.activation(rtb[:, :nt], pss[:, :nt], AF.Sqrt, bias=epst[:]), [pss, epst], [rtb])
                    op("dve", lambda e: e.reciprocal(rnb[:, :nt], rtb[:, :nt]), [rtb], [rnb])
                    op("dve", lambda e: e.tensor_tensor(acc[:, c, :nt], acc[:, c, :nt], rnb[:, :nt], ALU.mult), [acc, rnb], [acc])
                P.dma("sp", QNTv[:, :, t0:t0 + nt], acc[:, 0:4, :nt], reads=[acc], writes=[QNTs])
                P.dma("sp", KNTv[:, :, t0:t0 + nt], acc[:, 4:8, :nt], reads=[acc], writes=[KNTs])
                k_ = ktm[blk % 2]
                v_ = vtm[blk % 2]
                for sub in range(nsub):
                    for (dst, c0) in ((k_, 4), (v_, 8)):
                        pt = nb()
                        for j in range(4):
                            op("pe", lambda e: e.transpose(pt[:, j * 128:(j + 1) * 128], acc[:, c0 + j, sub * 128:(sub + 1) * 128], ident[:]), [acc, ident], [pt])
                        op("act", lambda e: e.copy(dst[:, sub, :], pt[:, :]), [pt], [dst])
                P.dma("sp", KTMv[:, blk * 4:blk * 4 + nsub, :], k_[:, 0:nsub, :], reads=[k_], writes=[KTMs])
                P.dma("sp", VTMv[:, blk * 4:blk * 4 + nsub, :], v_[:, 0:nsub, :], reads=[v_], writes=[VTMs])
            P.barrier()
            P.pop()

        if "DN" in parts:
            P.push()
            dnm = P.sbuf("dnm", [128, 6, 4, 128], F32)
            P.dma("sp", dnm[:], dnmd[:], writes=[dnm])
            dnk = P.sbuf("dnk", [128, 7, 4, 128], F32)
            P.dma("sp", dnk[:], dnkd[:], writes=[dnk])
            ones_f = dnm[:, 5, 2, :]
            KNTh = KNTs.rearrange("(h p) t -> p h t", p=128)
            QNTh = QNTs.rearrange("(h p) t -> p h t", p=128)
            Sst = [P.sbuf("Sst", [128, 4, 128], F32) for _ in range(2)]
            for d in range(2):
                op("dve", lambda e: e.memset(Sst[d][:], 0.0), writes=[Sst[d]])
            bnames = ["kT", "ktm", "vtm", "rhsg", "E1", "E2", "DmM", "DTm", "Pa", "Qa", "Dm", "Ck", "CkT", "Ee", "Ee2", "X", "vb", "kbg"]
            rnames = ["qT", "qdec", "AT", "u", "wT", "kdec", "vnew", "o"]
            bufs = [{n: P.sbuf("dn_" + n, [128, 4, 128], F32) for n in bnames} for _ in range(2)]
            rbufs = [[{n: P.sbuf("dr_" + n, [128, 4, 128], F32) for n in rnames} for _ in range(2)] for _ in range(2)]
            bgs = [[P.sbuf("dn_bg", [128, 16], F32) for _ in range(2)] for _ in range(2)]
            sms = [[P.sbuf("dn_sm", [128, 32], F32) for _ in range(2)] for _ in range(2)]
            isq = 1.0 / math.sqrt(128.0)

            def f512(b):
                return b[:].rearrange("p h j -> p (h j)")

            def dn_prep(tile, d, par, banks):
                free = list(banks)
                B_ = bufs[d]
                R_ = rbufs[d][par]
                bg = bgs[d][par]
                sm = sms[d][par]
                is_ctx = tile >= 64
                t0 = tile * 128
                kT, ktm_, vtm_ = B_["kT"], B_["ktm"], B_["vtm"]
                qT = R_["qT"]
                P.dma("sp", kT[:], KNTh[:, :, t0:t0 + 128], reads=[KNTs], writes=[kT])
                if not is_ctx:
                    P.dma("sp", qT[:], QNTh[:, :, t0:t0 + 128], reads=[QNTs], writes=[qT])
                P.dma("sp", f512(ktm_), KTMs[t0:t0 + 128, :], reads=[KTMs], writes=[ktm_])
                P.dma("sp", f512(vtm_), VTMs[t0:t0 + 128, :], reads=[VTMs], writes=[vtm_])
                P.dma("sp", bg[:], BGTs[t0:t0 + 128, :], reads=[BGTs], writes=[bg])
                yield
                bcol = lambda h_: bg[:, 4 * d + h_:4 * d + h_ + 1]
                gcols = bg[:, 8 + 4 * d:12 + 4 * d]
                TRI = dnm[:, 5, d, :]
                MS4 = dnm[:, 0 + d, :, :]
                MTI4 = dnm[:, 2 + d, :, :]
                I4 = dnm[:, 4, :, :]
                pg = free.pop()
                mm(pg[:, 0:4], TRI, gcols, True, True, [dnm, bg], [pg])
                mm(pg[:, 4:8], ones_f, gcols, True, True, [dnm, bg], [pg])
                rhsg = B_["rhsg"]
                for h_ in range(4):
                    op("dve", lambda e: e.tensor_scalar(rhsg[:, h_, :], TRI, bg[:, 8 + 4 * d + h_:9 + 4 * d + h_], None, ALU.mult), [dnm, bg], [rhsg])
                pr = free.pop()
                mm(pr[:, :], ones_f, f512(rhsg), True, True, [dnm, rhsg], [pr])
                yield
                op("act", lambda e: e.copy(sm[:, 0:4], pg[:, 0:4]), [pg], [sm])
                op("act", lambda e: e.copy(sm[:, 28:32], pg[:, 4:8]), [pg], [sm])
                free.append(pg)
                op("dve", lambda e: e.tensor_scalar(sm[:, 4:8], sm[:, 0:4], -1.0, None, ALU.mult), [sm], [sm])
                op("act", lambda e: e.activation(sm[:, 8:12], sm[:, 0:4], AF.Exp), [sm], [sm])
                yield
                op("dve", lambda e: e.tensor_tensor(sm[:, 16:20], sm[:, 28:32], sm[:, 0:4], ALU.subtract), [sm], [sm])
                op("act", lambda e: e.activation(sm[:, 16:20], sm[:, 16:20], AF.Exp), [sm], [sm])
                op("act", lambda e: e.activation(sm[:, 20:24], sm[:, 28:32], AF.Exp), [sm], [sm])
                op("dve", lambda e: e.tensor_tensor(sm[:, 24:28], bg[:, 4 * d:4 * d + 4], sm[:, 8:12], ALU.mult), [sm, bg], [sm])
                yield
                E1, E2, DmM, DTm = B_["E1"], B_["E2"], B_["DmM"], B_["DTm"]
                qdec = R_["qdec"]
                for h_ in range(4):
                    op("act", lambda e: e.activation(E1[:, h_, :], pr[:, h_ * 128:(h_ + 1) * 128], AF.Exp, bias=sm[:, h_:h_ + 1], scale=-1.0), [pr, sm], [E1])
                yield
                if not is_ctx:
                    for h_ in range(4):
                        op("act", lambda e: e.activation(E2[:, h_, :], pr[:, h_ * 128:(h_ + 1) * 128], AF.Exp, bias=sm[:, 4 + h_:5 + h_], scale=1.0), [pr, sm], [E2])
                    op("act", lambda e: e.activation(f512(qdec), pr[:, :], AF.Exp), [pr], [qdec])
                free.append(pr)
                yield
                op("dve", lambda e: e.scalar_tensor_tensor(f512(DmM), f512(E1), 1.0, MS4.rearrange("p h j -> p (h j)"), ALU.min, ALU.mult), [E1, dnm], [DmM])
                if not is_ctx:
                    op("dve", lambda e: e.scalar_tensor_tensor(f512(DTm), f512(E2), 1.0, MTI4.rearrange("p h j -> p (h j)"), ALU.min, ALU.mult), [E2, dnm], [DTm])
                    op("dve", lambda e: e.scalar_tensor_tensor(f512(qdec), f512(qT), isq, f512(qdec), ALU.mult, ALU.mult), [qT, qdec], [qdec])
                pk = free.pop()
                for h_ in range(4):
                    mm(pk[:, h_ * 128:(h_ + 1) * 128], kT[:, h_, :], kT[:, h_, :], True, True, [kT], [pk])
                AT = R_["AT"]
                if not is_ctx:
                    pa = free.pop()
                    for h_ in range(4):
                        mm(pa[:, h_ * 128:(h_ + 1) * 128], kT[:, h_, :], qT[:, h_, :], True, True, [kT, qT], [pa])
                yield
                Pa, Qa, X = B_["Pa"], B_["Qa"], B_["X"]
                for h_ in range(4):
                    op("dve", lambda e: e.scalar_tensor_tensor(Pa[:, h_, :], pk[:, h_ * 128:(h_ + 1) * 128], bcol(h_), DmM[:, h_, :], ALU.mult, ALU.mult),
                       [pk, bg, DmM], [Pa])
                free.append(pk)
                if not is_ctx:
                    op("dve", lambda e: e.tensor_tensor(f512(AT), pa[:, :], f512(DTm), ALU.mult), [pa, DTm], [AT])
                    free.append(pa)
                yield
                pq0 = free.pop()
                for h_ in range(4):
                    op("pe", lambda e: e.transpose(pq0[:, h_ * 128:(h_ + 1) * 128], Pa[:, h_, :], ident[:]), [Pa, ident], [pq0])
                I4f = I4.rearrange("p h j -> p (h j)")
                Mk = lambda k_: dnk[:, k_, :, :].rearrange("p h j -> p (h j)")
                Dm, Ck, CkT, Ee, Ee2 = B_["Dm"], B_["Ck"], B_["CkT"], B_["Ee"], B_["Ee2"]
                op("dve", lambda e: e.tensor_tensor(f512(Ck), f512(Pa), Mk(0), ALU.mult), [Pa, dnk], [Ck])
                yield
                op("act", lambda e: e.copy(f512(Qa), pq0[:, :]), [pq0], [Qa])
                free.append(pq0)
                op("dve", lambda e: e.tensor_tensor(f512(Dm), I4f, f512(Ck), ALU.subtract), [dnm, Ck], [Dm])
                yield
                op("dve", lambda e: e.tensor_tensor(f512(CkT), f512(Qa), Mk(0), ALU.mult), [Qa, dnk], [CkT])
                yield
                op("dve", lambda e: e.tensor_tensor(f512(X), I4f, f512(CkT), ALU.subtract), [dnm, CkT], [X])
                for k_ in range(1, 7):
                    op("dve", lambda e: e.tensor_tensor(f512(Ck), f512(Pa), Mk(k_), ALU.mult), [Pa, dnk], [Ck])
                    yield
                    pE2 = free.pop()
                    for h_ in range(4):
                        mm(pE2[:, h_ * 128:(h_ + 1) * 128], Ck[:, h_, :], X[:, h_, :], True, True, [Ck, X], [pE2])
                    yield
                    op("act", lambda e: e.copy(f512(Ee2), pE2[:, :]), [pE2], [Ee2])
                    free.append(pE2)
                    yield
                    pF2 = free.pop()
                    for h_ in range(4):
                        mm(pF2[:, h_ * 128:(h_ + 1) * 128], Dm[:, h_, :], Ee2[:, h_, :], True, True, [Dm, Ee2], [pF2])
                    yield
                    op("dve", lambda e: e.tensor_tensor(f512(X), f512(X), pF2[:, :], ALU.subtract), [X, pF2], [X])
                    free.append(pF2)
                    yield
                    if k_ < 6:
                        pT = free.pop()
                        for h_ in range(4):
                            op("pe", lambda e: e.transpose(pT[:, h_ * 128:(h_ + 1) * 128], X[:, h_, :], ident[:]), [X, ident], [pT])
                        yield
                        op("act", lambda e: e.copy(f512(Dm), pT[:, :]), [pT], [Dm])
                        free.append(pT)
                        yield
                vb, kbg = B_["vb"], B_["kbg"]
                kdec = R_["kdec"]
                for h_ in range(4):
                    op("dve", lambda e: e.tensor_scalar(vb[:, h_, :], vtm_[:, h_, :], bcol(h_), None, ALU.mult), [vtm_, bg], [vb])
                    op("dve", lambda e: e.tensor_scalar(kbg[:, h_, :], ktm_[:, h_, :], sm[:, 24 + h_:25 + h_], None, ALU.mult), [ktm_, sm], [kbg])
                    op("dve", lambda e: e.tensor_scalar(kdec[:, h_, :], ktm_[:, h_, :], sm[:, 16 + h_:17 + h_], None, ALU.mult), [ktm_, sm], [kdec])
                yield
                u, wT = R_["u"], R_["wT"]
                pu = free.pop()
                for h_ in range(4):
                    mm(pu[:, h_ * 128:(h_ + 1) * 128], X[:, h_, :], vb[:, h_, :], True, True, [X, vb], [pu])
                pw = free.pop()
                for h_ in range(4):
                    mm(pw[:, h_ * 128:(h_ + 1) * 128], kbg[:, h_, :], X[:, h_, :], True, True, [X, kbg], [pw])
                yield
                op("act", lambda e: e.copy(f512(u), pu[:, :]), [pu], [u])
                op("act", lambda e: e.copy(f512(wT), pw[:, :]), [pw], [wT])
                free.append(pu)
                free.append(pw)
                yield

            def dn_recur(tile, d, par, banks):
                free = list(banks)
                R_ = rbufs[d][par]
                sm = sms[d][par]
                S = Sst[d]
                is_ctx = tile >= 64
                t0 = tile * 128
                qdec, AT, u, wT, kdec, vnew, o = (R_[n] for n in ("qdec", "AT", "u", "wT", "kdec", "vnew", "o"))
                p1 = free.pop()
                for h_ in range(4):
                    mm(p1[:, h_ * 128:(h_ + 1) * 128], wT[:, h_, :], S[:, h_, :], True, True, [wT, S], [p1])
                yield
                op("dve", lambda e: e.tensor_tensor(f512(vnew), f512(u), p1[:, :], ALU.subtract), [u, p1], [vnew])
                free.append(p1)
                yield
                if not is_ctx:
                    po = free.pop()
                    for h_ in range(4):
                        mm(po[:, h_ * 128:(h_ + 1) * 128], qdec[:, h_, :], S[:, h_, :], True, False, [qdec, S], [po])
                        mm(po[:, h_ * 128:(h_ + 1) * 128], AT[:, h_, :], vnew[:, h_, :], False, True, [AT, vnew], [po])
                p4 = free.pop()
                for h_ in range(4):
                    mm(p4[:, h_ * 128:(h_ + 1) * 128], kdec[:, h_, :], vnew[:, h_, :], True, True, [kdec, vnew], [p4])
                yield
                for h_ in range(4):
                    op("dve", lambda e: e.scalar_tensor_tensor(S[:, h_, :], S[:, h_, :], sm[:, 20 + h_:21 + h_], p4[:, h_ * 128:(h_ + 1) * 128], ALU.mult, ALU.add),
                       [S, sm, p4], [S])
                free.append(p4)
                if not is_ctx:
                    op("act", lambda e: e.copy(f512(o), po[:, :]), [po], [o])
                    free.append(po)
                    P.dma("sp", ODs[d][t0:t0 + 128, :], f512(o), reads=[o], writes=[ODs[d]])
                yield

            order_f = [64, 65] + list(range(64))
            order_b = [65, 64] + list(range(63, -1, -1))
            for st in range(67):
                gens = []
                if st < 66:
                    gens += [dn_prep(order_f[st], 0, st % 2, (ps[0], ps[1])), dn_prep(order_b[st], 1, st % 2, (ps[2], ps[3]))]
                if st >= 1:
                    gens += [dn_recur(order_f[st - 1], 0, (st - 1) % 2, (ps[4], ps[5])), dn_recur(order_b[st - 1], 1, (st - 1) % 2, (ps[6], ps[7]))]
                while gens:
                    nxt = []
                    for g_ in gens:
                        try:
                            next(g_)
                            nxt.append(g_)
                        except StopIteration:
                            pass
                    gens = nxt
            P.barrier()
            P.pop()

        if "CMB" in parts:
            P.push()
            of = [P.sbuf("of", [128, 4, 128], F32) for _ in range(2)]
            ob = [P.sbuf("ob", [128, 4, 128], F32) for _ in range(2)]
            gs = [P.sbuf("gs", [128, 4, 128], BF16) for _ in range(2)]
            osq = P.sbuf("osq", [128, 4, 128], F32)
            ss4 = P.sbuf("ss4", [128, 4], F32)
            rt4 = P.sbuf("rt4", [128, 4], F32)
            rs4 = P.sbuf("rs4", [128, 4], F32)
            dnb = [P.sbuf("dnb", [128, 4, 128], F32) for _ in range(2)]

            def f512(b):
                return b[:].rearrange("p h j -> p (h j)")
            for tile in range(64):
                t0 = tile * 128
                i = tile % 2
                P.dma("sp", f512(of[i]), ODs[0][t0:t0 + 128, :], reads=[ODs[0]], writes=[of[i]])
                P.dma("sp", f512(ob[i]), ODs[1][t0:t0 + 128, :], reads=[ODs[1]], writes=[ob[i]])
                P.dma("sp", f512(gs[i]), GSs[t0:t0 + 128, :], reads=[GSs], writes=[gs[i]])
                op("dve", lambda e: e.tensor_tensor(f512(of[i]), f512(of[i]), f512(ob[i]), ALU.add), [of[i], ob[i]], [of[i]])
                op("act", lambda e: e.activation(f512(osq), f512(of[i]), AF.Square), [of[i]], [osq])
                op("dve", lambda e: e.tensor_reduce(ss4[:], osq[:], AX.X, ALU.add), [osq], [ss4])
                op("act", lambda e: e.activation(rt4[:], ss4[:], AF.Sqrt, bias=epst[:], scale=1.0 / 128.0), [ss4, epst], [rt4])
                op("dve", lambda e: e.reciprocal(rs4[:], rt4[:]), [rt4], [rs4])
                for h_ in range(4):
                    op("dve", lambda e: e.scalar_tensor_tensor(dnb[i][:, h_, :], of[i][:, h_, :], rs4[:, h_:h_ + 1], pbs("dn_ng"), ALU.mult, ALU.mult),
                       [of[i], rs4, pbt], [dnb[i]])
                op("dve", lambda e: e.tensor_tensor(f512(dnb[i]), f512(dnb[i]), f512(gs[i]), ALU.mult), [dnb[i], gs[i]], [dnb[i]])
                P.dma("sp", DNs[t0:t0 + 128, :], f512(dnb[i]), reads=[dnb[i]], writes=[DNs])
            P.barrier()
            P.pop()
        if "HYF" in parts or "HY" in parts:
            P.push()
            hycd = P.dram("hyc", [128, 1280], F32, "ExternalInput")
            hytd = P.dram("hyt", [128, 4, 2, 128], F32, "ExternalInput")
            GFs = scratch("GFs", [2, 16384, 512], BF16)
            HSs = scratch("HSs", [2, 8, 128, 16384], BF16)
            hyc = P.sbuf("hyc", [128, 1280], BF16)
            load_cast(hyc, lambda c0, n: hyc[:, c0:c0 + n], hycd, 1280)
            hyt = P.sbuf("hyt", [128, 4, 2, 128], F32)
            P.dma("sp", hyt[:], hytd[:], writes=[hyt])
            hytb = P.sbuf("hytb", [128, 4, 2, 128], BF16)
            op("act", lambda e: e.copy(hytb[:], hyt[:]), [hyt], [hytb])
            F1b = hyc[:, 0:256]
            F2re, F2im, F2imn = hyc[:, 256:384], hyc[:, 384:512], hyc[:, 512:640]
            G1a, G1b = hyc[:, 640:896], hyc[:, 896:1152]
            Ere, Eimn = hyc[:, 1152:1216], hyc[:, 1216:1280]
            TT = [P.sbuf("TT", [128, 512], BF16) for _ in range(8)]
            tti = [0]

            def nT():
                tti[0] = (tti[0] + 1) % 8
                return TT[tti[0]]

            PB = [P.sbuf("PB", [128, 2, 512], BF16) for _ in range(3)]
            pbi = [0]

            def cmul_evac(pv, tre, tim, out_re, out_im, n):
                pre, pim, pbuf = pv
                pbuf = list(pbuf) if isinstance(pbuf, (list, tuple)) else [pbuf]
                pbi[0] = (pbi[0] + 1) % 3
                pb = PB[pbi[0]]
                t1, t2, t3, t4 = nT(), nT(), nT(), nT()
                three = len(pre.shape) == 3
                v = lambda t: t[:, :n] if not three else t[:, :n].rearrange("p (a b) -> p a b", a=pre.shape[1])
                vb_ = lambda r: pb[:, r, :n] if not three else pb[:, r, :n].rearrange("p (a b) -> p a b", a=pre.shape[1])
                op("act", lambda e: e.copy(vb_(0), pre), pbuf, [pb])
                op("act", lambda e: e.copy(vb_(1), pim), pbuf, [pb])
                op("dve", lambda e: e.tensor_tensor(v(t1), vb_(0), tre, ALU.mult), [pb] + pbuf[1:], [t1])
                op("dve", lambda e: e.tensor_tensor(v(t2), vb_(1), tim, ALU.mult), [pb] + pbuf[1:], [t2])
                op("dve", lambda e: e.tensor_tensor(out_re[0], v(t1), v(t2), ALU.subtract), [t1, t2], [out_re[1]])
                op("dve", lambda e: e.tensor_tensor(v(t3), vb_(0), tim, ALU.mult), [pb] + pbuf[1:], [t3])
                op("dve", lambda e: e.tensor_tensor(v(t4), vb_(1), tre, ALU.mult), [pb] + pbuf[1:], [t4])
                op("dve", lambda e: e.tensor_tensor(out_im[0], v(t3), v(t4), ALU.add), [t3, t4], [out_im[1]])

            def fft_fwd(X, Kp, Bb, consume):
                for c2 in range(32):
                    pS = nb()
                    for cc in range(2):
                        c = c2 * 2 + cc
                        mm(pS[:, cc * 256:(cc + 1) * 256], X[0:Kp, c, :], F1b[0:Kp, :], True, True, [X, hyc], [pS])
                    pv4 = pS[:, :].rearrange("p (c r k) -> p c r k", c=2, r=2)
                    cmul_evac((pv4[:, :, 0, :], pv4[:, :, 1, :], [pS, hytb]), hytb[:, 0, :, :], hytb[:, 1, :, :],
                              (Bb[:, 0, c2 * 2:c2 * 2 + 2, :], Bb), (Bb[:, 1, c2 * 2:c2 * 2 + 2, :], Bb), 256)
                for j in range(16):
                    bre = Bb[:, 0, 4 * j:4 * j + 4, :].rearrange("p c k -> p (c k)")
                    bim = Bb[:, 1, 4 * j:4 * j + 4, :].rearrange("p c k -> p (c k)")
                    pXre = nb()
                    mm(pXre[:, :], F2re, bre, True, False, [hyc, Bb], [pXre])
                    mm(pXre[:, :], F2imn, bim, False, True, [hyc, Bb], [pXre])
                    pXim = nb()
                    mm(pXim[:, :], F2re, bim, True, False, [hyc, Bb], [pXim])
                    mm(pXim[:, :], F2im, bre, False, True, [hyc, Bb], [pXim])
                    consume(j, pXre, pXim)

            def fft_inv(Yh, Cb, gate, Xout):
                for c2 in range(32):
                    pC = nb()
                    for cc in range(2):
                        c = c2 * 2 + cc
                        mm(pC[:, cc * 256:(cc + 1) * 256], Yh[:, 0, c, :], G1a, True, False, [Yh, hyc], [pC])
                        mm(pC[:, cc * 256:(cc + 1) * 256], Yh[:, 1, c, :], G1b, False, True, [Yh, hyc], [pC])
                    pv4 = pC[:, :].rearrange("p (c r k) -> p c r k", c=2, r=2)
                    cmul_evac((pv4[:, :, 0, :], pv4[:, :, 1, :], [pC, hytb]), hytb[:, 2, :, :], hytb[:, 3, :, :],
                              (Cb[:, 0, c2 * 2:c2 * 2 + 2, :], Cb), (Cb[:, 1, c2 * 2:c2 * 2 + 2, :], Cb), 256)
                for j in range(16):
                    cre = Cb[:, 0, 4 * j:4 * j + 4, :].rearrange("p c k -> p (c k)")
                    cim = Cb[:, 1, 4 * j:4 * j + 4, :].rearrange("p c k -> p (c k)")
                    py = nb()
                    mm(py[0:64, :], Ere, cre, True, False, [hyc, Cb], [py])
                    mm(py[0:64, :], Eimn, cim, False, True, [hyc, Cb], [py])
                    op("dve", lambda e: e.tensor_tensor(Xout[:, 4 * j:4 * j + 4, :].rearrange("p c k -> p (c k)"), py[0:64, :],
                                                        gate[:, 4 * j:4 * j + 4, :].rearrange("p c k -> p (c k)"), ALU.mult), [py, gate], [Xout])

        if "HYF" in parts:
            P.push()
            hw1d = P.dram("hy_w1", [33, 64], F32, "ExternalInput")
            hw2d = P.dram("hy_w2", [64, 64], F32, "ExternalInput")
            hw3d = P.dram("hy_w3", [64, 2048], F32, "ExternalInput")
            hyvd = P.dram("hyv", [64, 4], F32, "ExternalInput")
            z2d = P.dram("Z2", [33, 16384], F32, "ExternalInput")
            dlbd = P.dram("dlb", [128, 512], F32, "ExternalInput")
            tcold = P.dram("tcoln", [128, 128], F32, "ExternalInput")
            skcd = P.dram("skc", [128, 1024], F32, "ExternalInput")
            w1s = P.sbuf("w1s", [33, 64], F32)
            w2s = P.sbuf("w2s", [64, 64], F32)
            w3s = P.sbuf("w3s", [64, 2048], F32)
            w3n = P.sbuf("w3n", [64, 2048], F32)
            hyv = P.sbuf("hyv", [64, 4], F32)
            dlb = P.sbuf("dlb", [128, 512], F32)
            tcoln = P.sbuf("tcoln", [128, 128], F32)
            skc = P.sbuf("skc", [128, 1024], F32)
            for dst, src in ((w1s, hw1d), (w2s, hw2d), (w3s, hw3d), (hyv, hyvd), (dlb, dlbd), (tcoln, tcold), (skc, skcd)):
                P.dma("sp", dst[:], src[:], writes=[dst])
            negpi = P.sbuf("negpi", [128, 1], F32)
            op("dve", lambda e: e.memset(negpi[:], -math.pi), writes=[negpi])
            hid2 = P.sbuf("hid2", [64, 16384], F32)
            zb = [P.sbuf("zb", [33, 512], F32) for _ in range(2)]
            a1 = P.sbuf("a1", [64, 512], F32)
            h1 = P.sbuf("h1", [64, 512], F32)
            TWO_PI = 2.0 * math.pi
            qi = P.sbuf("qi", [64, 512], mybir.dt.int32)
            qf = P.sbuf("qf", [64, 512], F32)
            mw = P.sbuf("mw", [64, 512], F32)

            def range_reduce():
                op("dve", lambda e: e.tensor_scalar(qf[:], a1[:], 1.0 / TWO_PI, None, ALU.mult), [a1], [qf])
                op("dve", lambda e: e.tensor_copy(qi[:], qf[:]), [qf], [qi])
                op("dve", lambda e: e.tensor_copy(qf[:], qi[:]), [qi], [qf])
                op("dve", lambda e: e.scalar_tensor_tensor(a1[:], qf[:], -TWO_PI, a1[:], ALU.mult, ALU.add), [qf, a1], [a1])
                op("dve", lambda e: e.tensor_scalar(mw[:], a1[:], -math.pi, TWO_PI, ALU.is_lt, ALU.mult), [a1], [mw])
                op("dve", lambda e: e.tensor_tensor(a1[:], a1[:], mw[:], ALU.add), [a1, mw], [a1])
                op("dve", lambda e: e.tensor_scalar(mw[:], a1[:], math.pi, -TWO_PI, ALU.is_gt, ALU.mult), [a1], [mw])
                op("dve", lambda e: e.tensor_tensor(a1[:], a1[:], mw[:], ALU.add), [a1, mw], [a1])
            for blk in range(32):
                z_ = zb[blk % 2]
                P.dma("sp", z_[:], z2d[:, blk * 512:(blk + 1) * 512], writes=[z_])
                p1 = nb()
                mm(p1[0:64, :], w1s[:, :], z_[:, :], True, True, [w1s, z_], [p1])
                op("dve", lambda e: e.tensor_scalar(a1[:], p1[0:64, :], hyv[:, 0:1], hyv[:, 1:2], ALU.add, ALU.mult), [p1, hyv], [a1])
                range_reduce()
                op("act", lambda e: e.activation(h1[:], a1[:], AF.Sin), [a1], [h1])
                p2 = nb()
                mm(p2[0:64, :], w2s[:, :], h1[:, :], True, True, [w2s, h1], [p2])
                op("dve", lambda e: e.tensor_scalar(a1[:], p2[0:64, :], hyv[:, 2:3], hyv[:, 3:4], ALU.add, ALU.mult), [p2, hyv], [a1])
                range_reduce()
                op("act", lambda e: e.activation(hid2[:, blk * 512:(blk + 1) * 512], a1[:], AF.Sin), [a1], [hid2])
            wnd = [P.sbuf("wnd", [128, 512], F32) for _ in range(2)]
            fa = [P.sbuf("fa", [128, 512], F32) for _ in range(4)]
            fab = [P.sbuf("fab", [128, 512], BF16) for _ in range(4)]
            rn = [P.sbuf("rn", [128, 512], F32) for _ in range(2)]
            b6 = [0]

            def nb6():
                b6[0] = (b6[0] + 1) % 6
                return ps[b6[0]]

            def window(tile):
                w_ = wnd[tile % 2]
                op("dve", lambda e: e.tensor_scalar(w_[:], dlb[:], tcoln[:, tile:tile + 1], None, ALU.mult), [dlb, tcoln], [w_])
                op("act", lambda e: e.activation(w_[:], w_[:], AF.Exp), [w_], [w_])
                return w_
            for tile in range(128):
                dr = 0 if tile < 64 else 1
                w_ = window(tile)
                for o in range(2):
                    pf = nb6()
                    cb = (o * 2 + dr) * 512
                    mm(pf[:, :], hid2[:, tile * 128:(tile + 1) * 128], w3s[:, cb:cb + 512], True, True, [hid2, w3s], [pf])
                    f_ = fa[o + 2 * (tile % 2)]
                    fb_ = fab[o + 2 * (tile % 2)]
                    op("dve", lambda e: e.tensor_tensor(f_[:], pf[:, :], w_[:], ALU.mult), [pf, w_], [f_])
                    op("act", lambda e: e.activation(fb_[:], f_[:], AF.Abs), [f_], [fb_])
                    mm(ps[6 + o][:, :], ones_bf[:], fb_[:], tile == 0, tile == 127, [ones_bf, fb_], [ps[6 + o]])
            for o in range(2):
                op("dve", lambda e: e.reciprocal(rn[o][:], ps[6 + o][:, :]), [ps[6 + o]], [rn[o]])
                for dr in range(2):
                    cb = (o * 2 + dr) * 512
                    op("dve", lambda e: e.tensor_tensor(w3n[:, cb:cb + 512], w3s[:, cb:cb + 512], rn[o][0:64, :], ALU.mult), [w3s, rn[o]], [w3n])
            for tile in range(128):
                dr = 0 if tile < 64 else 1
                w_ = window(tile)
                for o in range(2):
                    pf = nb6()
                    cb = (o * 2 + dr) * 512
                    mm(pf[:, :], hid2[:, tile * 128:(tile + 1) * 128], w3n[:, cb:cb + 512], True, True, [hid2, w3n], [pf])
                    fb_ = fab[o + 2 * (tile % 2)]
                    op("dve", lambda e: e.tensor_tensor(fb_[:], pf[:, :], w_[:], ALU.mult), [pf, w_], [fb_])
                    P.dma("sp", GFs[o, tile * 128:(tile + 1) * 128, :], fb_[:], reads=[fb_], writes=[GFs])
            P.barrier()
            P.pop()
            P.push()
            skc = P.sbuf("skc", [128, 1024], F32)
            P.dma("sp", skc[:], skcd[:], writes=[skc])
            ldf = [P.sbuf("ldf", [128, 128, 64], BF16) for _ in range(2)]
            Xf = P.sbuf("Xf", [128, 64, 128], BF16)
            Bb = P.sbuf("Bb", [128, 2, 64, 128], BF16)
            Hh = [P.sbuf("Hh", [128, 2, 64, 128], BF16) for _ in range(2)]
            for og in range(16):
                o, g = divmod(og, 8)
                l_ = ldf[og % 2]
                P.dma("sp", l_[:], GFs[o].rearrange("(a b) c -> a b c", b=128)[:, :, g * 64:(g + 1) * 64], reads=[GFs], writes=[l_])
                op("dve", lambda e: e.memset(l_[64:65, 0, :], 0.0), writes=[l_])
                op("act", lambda e: e.copy(Xf[:], l_[:].rearrange("p n c -> p c n")), [l_], [Xf])
                H_ = Hh[og % 2]

                def consume(j, pXre, pXim):
                    for cc in range(4):
                        ch = o * 512 + g * 64 + 4 * j + cc
                        op("act", lambda e: e.activation(H_[:, 0, 4 * j + cc, :], pXre[:, cc * 128:(cc + 1) * 128], AF.Identity, bias=skc[:, ch:ch + 1]),
                           [pXre, skc], [H_])
                    op("act", lambda e: e.copy(H_[:, 1, 4 * j:4 * j + 4, :].rearrange("p c k -> p (c k)"), pXim[:, :]), [pXim], [H_])
                fft_fwd(Xf, 128, Bb, consume)
                P.dma("sp", HSs[o, g], H_[:].rearrange("p r c k -> p (r c k)"), reads=[H_], writes=[HSs])
            P.barrier()
            P.pop()

        if "HY" in parts:
            P.push()
            ld = [P.sbuf("ld", [64, 128, 64], BF16)] * 2
            Xz = P.sbuf("Xz", [64, 64, 128], BF16)
            g1 = P.sbuf("g1", [64, 64, 128], BF16)
            g2 = P.sbuf("g2", [64, 64, 128], BF16)
            Bb = P.sbuf("Bb", [128, 2, 64, 128], BF16)
            Yh = P.sbuf("Yh", [128, 2, 64, 128], BF16)
            Hh = [P.sbuf("Hh", [128, 2, 64, 128], BF16)] * 2
            UH3 = UHs.rearrange("(a b) f -> a b f", b=128)
            HY3 = HYs.rearrange("(a b) f -> a b f", b=128)
            lc = 0
            for g in range(8):
                for part, dst in ((0, Xz), (1, g1), (2, g2)):
                    l_ = ld[lc % 2]; lc += 1
                    P.dma("sp", l_[:], UH3[:, :, part * 512 + g * 64:part * 512 + (g + 1) * 64], reads=[UHs], writes=[l_])
                    op("act", lambda e: e.copy(dst[:], l_[:].rearrange("p n c -> p c n")), [l_], [dst])
                for o in range(2):
                    H_ = Hh[o]
                    P.dma("sp", H_[:].rearrange("p r c k -> p (r c k)"), HSs[o, g], reads=[HSs], writes=[H_])

                    def consume(j, pXre, pXim):
                        sl = lambda t, r: t[:, r, 4 * j:4 * j + 4, :].rearrange("p c k -> p (c k)")
                        cmul_evac((pXre[:, :], pXim[:, :], [pXre, pXim, H_]), sl(H_, 0), sl(H_, 1), (sl(Yh, 0), Yh), (sl(Yh, 1), Yh), 512)
                    fft_fwd(Xz, 64, Bb, consume)
                    fft_inv(Yh, Bb, g1 if o == 0 else g2, Xz)
                l_ = ld[lc % 2]; lc += 1
                op("act", lambda e: e.copy(l_[:].rearrange("p n c -> p c n"), Xz[:]), [Xz], [l_])
                P.dma("sp", HY3[:, :, g * 64:(g + 1) * 64], l_[:], reads=[l_], writes=[HYs])
            P.barrier()
            P.pop()
        if "HYF" in parts or "HY" in parts:
            P.barrier()
            P.pop()
        if "OUT" in parts:
            P.push()
            odwod = P.dram("odwo", [D, D], F32, "ExternalInput")
            wout = P.sbuf("wout", [128, 8, 1024], BF16)
            for k in range(8):
                load_cast(wout, lambda c0, n, k=k: wout[:, k, c0:c0 + n], odwod[k * 128:(k + 1) * 128, :], 1024)
            hyt_ = [P.sbuf("hyt_", [128, 4, 512], F32) for _ in range(2)]
            hyb_ = [P.sbuf("hyb_", [128, 4, 512], BF16) for _ in range(2)]
            dnt_ = [P.sbuf("dnt_", [128, 4, 512], F32) for _ in range(2)]
            xsb = [P.sbuf("xs", [128, 8, 512], F32) for _ in range(2)]
            mixT = P.sbuf("mixT", [128, 8, 512], BF16)
            x1b = [P.sbuf("x1b", [128, 8, 512], F32) for _ in range(2)]
            HYv = HYs.rearrange("(n p) f -> p n f", p=128)
            DNv = DNs.rearrange("(n p) f -> p n f", p=128)

            def loadO(blk):
                t0, nt, s = blk_range(blk)
                P.dma("sp", hyb_[blk % 2][:], HYv[:, blk * 4:blk * 4 + 4, :], reads=[HYs], writes=[hyb_[blk % 2]])
                op("dve", lambda e: e.tensor_copy(hyt_[blk % 2][:], hyb_[blk % 2][:]), [hyb_[blk % 2]], [hyt_[blk % 2]])
                P.dma("sp", dnt_[blk % 2][:], DNv[:, blk * 4:blk * 4 + 4, :], reads=[DNs], writes=[dnt_[blk % 2]])
                P.dma("sp", xsb[blk % 2][:], X1v[:, :, t0:t0 + nt], reads=[X1s], writes=[xsb[blk % 2]])
            loadO(0)
            for blk in range(16):
                t0, nt, s = blk_range(blk)
                if blk + 1 < 16:
                    loadO(blk + 1)
                xs = xsb[blk % 2]
                x1 = x1b[blk % 2]
                for sub in range(4):
                    for src, base in ((hyt_[blk % 2], 0), (dnt_[blk % 2], 4)):
                        pt = nb()
                        for j in range(4):
                            op("pe", lambda e: e.transpose(pt[:, j * 128:(j + 1) * 128], src[:, sub, j * 128:(j + 1) * 128], ident[:]), [src, ident], [pt])
                        op("act", lambda e: e.copy(mixT[:, base:base + 4, sub * 128:(sub + 1) * 128], pt[:, :].rearrange("p (c t) -> p c t", c=4)), [pt], [mixT])
                for o in range(8):
                    po = nb()
                    for k in range(8):
                        mm(po[:, :nt], wout[:, k, o * 128:(o + 1) * 128], mixT[:, k, :nt], k == 0, k == 7, [wout, mixT], [po])
                    op("dve", lambda e: e.scalar_tensor_tensor(x1[:, o, :nt], po[:, :nt], mod[0][:, 16 + o:17 + o], xs[:, o, :nt], ALU.mult, ALU.add),
                       [po, mod[0], xs], [x1])
                P.dma("sp", XAv[:, :, t0:t0 + nt], x1[:, :, :nt], reads=[x1], writes=[XAs])
            P.barrier()
            P.pop()
            mlp_phase(1, XAv, XAs, Yv, Ys, 16, final=True)

    if 0 in layers:
        layer0()
        mlp_phase(0, XAv, XAs, X1v, X1s, NBLK)
    if 1 in layers:
        layer1()

    fin = [(k, v) for k, v in P.last_w.items() if k in ("X1s",) or k in debug]
    return P.finish([v for _, v in fin]), P


_CACHE = {}


def _get(key, **kw):
    if key not in _CACHE:
        _CACHE[key] = build(**kw)
    return _CACHE[key]


def _run(ncP, maps):
    nc, P = ncP
    in_maps = [{k: m[k] for k in P.ext_in} for m in maps]
    res = run_bass_kernel_spmd(nc, in_maps, core_ids=list(range(len(maps))))
    return res.results


def kernel(**inputs):
    inp = {k: np.asarray(v) for k, v in inputs.items()}
    nb_ = inp["x"].shape[0]
    consts = make_consts(inp)
    maps = [pack_core(inp, b, consts) for b in range(nb_)]
    r2 = _run(_get("fused", layers=(0, 1)), maps)
    out = np.stack([np.asarray(r["yT"], np.float32).T for r in r2], axis=0)
    return np.ascontiguousarray(out, np.float32)
```
